# Optimizing a Trainium2 kernel written in Bass

```python
import jax
import jax.numpy as jnp
from jax import lax
import numpy as np

D_MODEL = 1024
BATCH = 16
SEQ = 2048
DEPTH = 1

GRID_W = 64
CTX_LEN = 256
MLSTM_HEADS = 8
MLSTM_DH = 128
MLSTM_W = MLSTM_HEADS * MLSTM_DH
MLSTM_CHUNK = 64
GLA_HEADS = 4
GLA_DK_HEAD = 128
GLA_DV_HEAD = 256
GLA_DK = GLA_HEADS * GLA_DK_HEAD
GLA_DV = GLA_HEADS * GLA_DV_HEAD
GLA_RANK = 16
GLA_TAU = 16.0
GLA_CHUNK = 16
D_FF = 2816
ALPHA = (2.0 * DEPTH) ** 0.25
BETA = (8.0 * DEPTH) ** -0.25
EPS = 1e-5
SPLITS = (MLSTM_W, MLSTM_W, MLSTM_W, MLSTM_W, 4 * MLSTM_HEADS, GLA_DK, GLA_DK, GLA_DV, GLA_DV, 2 * GLA_RANK, D_MODEL, D_MODEL)
N_IN = sum(SPLITS)

kernel_name = 'hybrid_mlstm_gla_convglu_prefix'


def layer_norm(h, g, b):
    hf = h.astype(jnp.float32)
    mu = hf.mean(-1, keepdims=True)
    var = jnp.square(hf - mu).mean(-1, keepdims=True)
    return ((hf - mu) * lax.rsqrt(var + EPS)).astype(h.dtype) * g + b


def head_norm(h, g, center):
    if center:
        h = h - h.mean(-1, keepdims=True)
    h = h * lax.rsqrt(jnp.square(h).mean(-1, keepdims=True) + EPS)
    bsz, nh, t, d = h.shape
    return jnp.transpose(h, (0, 2, 1, 3)).reshape(bsz, t, nh * d) * g


def split_heads(a, nh):
    bsz, t, ch = a.shape
    return jnp.transpose(a.reshape(bsz, t, nh, ch // nh), (0, 2, 1, 3))


def to_chunks(a, size):
    bsz, nh, t = a.shape[:3]
    return jnp.moveaxis(a.reshape((bsz, nh, t // size, size) + a.shape[3:]), 2, 0)


def from_chunks(a):
    nc, bsz, nh, size = a.shape[:4]
    return jnp.moveaxis(a, 0, 2).reshape((bsz, nh, nc * size) + a.shape[4:])


def short_conv(a, w):
    ap = jnp.pad(a, ((0, 0), (1, 1), (0, 0)))
    return ap[:, :-2] * w[0] + ap[:, 1:-1] * w[1] + ap[:, 2:] * w[2]


def mlstm_scan(q, k, v, li, lf, state):
    size = MLSTM_CHUNK
    tril = jnp.tril(jnp.ones((size, size), bool))

    def step(carry, inp):
        c_mat, n_vec, m = carry
        qc, kc, vc, lic, lfc = inp
        b = jnp.cumsum(lfc, axis=-1)
        dlog = jnp.where(tril, b[..., :, None] - b[..., None, :] + lic[..., None, :], -jnp.inf)
        m_inter = b + m[..., None]
        m_row = jnp.maximum(m_inter, dlog.max(-1))
        s = jnp.einsum('bhld,bhsd->bhls', qc, kc) * jnp.exp(dlog - m_row[..., None])
        w_inter = jnp.exp(m_inter - m_row)
        num = w_inter[..., None] * jnp.einsum('bhld,bhde->bhle', qc, c_mat) + jnp.einsum('bhls,bhse->bhle', s, vc)
        den = w_inter * jnp.einsum('bhld,bhd->bhl', qc, n_vec) + s.sum(-1)
        h = num / jnp.maximum(jnp.abs(den), jnp.exp(-m_row))[..., None]
        b_end = b[..., -1]
        wlog = b_end[..., None] - b + lic
        m_new = jnp.maximum(b_end + m, wlog.max(-1))
        wk = kc * jnp.exp(wlog - m_new[..., None])[..., None]
        decay = jnp.exp(b_end + m - m_new)
        c_new = decay[..., None, None] * c_mat + jnp.einsum('bhld,bhle->bhde', wk, vc)
        n_new = decay[..., None] * n_vec + wk.sum(2)
        return (c_new, n_new, m_new), h

    state, h = lax.scan(step, state, tuple(to_chunks(a, size) for a in (q, k, v, li, lf)))
    return from_chunks(h), state


def gla_scan(q, k, v, la, s_mat):
    size = GLA_CHUNK
    tril = jnp.tril(jnp.ones((size, size), bool))[:, :, None]

    def step(s_mat, inp):
        qc, kc, vc, lac = inp
        bcum = jnp.cumsum(lac, axis=2)
        o_inter = jnp.einsum('bhld,bhde->bhle', qc * jnp.exp(bcum), s_mat)
        dec = jnp.exp(jnp.where(tril, bcum[:, :, :, None, :] - bcum[:, :, None, :, :], -jnp.inf))
        a = jnp.einsum('bhlsd,bhsd->bhls', qc[:, :, :, None, :] * dec, kc)
        o = o_inter + jnp.einsum('bhls,bhse->bhle', a, vc)
        b_end = bcum[:, :, -1]
        s_new = jnp.exp(b_end)[..., None] * s_mat + jnp.einsum('bhld,bhle->bhde', kc * jnp.exp(b_end[:, :, None] - bcum), vc)
        return s_new, o

    s_mat, o = lax.scan(step, s_mat, tuple(to_chunks(a, size) for a in (q, k, v, la)))
    return from_chunks(o), s_mat


def bidirectional(scan, ctx_shared, ctx_gates, lat_shared, lat_gates, init_state):
    out_ctx, out_lat = 0.0, 0.0
    for d in range(2):
        cs, cg, ls, lg = ctx_shared, ctx_gates[d], lat_shared, lat_gates[d]
        if d == 1:
            cs, cg, ls, lg = [tuple(jnp.flip(a, axis=2) for a in t) for t in (cs, cg, ls, lg)]
        h_c, st = scan(*cs, *cg, init_state)
        h_l, _ = scan(*ls, *lg, st)
        if d == 1:
            h_c, h_l = jnp.flip(h_c, axis=2), jnp.flip(h_l, axis=2)
        out_ctx = out_ctx + h_c
        out_lat = out_lat + h_l
    return out_ctx, out_lat


def mixer_inputs(u, w_in, b_in, conv_qk, gw_f, gb_f, gw_b, gb_b):
    f32 = lambda a: a.astype(jnp.float32)
    p = u @ w_in + b_in
    cuts = [int(i) for i in np.cumsum(SPLITS)[:-1]]
    q_m, k_m, v_m, o_m, g_m, q_g, k_g, v_g, z_g, r_g, gate_a, gate_b = jnp.split(p, cuts, axis=-1)
    qk = jax.nn.silu(short_conv(jnp.concatenate([q_m, k_m], axis=-1), conv_qk))
    q_m, k_m = jnp.split(qk, 2, axis=-1)
    mlstm_shared = (split_heads(f32(q_m), MLSTM_HEADS) * MLSTM_DH ** -0.5,
                    split_heads(f32(k_m), MLSTM_HEADS), split_heads(f32(v_m), MLSTM_HEADS))
    bsz, t, _ = u.shape
    i_f, i_b, f_f, f_b = jnp.transpose(f32(g_m).reshape(bsz, t, 4, MLSTM_HEADS), (2, 0, 3, 1))
    mlstm_gates = ((i_f, jax.nn.log_sigmoid(f_f)), (i_b, jax.nn.log_sigmoid(f_b)))
    gla_shared = (split_heads(f32(q_g), GLA_HEADS) * GLA_DK_HEAD ** -0.5,
                  split_heads(f32(k_g), GLA_HEADS), split_heads(f32(v_g), GLA_HEADS))
    r_f, r_b = jnp.split(f32(r_g), 2, axis=-1)
    la_f = split_heads(jax.nn.log_sigmoid(r_f @ gw_f + gb_f) / GLA_TAU, GLA_HEADS)
    la_b = split_heads(jax.nn.log_sigmoid(r_b @ gw_b + gb_b) / GLA_TAU, GLA_HEADS)
    gla_gates = ((la_f,), (la_b,))
    return mlstm_shared, mlstm_gates, gla_shared, gla_gates, (o_m, z_g, gate_a, gate_b)


def merge_branches(h_m, h_g, side, mlstm_norm_g, gla_norm_g, w_branch_mlstm, w_branch_gla, w_out, dtype):
    o_m, z_g, gate_a, gate_b = side
    y_m = (head_norm(h_m, mlstm_norm_g, True) * jax.nn.sigmoid(o_m.astype(jnp.float32))).astype(dtype) @ w_branch_mlstm
    y_g = (head_norm(h_g, gla_norm_g, False) * jax.nn.silu(z_g.astype(jnp.float32))).astype(dtype) @ w_branch_gla
    return (jax.nn.sigmoid(gate_a) * y_m + jax.nn.sigmoid(gate_b) * y_g) @ w_out


def conv_glu(u, rows, w_up, b_up, conv_w, conv_b, w_down, b_down):
    gate, val = jnp.split(u @ w_up + b_up, 2, axis=-1)
    bsz, t, ch = gate.shape
    img = gate.reshape(bsz, rows, t // rows, ch)
    img = lax.conv_general_dilated(img, conv_w[:, :, None, :], (1, 1), 'SAME',
                                   dimension_numbers=('NHWC', 'HWIO', 'NHWC'), feature_group_count=ch) + conv_b
    return (jax.nn.gelu(img.reshape(bsz, t, ch)) * val) @ w_down + b_down


def setup_inputs(seed: int = 0) -> dict:
    key = jax.random.key(seed)
    ks = jax.random.split(key, 28)

    def nrm(k, shape, scale):
        return jax.random.normal(k, shape, jnp.float32) * scale

    nl = DEPTH
    f_lo = 4 * MLSTM_W + 2 * MLSTM_HEADS
    b_in = nrm(ks[7], (nl, N_IN), 0.02).at[:, f_lo:f_lo + 2 * MLSTM_HEADS].add(3.0)
    return {
        'x': nrm(ks[0], (BATCH, SEQ, D_MODEL), 1.0),
        'c': nrm(ks[1], (BATCH, D_MODEL), 1.0),
        'ctx': nrm(ks[2], (BATCH, CTX_LEN, D_MODEL), 1.0),
        'c_ctx': nrm(ks[3], (D_MODEL,), 1.0),
        'w_mod': nrm(ks[4], (nl, D_MODEL, 6 * D_MODEL), 0.5 * D_MODEL ** -0.5),
        'b_mod': nrm(ks[5], (nl, 6 * D_MODEL), 0.02),
        'w_in': nrm(ks[6], (nl, D_MODEL, N_IN), D_MODEL ** -0.5),
        'b_in': b_in,
        'conv_qk': nrm(ks[8], (nl, 3, 2 * MLSTM_W), 3 ** -0.5),
        'gla_gate_w_fwd': nrm(ks[9], (nl, GLA_RANK, GLA_DK), GLA_RANK ** -0.5),
        'gla_gate_b_fwd': nrm(ks[10], (nl, GLA_DK), 0.02),
        'gla_gate_w_bwd': nrm(ks[11], (nl, GLA_RANK, GLA_DK), GLA_RANK ** -0.5),
        'gla_gate_b_bwd': nrm(ks[12], (nl, GLA_DK), 0.02),
        'mlstm_norm_g': 1.0 + nrm(ks[13], (nl, MLSTM_W), 0.02),
        'gla_norm_g': 1.0 + nrm(ks[14], (nl, GLA_DV), 0.02),
        'w_branch_mlstm': nrm(ks[15], (nl, MLSTM_W, D_MODEL), BETA * MLSTM_W ** -0.5),
        'w_branch_gla': nrm(ks[16], (nl, GLA_DV, D_MODEL), BETA * GLA_DV ** -0.5),
        'w_out': nrm(ks[17], (nl, D_MODEL, D_MODEL), BETA * D_MODEL ** -0.5),
        'ln1_g': 1.0 + nrm(ks[18], (nl, D_MODEL), 0.02),
        'ln1_b': nrm(ks[19], (nl, D_MODEL), 0.02),
        'w_up': nrm(ks[20], (nl, D_MODEL, 2 * D_FF), D_MODEL ** -0.5),
        'b_up': nrm(ks[21], (nl, 2 * D_FF), 0.02),
        'ffn_conv_w': nrm(ks[22], (nl, 3, 3, D_FF), 1.0 / 3.0),
        'ffn_conv_b': nrm(ks[23], (nl, D_FF), 0.02),
        'w_down': nrm(ks[24], (nl, D_FF, D_MODEL), BETA * D_FF ** -0.5),
        'b_down': nrm(ks[25], (nl, D_MODEL), 0.02),
        'ln2_g': 1.0 + nrm(ks[26], (nl, D_MODEL), 0.02),
        'ln2_b': nrm(ks[27], (nl, D_MODEL), 0.02),
    }


def reference(x, c, ctx, c_ctx, w_mod, b_mod, w_in, b_in, conv_qk, gla_gate_w_fwd, gla_gate_b_fwd,
              gla_gate_w_bwd, gla_gate_b_bwd, mlstm_norm_g, gla_norm_g, w_branch_mlstm, w_branch_gla, w_out,
              ln1_g, ln1_b, w_up, b_up, ffn_conv_w, ffn_conv_b, w_down, b_down, ln2_g, ln2_b):
    bsz, seq, _ = x.shape
    rows = seq // GRID_W
    h, hc = x, ctx
    for l in range(DEPTH):
        mod = jax.nn.silu(c) @ w_mod[l] + b_mod[l]
        sh1, sc1, g1, sh2, sc2, g2 = jnp.split(mod[:, None, :], 6, axis=-1)
        mod_c = jax.nn.silu(c_ctx) @ w_mod[l] + b_mod[l]
        csh1, csc1, cg1, csh2, csc2, cg2 = jnp.split(mod_c, 6, axis=-1)
        proj = (w_in[l], b_in[l], conv_qk[l], gla_gate_w_fwd[l], gla_gate_b_fwd[l], gla_gate_w_bwd[l], gla_gate_b_bwd[l])
        m_sh_c, m_g_c, g_sh_c, g_g_c, side_c = mixer_inputs(hc * (1 + csc1) + csh1, *proj)
        m_sh_l, m_g_l, g_sh_l, g_g_l, side_l = mixer_inputs(h * (1 + sc1) + sh1, *proj)
        m_init = (jnp.zeros((bsz, MLSTM_HEADS, MLSTM_DH, MLSTM_DH), jnp.float32),
                  jnp.zeros((bsz, MLSTM_HEADS, MLSTM_DH), jnp.float32),
                  jnp.zeros((bsz, MLSTM_HEADS), jnp.float32))
        g_init = jnp.zeros((bsz, GLA_HEADS, GLA_DK_HEAD, GLA_DV_HEAD), jnp.float32)
        hm_c, hm_l = bidirectional(mlstm_scan, m_sh_c, m_g_c, m_sh_l, m_g_l, m_init)
        hg_c, hg_l = bidirectional(gla_scan, g_sh_c, g_g_c, g_sh_l, g_g_l, g_init)
        outp = (mlstm_norm_g[l], gla_norm_g[l], w_branch_mlstm[l], w_branch_gla[l], w_out[l])
        ffn = (w_up[l], b_up[l], ffn_conv_w[l], ffn_conv_b[l], w_down[l], b_down[l])
        mix = merge_branches(hm_l, hg_l, side_l, *outp, x.dtype)
        h = layer_norm(ALPHA * h + g1 * mix, ln1_g[l], ln1_b[l])
        f = conv_glu(h * (1 + sc2) + sh2, rows, *ffn)
        h = layer_norm(ALPHA * h + g2 * f, ln2_g[l], ln2_b[l])
        if l < DEPTH - 1:
            mix_c = merge_branches(hm_c, hg_c, side_c, *outp, x.dtype)
            hc = layer_norm(ALPHA * hc + cg1 * mix_c, ln1_g[l], ln1_b[l])
            f_c = conv_glu(hc * (1 + csc2) + csh2, 1, *ffn)
            hc = layer_norm(ALPHA * hc + cg2 * f_c, ln2_g[l], ln2_b[l])
    return h
```

```python
import numpy as np
import concourse.bass as bass
import concourse.mybir as mybir
from concourse.bass_utils import run_bass_kernel_spmd

F32 = mybir.dt.float32
BF16 = mybir.dt.bfloat16
U8 = mybir.dt.uint8
AF = mybir.ActivationFunctionType
ALU = mybir.AluOpType
AX = mybir.AxisListType

D = 1024
TC = 256
TL = 2048
T = TC + TL
NCH = T // 128
DFF = 2816
NFC = DFF // 128
ALPHA = 2.0 ** 0.25
EPS = 1e-5
NSLOT = 24

_VSPEC = [("bq_m", 8), ("bk_m", 8), ("bv_m", 8), ("bo_m", 8), ("cw0", 16), ("cw1", 16), ("cw2", 16),
          ("bq_g", 4), ("bk_g", 4), ("bv_g", 8), ("bz_g", 8), ("gb_f", 4), ("gb_b", 4),
          ("gn_m", 8), ("gn_g", 8), ("ba", 8), ("bb", 8), ("bup_g", NFC), ("bup_v", NFC),
          ("fcw", 9 * NFC), ("fcb", NFC), ("bdn", 8), ("bmod", 48), ("br", 2)]
VOFF = {}
_o = 0
for _n, _c in _VSPEC:
    VOFF[_n] = _o
    _o += _c
NV = _o


class Own:
    def __init__(self, name, sem, inc):
        self.name, self.sem, self.inc, self.count = name, sem, inc, 0


class Iss:
    def __init__(self, h, own=None):
        self.h, self.own, self.seen = h, own, {}


class Buf:
    __slots__ = ("ap", "w", "r")

    def __init__(self, ap):
        self.ap, self.w, self.r = ap, None, {}

    def __getitem__(self, k):
        return self.ap[k]


class Pair:
    def __init__(self, ap, bufs):
        self.ap, self.bufs = ap, bufs

    def __getitem__(self, k):
        return self.ap[k]


def _flat(lst):
    out = []
    for t in lst:
        if isinstance(t, Pair):
            out.extend(t.bufs)
        else:
            out.append(t)
    return out


class KB:
    def __init__(self, nc, sbuf_bytes):
        self.nc = nc
        mk = lambda n, inc: Own(n, nc.alloc_semaphore(n), inc)
        self.oPE, self.oACT, self.oDVE, self.oPOOL = mk("pe", 1), mk("act", 1), mk("dve", 1), mk("pool", 1)
        self.PE, self.ACT = Iss(nc.tensor, self.oPE), Iss(nc.scalar, self.oACT)
        self.DVE, self.POOL = Iss(nc.vector, self.oDVE), Iss(nc.gpsimd, self.oPOOL)
        self.SP = Iss(nc.sync, None)
        self.slots = [mk("dma%d" % i, 16) for i in range(NSLOT)]
        self.dma_i = 0
        self.big = nc.alloc_sbuf_tensor("big", [128, sbuf_bytes], U8)
        self.cap = sbuf_bytes
        self.top = 0

    def alloc(self, free, dt):
        sz = 4 if dt == F32 else 2
        n = 1
        for q in free:
            n *= q
        off = (self.top + 31) // 32 * 32
        self.top = off + n * sz
        assert self.top <= self.cap, ("SBUF overflow", self.top, self.cap)
        ap = self.big[:, off:off + n * sz].bitcast(dt)
        if len(free) == 2:
            ap = ap.rearrange("p (a b) -> p a b", a=free[0])
        elif len(free) == 3:
            ap = ap.rearrange("p (a b c) -> p a b c", a=free[0], b=free[1])
        return Buf(ap)

    def _waits(self, iss, reads, writes):
        need = {}
        for t in reads:
            if t.w is not None:
                o, v = t.w
                if need.get(o, 0) < v:
                    need[o] = v
        for t in writes:
            if t.w is not None:
                o, v = t.w
                if need.get(o, 0) < v:
                    need[o] = v
            for o, v in t.r.items():
                if need.get(o, 0) < v:
                    need[o] = v
        for o, v in need.items():
            if o is iss.own and o is self.oPE:
                continue
            if iss.seen.get(o, 0) >= v:
                continue
            iss.h.wait_ge(o.sem, v)
            iss.seen[o] = v

    @staticmethod
    def _record(own, v, reads, writes):
        for t in reads:
            if t.r.get(own, 0) < v:
                t.r[own] = v
        for t in writes:
            t.w = (own, v)
            t.r = {}

    def op(self, iss, fn, reads=(), writes=(), inc=True):
        reads, writes = _flat(reads), _flat(writes)
        self._waits(iss, reads, writes)
        ins = fn(iss.h)
        own = iss.own
        if inc:
            own.count += 1
            ins.then_inc(own.sem, 1)
            v = own.count
        else:
            v = own.count + 1
        self._record(own, v, reads, writes)

    def dma(self, iss, out, in_, reads=(), writes=()):
        reads, writes = _flat(reads), _flat(writes)
        slot = self.slots[self.dma_i % NSLOT]
        self.dma_i += 1
        if slot.count > iss.seen.get(slot, 0):
            iss.h.wait_ge(slot.sem, slot.count)
            iss.seen[slot] = slot.count
        self._waits(iss, reads, writes)
        ins = iss.h.dma_start(out=out, in_=in_, max_dma_last_dim=4096) if iss is self.POOL else iss.h.dma_start(out=out, in_=in_)
        slot.count += 16
        ins.then_inc(slot.sem, 16)
        self._record(slot, slot.count, reads, writes)

    def barrier(self):
        owners = [self.oPE, self.oACT, self.oDVE, self.oPOOL] + self.slots
        for iss in (self.PE, self.ACT, self.DVE, self.POOL, self.SP):
            for o in owners:
                if o.count > iss.seen.get(o, 0):
                    iss.h.wait_ge(o.sem, o.count)
                    iss.seen[o] = o.count


def bc_last(ap, n):
    sh = list(ap.shape)
    sh[-1] = n
    return ap.broadcast_to(sh)


def build(NB=2, dbg=None, stage=99):
    nc = bass.Bass("TRN2", target_bir_lowering=False)
    dram = lambda n, s, k="ExternalInput": nc.dram_tensor(n, list(s), F32, kind=k).ap()
    xin = dram("xin", [NB, T, D])
    cT_d = dram("cT", [128, 8, 4])
    wmod_d = dram("wmod", [128, 8, 6 * D])
    vecs_d = dram("vecs", [128, NV])
    wm_d = dram("w_m", [8, 128, 8, 512])
    wg_d = dram("w_g", [4, 128, 8, 768])
    ws_d = dram("w_s", [128, 8, 64])
    gw_d = dram("gw", [16, 1024])
    rows_d = dram("rows", [8, D])
    wc1_d = dram("wc1", [8, 128, 8, 512])
    wout_d = dram("wout", [128, 8, D])
    wup_d = dram("wup", [NFC, 128, 8, 256])
    wdn_d = dram("wdn", [8, 128, NFC, 128])
    cst_d = dram("cst", [128, 4, 128])
    rst_d = dram("rst", [128, T])
    out_d = dram("out", [NB, TL, D], "ExternalOutput")
    dbg_d = {}
    if dbg:
        for n, (s, dt_) in dbg.items():
            dbg_d[n] = nc.dram_tensor("dbg_" + n, list(s), dt_, kind="ExternalOutput").ap()

    k = KB(nc, 206 * 1024)
    PE, ACT, DVE, POOL, SP = k.PE, k.ACT, k.DVE, k.POOL, k.SP
    op, dma = k.op, k.dma

    _pt = [nc.alloc_psum_tensor("pp%d" % i, [128, 1024], F32)[:, :] for i in range(4)]
    PB = [[Buf(_pt[i][:, 0:512]), Buf(_pt[i][:, 512:1024])] for i in range(4)]
    PP = [Pair(_pt[i], PB[i]) for i in range(4)]
    pb_rr = [0]

    def next_bank():
        i = pb_rr[0] % 4
        pb_rr[0] += 1
        return PB[i // 2][i % 2]

    def dump(name, buf, ap=None):
        if name in dbg_d:
            dma(SP, dbg_d[name], buf.ap if ap is None else ap, reads=[buf])

    cst = k.alloc([4, 128], F32)
    identf, maskU, maskL, onesf = cst[:, 0, :], cst[:, 1, :], cst[:, 2, :], cst[:, 3, :]
    dma(SP, cst.ap, cst_d, writes=[cst])
    identb = k.alloc([128], BF16)
    op(DVE, lambda e: e.tensor_copy(out=identb.ap, in_=identf), [cst], [identb])
    vecs = k.alloc([NV], F32)
    dma(SP, vecs.ap, vecs_d, writes=[vecs])
    V = lambda name, i=0: vecs[:, VOFF[name] + i:VOFF[name] + i + 1]
    cT = k.alloc([8, 4], F32)
    dma(SP, cT.ap, cT_d, writes=[cT])
    scT = k.alloc([8, 4], BF16)
    op(ACT, lambda e: e.activation(out=scT.ap, in_=cT.ap, func=AF.Silu), [cT], [scT])
    ws = k.alloc([8, 64], BF16)
    dma(POOL, ws.ap, ws_d, writes=[ws])
    gwb = k.alloc([1024], BF16)
    dma(POOL, gwb[0:16, :], gw_d, writes=[gwb])
    ngb = k.alloc([8], F32)
    op(DVE, lambda e: e.tensor_scalar(out=ngb.ap, in0=vecs[:, VOFF["gb_f"]:VOFF["gb_f"] + 8], scalar1=-1.0,
                                      scalar2=None, op0=ALU.mult), [vecs], [ngb])
    MOD = k.alloc([48, 4], F32)
    SC1 = k.alloc([8, 4], F32)
    A2 = k.alloc([8, 4], F32)
    B2 = k.alloc([8, 4], F32)
    GB = k.alloc([8, 4], F32)
    persist_top = k.top

    wpiece = k.alloc([8, 512], BF16)
    modps = PB[3][1]
    for nb in range(12):
        dma(POOL, wpiece.ap, wmod_d[:, :, nb * 512:(nb + 1) * 512], writes=[wpiece])
        for j in range(4):
            n = nb * 4 + j
            for kc in range(8):
                op(PE, lambda e, j=j, kc=kc, n=n: e.matmul(modps[:, n * 4:n * 4 + 4], wpiece[:, kc, j * 128:(j + 1) * 128],
                                                           scT[:, kc, :], start=(kc == 0), stop=(kc == 7)),
                   [wpiece, scT], [modps], inc=(kc == 7))
    bm3 = vecs[:, VOFF["bmod"]:VOFF["bmod"] + 48].rearrange("p (a b) -> p a b", b=1)
    op(DVE, lambda e: e.tensor_tensor(out=MOD.ap, in0=modps[:, 0:192].rearrange("p (a b) -> p a b", b=4),
                                      in1=bc_last(bm3, 4), op=ALU.add), [modps, vecs], [MOD])
    op(DVE, lambda e: e.tensor_scalar(out=SC1.ap, in0=MOD[:, 8:16, :], scalar1=1.0, scalar2=None, op0=ALU.add), [MOD], [SC1])
    bd3 = vecs[:, VOFF["bdn"]:VOFF["bdn"] + 8].rearrange("p (a b) -> p a b", b=1)
    op(DVE, lambda e: e.tensor_tensor(out=GB.ap, in0=MOD[:, 40:48, :], in1=bc_last(bd3, 4), op=ALU.mult), [MOD, vecs], [GB])
    op(DVE, lambda e: e.tensor_scalar(out=A2.ap, in0=MOD[:, 32:40, :], scalar1=1.0, scalar2=1.0 / ALPHA,
                                      op0=ALU.add, op1=ALU.mult), [MOD], [A2])
    op(DVE, lambda e: e.tensor_tensor(out=B2.ap, in0=GB.ap, in1=A2.ap, op=ALU.mult), [GB, A2], [B2])
    op(DVE, lambda e: e.tensor_tensor(out=B2.ap, in0=MOD[:, 24:32, :], in1=B2.ap, op=ALU.subtract), [MOD, B2], [B2])
    k.barrier()
    k.top = persist_top

    R2_off = (k.top + 31) // 32 * 32
    gated = [k.alloc([TL], BF16) for _ in range(16)]
    uT_off = k.top
    uT = k.alloc([8, T], BF16)
    work_top = k.top

    def h1s_view():
        ap = k.big[:, R2_off:R2_off + 8 * TL * 4].bitcast(F32).rearrange("p (a b) -> p a b", a=8)
        return Buf(ap)

    for b in range(NB):
        k.top = work_top
        xts = [k.alloc([D], F32) for _ in range(2)]
        tmpA = [k.alloc([8, 128], F32) for _ in range(2)]
        for tt in range(NCH):
            xt = xts[tt % 2]
            pp = PP[tt % 2]
            col = NB if tt < 2 else b
            dma(SP, xt.ap, xin[b, tt * 128:(tt + 1) * 128, :], writes=[xt])
            for kc in range(8):
                op(PE, lambda e, kc=kc, xt=xt, pp=pp: e.transpose(pp[:, kc * 128:(kc + 1) * 128], xt[:, kc * 128:(kc + 1) * 128], identf),
                   [xt, cst], [pp], inc=(kc == 7))
            tm = tmpA[tt % 2]
            p3 = pp.ap.rearrange("p (a b) -> p a b", a=8)
            op(DVE, lambda e, tm=tm, p3=p3, col=col: e.tensor_tensor(out=tm.ap, in0=p3, in1=bc_last(SC1[:, :, col:col + 1], 128), op=ALU.mult),
               [pp, SC1], [tm])
            op(DVE, lambda e, tm=tm, tt=tt, col=col: e.tensor_tensor(out=uT[:, :, tt * 128:(tt + 1) * 128], in0=tm.ap,
                                                                      in1=bc_last(MOD[:, 0:8, col:col + 1], 128), op=ALU.add),
               [tm, MOD], [uT])
        k.barrier()
        dump("uT%d" % b, uT)
        if stage == 1:
            k.barrier()
            return nc

        k.top = work_top
        TB = [(0, 256)] + [(256 + i * 512, 512) for i in range(4)]
        RSTm = k.alloc([768], BF16)
        dma(POOL, RSTm.ap, rst_d[:, 0:768], writes=[RSTm])
        RT = [k.alloc([T], BF16) for _ in range(2)]
        gla_base = k.top
        GM = k.alloc([NCH, 32], F32)
        LFN = k.alloc([NCH, 16], F32)
        NBt = k.alloc([NCH, 32], F32)
        WS = k.alloc([NCH, 16], F32)
        TH = k.alloc([NCH, 16], F32)
        EE = k.alloc([NCH, 16], F32)
        bgm = k.alloc([32], F32)
        dma(SP, bgm.ap, rows_d[6:7, 0:32].partition_broadcast(128), writes=[bgm])
        gp = PP[2]
        for tt in range(NCH):
            for kc in range(8):
                op(PE, lambda e, tt=tt, kc=kc: e.matmul(gp[:, tt * 32:(tt + 1) * 32], uT[:, kc, tt * 128:(tt + 1) * 128], ws[:, kc, 0:32],
                                                        start=(kc == 0), stop=(kc == 7)), [uT, ws], [gp], inc=(kc == 7))
        b3 = bgm.ap.rearrange("p (a b) -> p a b", a=1).broadcast_to([128, NCH, 32])
        op(DVE, lambda e: e.tensor_tensor(out=GM.ap, in0=gp[:, 0:NCH * 32].rearrange("p (a b) -> p a b", b=32), in1=b3, op=ALU.add),
           [gp, bgm], [GM])
        op(ACT, lambda e: e.activation(out=LFN.ap, in_=GM[:, :, 16:32], func=AF.Exp, scale=-1.0), [GM], [LFN])
        op(ACT, lambda e: e.activation(out=LFN.ap, in_=LFN.ap, func=AF.Ln, bias=1.0), [LFN], [LFN])
        gp2 = PP[3]
        for c in range(NCH):
            op(PE, lambda e, c=c: e.matmul(gp2[:, c * 32:c * 32 + 8], maskU, LFN[:, c, 0:8], start=True, stop=True), [cst, LFN], [gp2], inc=False)
            op(PE, lambda e, c=c: e.matmul(gp2[:, c * 32 + 8:c * 32 + 16], maskL, LFN[:, c, 8:16], start=True, stop=True), [cst, LFN], [gp2], inc=False)
            op(PE, lambda e, c=c: e.matmul(gp2[:, c * 32 + 16:c * 32 + 32], onesf, LFN[:, c, 0:16], start=True, stop=True), [cst, LFN], [gp2])
        op(DVE, lambda e: e.tensor_copy(out=NBt.ap, in_=gp2[:, 0:NCH * 32].rearrange("p (a b) -> p a b", b=32)), [gp2], [NBt])
        op(DVE, lambda e: e.tensor_tensor(out=WS.ap, in0=GM[:, :, 0:16], in1=NBt[:, :, 0:16], op=ALU.add), [GM, NBt], [WS])
        op(ACT, lambda e: e.activation(out=WS.ap, in_=WS.ap, func=AF.Exp), [WS], [WS])
        lnc = k.alloc([1], F32)
        op(DVE, lambda e: e.memset(lnc.ap, 0.5 * float(np.log(128.0))), [], [lnc])
        op(ACT, lambda e: e.activation(out=TH.ap, in_=NBt[:, :, 0:16], func=AF.Exp, bias=lnc[:, 0:1]), [NBt, lnc], [TH])
        op(ACT, lambda e: e.activation(out=EE.ap, in_=NBt[:, :, 16:32], func=AF.Exp, scale=-1.0), [NBt], [EE])
        for d in range(2):
            for (t0, n) in TB:
                pb = next_bank()
                for kc in range(8):
                    op(PE, lambda e, d=d, kc=kc, t0=t0, n=n, pb=pb: e.matmul(pb[0:16, 0:n], ws[:, kc, 32 + d * 16:48 + d * 16], uT[:, kc, t0:t0 + n],
                                                                             start=(kc == 0), stop=(kc == 7)), [ws, uT], [pb], inc=(kc == 7))
                op(ACT, lambda e, d=d, t0=t0, n=n, pb=pb: e.activation(out=RT[d][0:16, t0:t0 + n], in_=pb[0:16, 0:n], func=AF.Identity,
                                                                       bias=vecs[0:16, VOFF["br"] + d:VOFF["br"] + d + 1]), [pb, vecs], [RT[d]])
        mix_top = k.top

        def chain(qT, kT, vtok, Wv, colscale, decay, emit_out):
            C32 = [k.alloc([Wv], F32) for _ in range(2)]
            Cbf = [k.alloc([Wv], BF16) for _ in range(2)]
            Ctmp = [k.alloc([Wv], F32) for _ in range(2)]
            ktok = [[k.alloc([128], BF16) for _ in range(2)] for _ in range(2)]
            SPm = [[k.alloc([128], BF16) for _ in range(2)] for _ in range(2)]
            order = [list(range(NCH)), [1, 0] + list(range(NCH - 1, 1, -1))]
            for d in range(2):
                op(DVE, lambda e, d=d: e.memset(Cbf[d].ap, 0.0), [], [Cbf[d]])
            for step in range(NCH):
                for d in range(2):
                    c = order[d][step]
                    last = step == NCH - 1
                    is_lat = c >= 2
                    bS, bO = PB[2 + d]
                    kt, sp = ktok[d][step % 2], SPm[d][step % 2]
                    qc, kc_ = qT[d][:, c * 128:(c + 1) * 128], kT[d][:, c * 128:(c + 1) * 128]
                    cs = None if colscale is None else colscale(d, c)
                    if not last:
                        ptr = bS[:, 128:192].bitcast(BF16)
                        op(PE, lambda e, kc_=kc_, ptr=ptr: e.transpose(ptr, kc_, identb.ap), [kT[d], identb], [bS])
                        if cs is None:
                            op(ACT, lambda e, kt=kt, ptr=ptr: e.activation(out=kt.ap, in_=ptr, func=AF.Copy), [bS], [kt])
                        else:
                            op(ACT, lambda e, kt=kt, ptr=ptr, cs=cs: e.activation(out=kt.ap, in_=ptr, func=AF.Copy, scale=cs[0]), [bS, cs[1]], [kt])
                    if is_lat:
                        op(PE, lambda e, kc_=kc_, qc=qc, bS=bS: e.matmul(bS[:, 0:128], kc_, qc, start=True, stop=True), [kT[d], qT[d]], [bS])
                        msk = maskU if d == 0 else maskL
                        if cs is None:
                            op(DVE, lambda e, sp=sp, bS=bS, msk=msk: e.tensor_tensor(out=sp.ap, in0=bS[:, 0:128], in1=msk, op=ALU.mult), [bS, cst], [sp])
                        else:
                            op(DVE, lambda e, sp=sp, bS=bS, msk=msk, cs=cs: e.scalar_tensor_tensor(out=sp.ap, in0=bS[:, 0:128], scalar=cs[0], in1=msk,
                                                                                                   op0=ALU.mult, op1=ALU.mult), [bS, cst, cs[1]], [sp])
                        op(PE, lambda e, sp=sp, c=c, bO=bO: e.matmul(bO[:, 0:Wv], sp.ap, vtok[:, c, 0:Wv], start=True, stop=False), [sp, vtok], [bO], inc=False)
                        op(PE, lambda e, qc=qc, d=d, bO=bO: e.matmul(bO[:, 0:Wv], qc, Cbf[d].ap, start=False, stop=True), [qT[d], Cbf[d]], [bO])
                        emit_out(d, c - 2, bO, step)
                    if not last:
                        dc = decay(d, c)
                        op(PE, lambda e, kt=kt, c=c, bO=bO: e.matmul(bO[:, 256:256 + Wv], kt.ap, vtok[:, c, 0:Wv], start=True, stop=True), [kt, vtok], [bO])
                        if step == 0:
                            op(DVE, lambda e, d=d, bO=bO, dc=dc: e.tensor_scalar(out=C32[d].ap, in0=bO[:, 256:256 + Wv], scalar1=dc[0], scalar2=None, op0=ALU.mult),
                               [bO, dc[1]], [C32[d]])
                        else:
                            op(POOL, lambda e, d=d, dc=dc: e.tensor_scalar(out=Ctmp[d].ap, in0=C32[d].ap, scalar1=dc[0], scalar2=None, op0=ALU.mult),
                               [C32[d], dc[1]], [Ctmp[d]])
                            op(DVE, lambda e, d=d, bO=bO, dc=dc: e.scalar_tensor_tensor(out=C32[d].ap, in0=bO[:, 256:256 + Wv], scalar=dc[0], in1=Ctmp[d].ap,
                                                                                        op0=ALU.mult, op1=ALU.add), [bO, Ctmp[d], dc[1]], [C32[d]])
                        op(ACT, lambda e, d=d: e.activation(out=Cbf[d].ap, in_=C32[d].ap, func=AF.Copy), [C32[d]], [Cbf[d]])

        def load_vtok(W, wcol0, nfeat_chunks, bias_name, bias_i0, vtok, vTb):
            for bi, (t0, n) in enumerate(TB):
                for f in range(nfeat_chunks):
                    pb = next_bank()
                    for kc in range(8):
                        op(PE, lambda e, kc=kc, f=f, t0=t0, n=n, pb=pb: e.matmul(pb[:, 0:n], W[:, kc, wcol0 + f * 128:wcol0 + (f + 1) * 128], uT[:, kc, t0:t0 + n],
                                                                                 start=(kc == 0), stop=(kc == 7)), [W, uT], [pb], inc=(kc == 7))
                    vb = vTb[(bi * nfeat_chunks + f) % 2]
                    op(ACT, lambda e, f=f, n=n, pb=pb, vb=vb: e.activation(out=vb[:, 0:n], in_=pb[:, 0:n], func=AF.Identity, bias=V(bias_name, bias_i0 + f)),
                       [pb, vecs], [vb])
                    pt = next_bank()
                    ptb = pt.ap.bitcast(BF16)
                    nj = n // 128
                    for j in range(nj):
                        op(PE, lambda e, j=j, vb=vb, ptb=ptb: e.transpose(ptb[:, j * 128:(j + 1) * 128], vb[:, j * 128:(j + 1) * 128], identb.ap),
                           [vb, identb], [pt], inc=(j == nj - 1))
                    c0 = t0 // 128
                    op(DVE, lambda e, f=f, nj=nj, c0=c0, ptb=ptb: e.tensor_copy(out=vtok[:, c0:c0 + nj, f * 128:(f + 1) * 128],
                                                                                in_=ptb[:, 0:nj * 128].rearrange("p (a b) -> p a b", a=nj)), [pt], [vtok])

        k.top = mix_top
        Wm = k.alloc([8, 512], BF16)
        Pq = [k.alloc([T + 3], F32) for _ in range(2)]
        Y = k.alloc([T + 1], F32)
        qkT = [k.alloc([T], BF16) for _ in range(2)]
        vtok = k.alloc([NCH, 129], BF16)
        vTb = [k.alloc([512], BF16) for _ in range(2)]
        RAW = k.alloc([16, 128], F32)
        sig = [k.alloc([512], F32) for _ in range(2)]
        st = k.alloc([16, 8], F32)
        den3 = k.alloc([2, 4], F32)
        for p_ in Pq:
            op(DVE, lambda e, p_=p_: e.memset(p_.ap, 0.0), [], [p_])
        op(DVE, lambda e: e.memset(vtok.ap, 1.0), [], [vtok])
        chain_top = k.top
        for h in range(8):
            k.top = chain_top
            dma(POOL, Wm.ap, wm_d[h], writes=[Wm])
            for kind in range(2):
                P_ = Pq[kind]
                for (t0, n) in TB:
                    pb = next_bank()
                    for kc in range(8):
                        op(PE, lambda e, kc=kc, kind=kind, t0=t0, n=n, pb=pb: e.matmul(pb[:, 0:n], Wm[:, kc, kind * 128:(kind + 1) * 128], uT[:, kc, t0:t0 + n],
                                                                                       start=(kc == 0), stop=(kc == 7)), [Wm, uT], [pb], inc=(kc == 7))
                    pc = 1 if t0 == 0 else t0 + 2
                    op(ACT, lambda e, kind=kind, n=n, pb=pb, pc=pc, P_=P_: e.activation(out=P_[:, pc:pc + n], in_=pb[:, 0:n], func=AF.Identity,
                                                                                         bias=V("bq_m" if kind == 0 else "bk_m", h)), [pb, vecs], [P_])
                ci = kind * 8 + h
                op(DVE, lambda e, P_=P_, ci=ci: e.tensor_scalar(out=Y.ap, in0=P_[:, 1:T + 2], scalar1=V("cw1", ci), scalar2=None, op0=ALU.mult), [P_, vecs], [Y])
                op(DVE, lambda e, P_=P_, ci=ci: e.scalar_tensor_tensor(out=Y.ap, in0=P_[:, 0:T + 1], scalar=V("cw0", ci), in1=Y.ap, op0=ALU.mult, op1=ALU.add),
                   [P_, vecs, Y], [Y])
                op(DVE, lambda e, P_=P_, ci=ci: e.scalar_tensor_tensor(out=Y.ap, in0=P_[:, 2:T + 3], scalar=V("cw2", ci), in1=Y.ap, op0=ALU.mult, op1=ALU.add),
                   [P_, vecs, Y], [Y])
                op(ACT, lambda e, kind=kind: e.activation(out=qkT[kind][:, 0:TC], in_=Y[:, 0:TC], func=AF.Silu), [Y], [qkT[kind]])
                op(ACT, lambda e, kind=kind: e.activation(out=qkT[kind][:, TC:T], in_=Y[:, TC + 1:T + 1], func=AF.Silu), [Y], [qkT[kind]])
            load_vtok(Wm, 256, 1, "bv_m", h, vtok, vTb)

            def m_out(d, ci, bO, step, h=h):
                c = ci + 2
                col = d * 8 + h
                dn = den3[:, d, :]
                op(DVE, lambda e: e.tensor_scalar(out=dn[:, 0:1], in0=bO[:, 128:129], scalar1=-1.0, scalar2=None, op0=ALU.mult), [bO], [den3])
                op(DVE, lambda e: e.scalar_tensor_tensor(out=dn[:, 1:2], in0=bO[:, 128:129], scalar=TH[:, c, col:col + 1], in1=dn[:, 0:1],
                                                         op0=ALU.max, op1=ALU.max), [bO, TH, den3], [den3])
                op(DVE, lambda e: e.reciprocal(out=dn[:, 2:3], in_=dn[:, 1:2]), [den3], [den3])
                mine = ci + 2 if d == 0 else NCH - 1 - ci
                other = NCH - 1 - ci if d == 0 else ci + 2
                first = mine < other or (mine == other and d == 0)
                if first:
                    op(ACT, lambda e: e.activation(out=RAW[:, ci, :], in_=bO[:, 0:128], func=AF.Copy, scale=dn[:, 2:3]), [bO, den3], [RAW])
                else:
                    op(DVE, lambda e: e.scalar_tensor_tensor(out=RAW[:, ci, :], in0=bO[:, 0:128], scalar=dn[:, 2:3], in1=RAW[:, ci, :],
                                                             op0=ALU.mult, op1=ALU.add), [bO, den3, RAW], [RAW])

            chain([qkT[0], qkT[0]], [qkT[1], qkT[1]], vtok, 129,
                  lambda d, c, h=h: (WS[:, c, d * 8 + h:d * 8 + h + 1], WS),
                  lambda d, c, h=h: (EE[:, c, d * 8 + h:d * 8 + h + 1], EE), m_out)
            SQ = Y.ap[:, 0:2048].rearrange("p (a b) -> p a b", a=16)
            op(DVE, lambda e: e.tensor_reduce(out=st[:, :, 0], in_=RAW.ap, axis=AX.X, op=ALU.add), [RAW], [st])
            op(ACT, lambda e: e.activation(out=SQ, in_=RAW.ap, func=AF.Square), [RAW], [Y])
            op(DVE, lambda e: e.tensor_reduce(out=st[:, :, 1], in_=SQ, axis=AX.X, op=ALU.add), [Y], [st])
            op(DVE, lambda e: e.tensor_scalar(out=st[:, :, 2], in0=st[:, :, 0], scalar1=1.0 / 128, scalar2=None, op0=ALU.mult), [st], [st])
            op(DVE, lambda e: e.tensor_tensor(out=st[:, :, 3], in0=st[:, :, 2], in1=st[:, :, 2], op=ALU.mult), [st], [st])
            op(DVE, lambda e: e.scalar_tensor_tensor(out=st[:, :, 4], in0=st[:, :, 1], scalar=1.0 / 128, in1=st[:, :, 3], op0=ALU.mult, op1=ALU.subtract),
               [st], [st])
            op(DVE, lambda e: e.tensor_scalar(out=st[:, :, 4], in0=st[:, :, 4], scalar1=EPS, scalar2=None, op0=ALU.add), [st], [st])
            op(ACT, lambda e: e.activation(out=st[:, :, 5], in_=st[:, :, 4], func=AF.Sqrt), [st], [st])
            op(DVE, lambda e: e.reciprocal(out=st[:, :, 6], in_=st[:, :, 5]), [st], [st])
            op(DVE, lambda e: e.tensor_tensor(out=RAW.ap, in0=RAW.ap, in1=bc_last(st[:, :, 2:3], 128), op=ALU.subtract), [RAW, st], [RAW])
            op(DVE, lambda e: e.tensor_tensor(out=RAW.ap, in0=RAW.ap, in1=bc_last(st[:, :, 6:7], 128), op=ALU.mult), [RAW, st], [RAW])
            for tb in range(4):
                t0 = 256 + tb * 512
                pb = next_bank()
                for kc in range(8):
                    op(PE, lambda e, kc=kc, t0=t0, pb=pb: e.matmul(pb.ap, Wm[:, kc, 384:512], uT[:, kc, t0:t0 + 512], start=(kc == 0), stop=(kc == 7)),
                       [Wm, uT], [pb], inc=(kc == 7))
                sg = sig[tb % 2]
                op(ACT, lambda e, pb=pb, sg=sg: e.activation(out=sg.ap, in_=pb.ap, func=AF.Sigmoid, bias=V("bo_m", h)), [pb, vecs], [sg])
                pt = next_bank()
                for j in range(4):
                    op(PE, lambda e, j=j, tb=tb, pt=pt: e.transpose(pt[:, j * 128:(j + 1) * 128], RAW[:, tb * 4 + j, :], identf), [RAW, cst], [pt], inc=(j == 3))
                op(DVE, lambda e, tb=tb, pt=pt, sg=sg: e.scalar_tensor_tensor(out=gated[h][:, tb * 512:(tb + 1) * 512], in0=pt.ap, scalar=V("gn_m", h), in1=sg.ap,
                                                                              op0=ALU.mult, op1=ALU.mult), [pt, vecs, sg], [gated[h]])
        k.barrier()
        if b == 0:
            for h in (0, 7):
                dump("gm%d" % h, gated[h])
        if stage == 2:
            k.barrier()
            return nc

        k.top = gla_base
        Wg = k.alloc([8, 768], BF16)
        HT = 768
        Qr = k.alloc([HT], F32)
        Kr = k.alloc([HT], F32)
        T1 = k.alloc([HT], F32)
        T2 = k.alloc([HT], F32)
        qt = [k.alloc([T], BF16) for _ in range(2)]
        ktl = [k.alloc([T], BF16) for _ in range(2)]
        vtg = k.alloc([NCH, 256], BF16)
        vTg = [k.alloc([512], BF16) for _ in range(2)]
        RAWg = k.alloc([16, 256], F32)
        DEC = [k.alloc([NCH], F32) for _ in range(2)]
        TOT = k.alloc([6], F32)
        sg2 = [k.alloc([512], F32) for _ in range(2)]
        stg = k.alloc([16, 4], F32)
        chain_top = k.top
        HALF = [(0, 768, [(0, 256), (256, 512)]), (768, 768, [(768, 512), (1280, 256)]), (1536, 768, [(1536, 512), (2048, 256)])]
        for g in range(4):
            k.top = chain_top
            dma(POOL, Wg.ap, wg_d[g], writes=[Wg])
            for (h0, hn, blocks) in HALF:
                nch = hn // 128
                c0 = h0 // 128
                for kind, dst in ((0, Qr), (1, Kr)):
                    for (t0, n) in blocks:
                        pb = next_bank()
                        for kc in range(8):
                            op(PE, lambda e, kc=kc, kind=kind, t0=t0, n=n, pb=pb: e.matmul(pb[:, 0:n], Wg[:, kc, kind * 128:(kind + 1) * 128], uT[:, kc, t0:t0 + n],
                                                                                           start=(kc == 0), stop=(kc == 7)), [Wg, uT], [pb], inc=(kc == 7))
                        op(ACT, lambda e, kind=kind, t0=t0, n=n, pb=pb, dst=dst: e.activation(out=dst[:, t0 - h0:t0 - h0 + n], in_=pb[:, 0:n], func=AF.Identity,
                                                                                               bias=V("bq_g" if kind == 0 else "bk_g", g)), [pb, vecs], [dst])
                for d in range(2):
                    for (t0, n) in blocks:
                        pb = next_bank()
                        op(PE, lambda e, d=d, t0=t0, n=n, pb=pb: e.matmul(pb[:, 0:n], gwb[0:16, d * 512 + g * 128:d * 512 + (g + 1) * 128], RT[d][0:16, t0:t0 + n],
                                                                          start=True, stop=True), [gwb, RT[d]], [pb])
                        op(ACT, lambda e, d=d, t0=t0, n=n, pb=pb: e.activation(out=T1[:, t0 - h0:t0 - h0 + n], in_=pb[:, 0:n], func=AF.Exp, scale=-1.0,
                                                                               bias=ngb[:, d * 4 + g:d * 4 + g + 1]), [pb, ngb], [T1])
                    op(ACT, lambda e: e.activation(out=T1[:, 0:hn], in_=T1[:, 0:hn], func=AF.Ln, bias=1.0), [T1], [T1])
                    op(DVE, lambda e: e.tensor_tensor_scan(out=T2[:, 0:hn], data0=RSTm[:, 0:hn], data1=T1[:, 0:hn], initial=0.0, op0=ALU.mult, op1=ALU.add),
                       [RSTm, T1], [T2])
                    t23 = T2[:, 0:hn].rearrange("p (a b) -> p a b", b=128)
                    if d == 1:
                        op(DVE, lambda e: e.tensor_copy(out=TOT[:, 0:nch], in_=t23[:, :, 127]), [T2], [TOT])
                        op(DVE, lambda e: e.tensor_tensor(out=T2[:, 0:hn], in0=T1[:, 0:hn], in1=T2[:, 0:hn], op=ALU.subtract), [T1, T2], [T2])
                        tot3 = TOT[:, 0:nch].rearrange("p (a b) -> p a b", b=1)
                        op(DVE, lambda e: e.tensor_tensor(out=t23, in0=t23, in1=bc_last(tot3, 128), op=ALU.add), [T2, TOT], [T2])
                    op(ACT, lambda e: e.activation(out=T1[:, 0:hn], in_=T2[:, 0:hn], func=AF.Exp, scale=-1.0 / 16), [T2], [T1])
                    t13 = T1[:, 0:hn].rearrange("p (a b) -> p a b", b=128)
                    op(DVE, lambda e, d=d: e.tensor_copy(out=DEC[d][:, c0:c0 + nch], in_=t13[:, :, 127 if d == 0 else 0]), [T1], [DEC[d]])
                    op(DVE, lambda e, d=d: e.tensor_tensor(out=qt[d][:, h0:h0 + hn], in0=Qr[:, 0:hn], in1=T1[:, 0:hn], op=ALU.mult), [Qr, T1], [qt[d]])
                    op(ACT, lambda e: e.activation(out=T1[:, 0:hn], in_=T2[:, 0:hn], func=AF.Exp, scale=1.0 / 16), [T2], [T1])
                    op(DVE, lambda e, d=d: e.tensor_tensor(out=ktl[d][:, h0:h0 + hn], in0=Kr[:, 0:hn], in1=T1[:, 0:hn], op=ALU.mult), [Kr, T1], [ktl[d]])
            load_vtok(Wg, 256, 2, "bv_g", g * 2, vtg, vTg)

            def g_out(d, ci, bO, step):
                mine = ci + 2 if d == 0 else NCH - 1 - ci
                other = NCH - 1 - ci if d == 0 else ci + 2
                first = mine < other or (mine == other and d == 0)
                if first:
                    op(ACT, lambda e: e.activation(out=RAWg[:, ci, :], in_=bO[:, 0:256], func=AF.Copy), [bO], [RAWg])
                else:
                    op(DVE, lambda e: e.tensor_tensor(out=RAWg[:, ci, :], in0=bO[:, 0:256], in1=RAWg[:, ci, :], op=ALU.add), [bO, RAWg], [RAWg])

            chain(qt, ktl, vtg, 256, None, lambda d, c: (DEC[d][:, c:c + 1], DEC[d]), g_out)
            for ci in range(16):
                op(ACT, lambda e, ci=ci: e.activation(out=T1[:, 0:256], in_=RAWg[:, ci, :], func=AF.Square, accum_out=stg[:, ci, 0:1]), [RAWg], [T1, stg])
            op(DVE, lambda e: e.tensor_scalar(out=stg[:, :, 1], in0=stg[:, :, 0], scalar1=1.0 / 256, scalar2=EPS * 128.0, op0=ALU.mult, op1=ALU.add), [stg], [stg])
            op(ACT, lambda e: e.activation(out=stg[:, :, 2], in_=stg[:, :, 1], func=AF.Sqrt), [stg], [stg])
            op(DVE, lambda e: e.reciprocal(out=stg[:, :, 3], in_=stg[:, :, 2]), [stg], [stg])
            op(DVE, lambda e: e.tensor_tensor(out=RAWg.ap, in0=RAWg.ap, in1=bc_last(stg[:, :, 3:4], 256), op=ALU.mult), [RAWg, stg], [RAWg])
            for tb in range(4):
                t0 = 256 + tb * 512
                for fh in range(2):
                    pb = next_bank()
                    for kc in range(8):
                        op(PE, lambda e, kc=kc, t0=t0, fh=fh, pb=pb: e.matmul(pb.ap, Wg[:, kc, 512 + fh * 128:512 + (fh + 1) * 128], uT[:, kc, t0:t0 + 512],
                                                                              start=(kc == 0), stop=(kc == 7)), [Wg, uT], [pb], inc=(kc == 7))
                    sg = sg2[fh]
                    op(ACT, lambda e, pb=pb, sg=sg, fh=fh: e.activation(out=sg.ap, in_=pb.ap, func=AF.Silu, bias=V("bz_g", g * 2 + fh)), [pb, vecs], [sg])
                    pt = next_bank()
                    for j in range(4):
                        op(PE, lambda e, j=j, tb=tb, fh=fh, pt=pt: e.transpose(pt[:, j * 128:(j + 1) * 128], RAWg[:, tb * 4 + j, fh * 128:(fh + 1) * 128], identf),
                           [RAWg, cst], [pt], inc=(j == 3))
                    gi = 8 + g * 2 + fh
                    op(DVE, lambda e, tb=tb, pt=pt, sg=sg, gi=gi, fh=fh: e.scalar_tensor_tensor(out=gated[gi][:, tb * 512:(tb + 1) * 512], in0=pt.ap,
                                                                                                scalar=V("gn_g", g * 2 + fh), in1=sg.ap, op0=ALU.mult, op1=ALU.mult),
                       [pt, vecs, sg], [gated[gi]])
        k.barrier()
        if b == 0:
            for gi in (8, 15):
                dump("gg%d" % gi, gated[gi])
        if stage == 3:
            k.barrier()
            return nc

        k.top = work_top
        mT = [k.alloc([TL], BF16) for _ in range(8)]
        c2_top = k.top
        Wc = [k.alloc([8, 512], BF16) for _ in range(2)]
        SA = k.alloc([512], F32)
        SBb = k.alloc([512], F32)
        M1 = k.alloc([512], F32)
        M2 = k.alloc([512], F32)
        for n in range(8):
            W = Wc[n % 2]
            dma(POOL, W.ap, wc1_d[n], writes=[W])
            for tb in range(4):
                l0 = tb * 512
                pa, pbk, pm, pg = next_bank(), next_bank(), next_bank(), next_bank()
                for kc in range(8):
                    op(PE, lambda e, kc=kc, pa=pa, l0=l0, W=W: e.matmul(pa.ap, W[:, kc, 256:384], uT[:, kc, TC + l0:TC + l0 + 512], start=(kc == 0), stop=(kc == 7)),
                       [W, uT], [pa], inc=(kc == 7))
                op(ACT, lambda e, pa=pa: e.activation(out=SA.ap, in_=pa.ap, func=AF.Sigmoid, bias=V("ba", n)), [pa, vecs], [SA])
                for kc in range(8):
                    op(PE, lambda e, kc=kc, pbk=pbk, l0=l0, W=W: e.matmul(pbk.ap, W[:, kc, 384:512], uT[:, kc, TC + l0:TC + l0 + 512], start=(kc == 0), stop=(kc == 7)),
                       [W, uT], [pbk], inc=(kc == 7))
                op(ACT, lambda e, pbk=pbk: e.activation(out=SBb.ap, in_=pbk.ap, func=AF.Sigmoid, bias=V("bb", n)), [pbk, vecs], [SBb])
                for kc in range(8):
                    op(PE, lambda e, kc=kc, pm=pm, l0=l0, W=W: e.matmul(pm.ap, W[:, kc, 0:128], gated[kc][:, l0:l0 + 512], start=(kc == 0), stop=(kc == 7)),
                       [W, gated[kc]], [pm], inc=(kc == 7))
                op(DVE, lambda e, pm=pm: e.tensor_tensor(out=M1.ap, in0=pm.ap, in1=SA.ap, op=ALU.mult), [pm, SA], [M1])
                for kc in range(8):
                    op(PE, lambda e, kc=kc, pg=pg, l0=l0, W=W: e.matmul(pg.ap, W[:, kc, 128:256], gated[8 + kc][:, l0:l0 + 512], start=(kc == 0), stop=(kc == 7)),
                       [W, gated[8 + kc]], [pg], inc=(kc == 7))
                op(DVE, lambda e, pg=pg: e.tensor_tensor(out=M2.ap, in0=pg.ap, in1=SBb.ap, op=ALU.mult), [pg, SBb], [M2])
                op(DVE, lambda e, n=n, l0=l0: e.tensor_tensor(out=mT[n][:, l0:l0 + 512], in0=M1.ap, in1=M2.ap, op=ALU.add), [M1, M2], [mT[n]])
        k.barrier()
        if b == 0:
            dump("mT0", mT[0])
        if stage == 4:
            k.barrier()
            return nc

        k.top = c2_top
        h1s = h1s_view()
        wout = k.alloc([8, D], BF16)
        dma(POOL, wout.ap, wout_d, writes=[wout])
        G1bc = k.alloc([D], F32)
        LN1G = k.alloc([D], F32)
        LN1B = k.alloc([D], F32)
        dma(SP, LN1G.ap, rows_d[0:1, :].partition_broadcast(128), writes=[LN1G])
        dma(SP, LN1B.ap, rows_d[1:2, :].partition_broadcast(128), writes=[LN1B])
        dma(SP, G1bc.ap, rows_d[4:5, :].partition_broadcast(128), writes=[G1bc])
        xts = [k.alloc([D], F32) for _ in range(2)]
        Rb = [k.alloc([D], F32) for _ in range(2)]
        stc = k.alloc([2, 6], F32)
        mv = k.alloc([8], F32)
        c2w_top = k.top
        wg1 = k.alloc([8, D], BF16)
        screp = k.alloc([8, 128], BF16)
        dma(POOL, wg1.ap, wmod_d[:, :, 2 * D:3 * D], writes=[wg1])
        op(DVE, lambda e: e.tensor_copy(out=screp.ap, in_=bc_last(scT[:, :, b:b + 1], 128)), [scT], [screp])
        pg1 = PP[0]
        for hf in range(2):
            for kc in range(8):
                op(PE, lambda e, hf=hf, kc=kc: e.matmul(pg1[:, hf * 512:(hf + 1) * 512], screp[:, kc, :], wg1[:, kc, hf * 512:(hf + 1) * 512],
                                                        start=(kc == 0), stop=(kc == 7)), [screp, wg1], [pg1], inc=(kc == 7 and hf == 1))
        op(DVE, lambda e: e.tensor_tensor(out=G1bc.ap, in0=pg1.ap, in1=G1bc.ap, op=ALU.add), [pg1, G1bc], [G1bc])
        k.barrier()
        k.top = c2w_top
        for tt in range(16):
            xt, R = xts[tt % 2], Rb[tt % 2]
            pm, ptr = PP[tt % 2], PP[2 + tt % 2]
            dma(SP, xt.ap, xin[b, TC + tt * 128:TC + (tt + 1) * 128, :], writes=[xt])
            for hf in range(2):
                for kc in range(8):
                    op(PE, lambda e, hf=hf, kc=kc, tt=tt, pm=pm: e.matmul(pm[:, hf * 512:(hf + 1) * 512], mT[kc][:, tt * 128:(tt + 1) * 128], wout[:, kc, hf * 512:(hf + 1) * 512],
                                                                          start=(kc == 0), stop=(kc == 7)), [mT[kc], wout], [pm], inc=(kc == 7 and hf == 1))
            op(DVE, lambda e, R=R, pm=pm: e.tensor_tensor(out=R.ap, in0=pm.ap, in1=G1bc.ap, op=ALU.mult), [pm, G1bc], [R])
            op(DVE, lambda e, R=R, xt=xt: e.scalar_tensor_tensor(out=R.ap, in0=xt.ap, scalar=ALPHA, in1=R.ap, op0=ALU.mult, op1=ALU.add), [xt, R], [R])
            for hf in range(2):
                op(DVE, lambda e, R=R, hf=hf: e.bn_stats(out=stc[:, hf, :], in_=R[:, hf * 512:(hf + 1) * 512]), [R], [stc])
            op(DVE, lambda e: e.bn_aggr(out=mv[:, 0:2], in_=stc.ap.rearrange("p a b -> p (a b)")), [stc], [mv])
            op(DVE, lambda e: e.tensor_scalar(out=mv[:, 2:3], in0=mv[:, 1:2], scalar1=EPS, scalar2=None, op0=ALU.add), [mv], [mv])
            op(ACT, lambda e: e.activation(out=mv[:, 3:4], in_=mv[:, 2:3], func=AF.Sqrt), [mv], [mv])
            op(DVE, lambda e: e.reciprocal(out=mv[:, 4:5], in_=mv[:, 3:4]), [mv], [mv])
            op(DVE, lambda e: e.tensor_scalar(out=mv[:, 5:6], in0=mv[:, 0:1], scalar1=-1.0, scalar2=mv[:, 4:5], op0=ALU.mult, op1=ALU.mult), [mv], [mv])
            op(DVE, lambda e, R=R: e.tensor_scalar(out=R.ap, in0=R.ap, scalar1=mv[:, 4:5], scalar2=mv[:, 5:6], op0=ALU.mult, op1=ALU.add), [R, mv], [R])
            op(DVE, lambda e, R=R: e.tensor_tensor(out=R.ap, in0=R.ap, in1=LN1G.ap, op=ALU.mult), [R, LN1G], [R])
            op(DVE, lambda e, R=R: e.tensor_tensor(out=R.ap, in0=R.ap, in1=LN1B.ap, op=ALU.add), [R, LN1B], [R])
            if b == 0 and tt == 0:
                dump("h1t0", R)
            for kc in range(8):
                op(PE, lambda e, kc=kc, R=R, ptr=ptr: e.transpose(ptr[:, kc * 128:(kc + 1) * 128], R[:, kc * 128:(kc + 1) * 128], identf), [R, cst], [ptr], inc=(kc == 7))
            op(DVE, lambda e, tt=tt, ptr=ptr: e.scalar_tensor_tensor(out=h1s[:, :, tt * 128:(tt + 1) * 128], in0=ptr.ap.rearrange("p (a b) -> p a b", a=8),
                                                                      scalar=ALPHA, in1=bc_last(GB[:, :, b:b + 1], 128), op0=ALU.mult, op1=ALU.add),
               [ptr, GB], [h1s])
        k.barrier()

        k.top = uT_off
        stc = k.alloc([2, 6], F32)
        mv = k.alloc([8], F32)
        aT = [k.alloc([1024], BF16) for _ in range(NFC)]
        U2 = k.alloc([8, 1088], BF16)
        G = [k.alloc([18, 66], BF16) for _ in range(2)]
        DG = [k.alloc([9, 128], BF16) for _ in range(2)]
        Wu = [k.alloc([8, 256], BF16) for _ in range(2)]
        GL = [k.alloc([512], F32) for _ in range(2)]
        Wd = [k.alloc([NFC, 128], BF16) for _ in range(2)]
        R2T = k.alloc([8, 512], F32)
        LN2G = k.alloc([D], F32)
        LN2B = k.alloc([D], F32)
        OT = [k.alloc([D], F32) for _ in range(2)]
        dma(SP, LN2G.ap, rows_d[2:3, :].partition_broadcast(128), writes=[LN2G])
        dma(SP, LN2B.ap, rows_d[3:4, :].partition_broadcast(128), writes=[LN2B])
        for hf in range(2):
            t0 = hf * 1024
            lo = 0 if hf == 0 else 960
            grow0 = 1 if hf == 0 else 0
            voff = t0 - lo
            for kc in range(8):
                op(DVE, lambda e, kc=kc, lo=lo: e.tensor_scalar(out=U2[:, kc, :], in0=h1s[:, kc, lo:lo + 1088], scalar1=A2[:, kc, b:b + 1], scalar2=B2[:, kc, b:b + 1],
                                                                op0=ALU.mult, op1=ALU.add), [h1s, A2, B2], [U2])
            for gb_ in G:
                op(POOL, lambda e, gb_=gb_: e.memset(gb_.ap, 0.0), [], [gb_])
            for c in range(NFC):
                W, Gc, Dg = Wu[c % 2], G[c % 2], DG[c % 2]
                dma(POOL, W.ap, wup_d[c], writes=[W])
                for tap in range(9):
                    op(POOL, lambda e, tap=tap, c=c, Dg=Dg: e.tensor_scalar(out=Dg[:, tap, :], in0=identb.ap, scalar1=V("fcw", tap * NFC + c), scalar2=None, op0=ALU.mult), [identb, vecs], [Dg])
                for (p0, n) in ((0, 512), (512, 512), (1024, 64)):
                    pb = next_bank()
                    for kc in range(8):
                        op(PE, lambda e, kc=kc, p0=p0, n=n, pb=pb, W=W: e.matmul(pb[:, 0:n], W[:, kc, 0:128], U2[:, kc, p0:p0 + n], start=(kc == 0), stop=(kc == 7)),
                           [W, U2], [pb], inc=(kc == 7))
                    r_ = grow0 + p0 // 64
                    nr = n // 64
                    op(ACT, lambda e, pb=pb, n=n, r_=r_, nr=nr, Gc=Gc, c=c: e.activation(out=Gc[:, r_:r_ + nr, 1:65], in_=pb[:, 0:n].rearrange("p (a b) -> p a b", b=64),
                                                                                         func=AF.Identity, bias=V("bup_g", c)), [pb, vecs], [Gc])
                for blk in range(2):
                    pc = next_bank()
                    for tap in range(9):
                        dr, dcol = tap // 3, tap % 3
                        op(PE, lambda e, tap=tap, dr=dr, dcol=dcol, blk=blk, pc=pc, Gc=Gc, Dg=Dg: e.matmul(pc.ap, Dg[:, tap, :], Gc[:, blk * 8 + dr:blk * 8 + dr + 8, dcol:dcol + 64],
                                                                                                            start=(tap == 0), stop=(tap == 8)), [Dg, Gc], [pc], inc=(tap == 8))
                    gl = GL[blk]
                    op(ACT, lambda e, pc=pc, gl=gl, c=c: e.activation(out=gl.ap, in_=pc.ap, func=AF.Gelu_apprx_tanh, bias=V("fcb", c)), [pc, vecs], [gl])
                    pv = next_bank()
                    for kc in range(8):
                        op(PE, lambda e, kc=kc, blk=blk, pv=pv, W=W: e.matmul(pv.ap, W[:, kc, 128:256], U2[:, kc, voff + blk * 512:voff + (blk + 1) * 512],
                                                                              start=(kc == 0), stop=(kc == 7)), [W, U2], [pv], inc=(kc == 7))
                    op(DVE, lambda e, pv=pv, gl=gl, c=c, blk=blk: e.scalar_tensor_tensor(out=aT[c][:, blk * 512:(blk + 1) * 512], in0=pv.ap, scalar=V("bup_v", c), in1=gl.ap,
                                                                                         op0=ALU.add, op1=ALU.mult), [pv, vecs, gl], [aT[c]])
            for tb in range(2):
                for n in range(8):
                    W = Wd[n % 2]
                    dma(POOL, W.ap, wdn_d[n], writes=[W])
                    pb = next_bank()
                    for kc in range(NFC):
                        op(PE, lambda e, kc=kc, pb=pb, W=W, tb=tb: e.matmul(pb.ap, W[:, kc, :], aT[kc][:, tb * 512:(tb + 1) * 512], start=(kc == 0), stop=(kc == NFC - 1)),
                           [W, aT[kc]], [pb], inc=(kc == NFC - 1))
                    tg = t0 + tb * 512
                    op(DVE, lambda e, n=n, pb=pb, tg=tg: e.scalar_tensor_tensor(out=R2T[:, n, :], in0=pb.ap, scalar=MOD[:, 40 + n, b:b + 1], in1=h1s[:, n, tg:tg + 512],
                                                                                op0=ALU.mult, op1=ALU.add), [pb, MOD, h1s], [R2T])
                for tt in range(4):
                    ptr = PP[2 + tt % 2]
                    ot = OT[tt % 2]
                    for n in range(8):
                        op(PE, lambda e, n=n, tt=tt, ptr=ptr: e.transpose(ptr[:, n * 128:(n + 1) * 128], R2T[:, n, tt * 128:(tt + 1) * 128], identf), [R2T, cst], [ptr], inc=(n == 7))
                    for q in range(2):
                        op(DVE, lambda e, q=q, ptr=ptr: e.bn_stats(out=stc[:, q, :], in_=ptr[:, q * 512:(q + 1) * 512]), [ptr], [stc])
                    op(DVE, lambda e: e.bn_aggr(out=mv[:, 0:2], in_=stc.ap.rearrange("p a b -> p (a b)")), [stc], [mv])
                    op(DVE, lambda e: e.tensor_scalar(out=mv[:, 2:3], in0=mv[:, 1:2], scalar1=EPS, scalar2=None, op0=ALU.add), [mv], [mv])
                    op(ACT, lambda e: e.activation(out=mv[:, 3:4], in_=mv[:, 2:3], func=AF.Sqrt), [mv], [mv])
                    op(DVE, lambda e: e.reciprocal(out=mv[:, 4:5], in_=mv[:, 3:4]), [mv], [mv])
                    op(DVE, lambda e: e.tensor_scalar(out=mv[:, 5:6], in0=mv[:, 0:1], scalar1=-1.0, scalar2=mv[:, 4:5], op0=ALU.mult, op1=ALU.mult), [mv], [mv])
                    op(DVE, lambda e, ot=ot, ptr=ptr: e.tensor_scalar(out=ot.ap, in0=ptr.ap, scalar1=mv[:, 4:5], scalar2=mv[:, 5:6], op0=ALU.mult, op1=ALU.add), [ptr, mv], [ot])
                    op(DVE, lambda e, ot=ot: e.tensor_tensor(out=ot.ap, in0=ot.ap, in1=LN2G.ap, op=ALU.mult), [ot, LN2G], [ot])
                    op(DVE, lambda e, ot=ot: e.tensor_tensor(out=ot.ap, in0=ot.ap, in1=LN2B.ap, op=ALU.add), [ot, LN2B], [ot])
                    r0 = t0 + tb * 512 + tt * 128
                    dma(SP, out_d[b, r0:r0 + 128, :], ot.ap, reads=[ot])
        k.barrier()
    k.barrier()
    return nc


def _fm(v, nchunk):
    return np.ascontiguousarray(np.asarray(v, np.float32).reshape(nchunk, 128).T)


def _kmaj(w):
    K, N = w.shape
    return np.ascontiguousarray(np.asarray(w, np.float32).reshape(K // 128, 128, N).transpose(1, 0, 2))


def prep_shared(inp):
    g = lambda n: np.asarray(inp[n], np.float32)[0]
    w_in, b_in = g("w_in"), g("b_in")
    o_q, o_k, o_v, o_o, o_g = 0, 1024, 2048, 3072, 4096
    o_qg = 4128
    o_kg, o_vg, o_zg, o_r = o_qg + 512, o_qg + 1024, o_qg + 2048, o_qg + 3072
    o_a = o_r + 32
    o_b = o_a + 1024
    vec = np.zeros((128, NV), np.float32)

    def put(name, arr):
        vec[:, VOFF[name]:VOFF[name] + arr.shape[1]] = arr

    put("bq_m", _fm(b_in[o_q:o_q + 1024], 8)); put("bk_m", _fm(b_in[o_k:o_k + 1024], 8))
    put("bv_m", _fm(b_in[o_v:o_v + 1024], 8)); put("bo_m", _fm(b_in[o_o:o_o + 1024], 8))
    cq = g("conv_qk")
    for j in range(3):
        put("cw%d" % j, _fm(cq[j], 16))
    put("bq_g", _fm(b_in[o_qg:o_qg + 512], 4)); put("bk_g", _fm(b_in[o_kg:o_kg + 512], 4))
    put("bv_g", _fm(b_in[o_vg:o_vg + 1024], 8)); put("bz_g", _fm(b_in[o_zg:o_zg + 1024], 8))
    put("gb_f", _fm(g("gla_gate_b_fwd"), 4)); put("gb_b", _fm(g("gla_gate_b_bwd"), 4))
    put("gn_m", _fm(g("mlstm_norm_g"), 8)); put("gn_g", _fm(g("gla_norm_g"), 8))
    put("ba", _fm(b_in[o_a:o_a + 1024], 8)); put("bb", _fm(b_in[o_b:o_b + 1024], 8))
    bup = g("b_up")
    put("bup_g", _fm(bup[:DFF], NFC)); put("bup_v", _fm(bup[DFF:], NFC))
    fcw = g("ffn_conv_w").reshape(9, DFF)
    put("fcw", np.concatenate([_fm(fcw[t], NFC) for t in range(9)], axis=1))
    put("fcb", _fm(g("ffn_conv_b"), NFC)); put("bdn", _fm(g("b_down"), 8)); put("bmod", _fm(g("b_mod"), 48))
    br = np.zeros((128, 2), np.float32)
    br[0:16, 0] = b_in[o_r:o_r + 16]
    br[0:16, 1] = b_in[o_r + 16:o_r + 32]
    put("br", br)
    wk = _kmaj(w_in)
    w_m = np.stack([np.concatenate([wk[:, :, o + h * 128:o + (h + 1) * 128] for o in (o_q, o_k, o_v, o_o)], axis=2) for h in range(8)])
    w_g = np.stack([np.concatenate([wk[:, :, o_qg + h * 128:o_qg + (h + 1) * 128], wk[:, :, o_kg + h * 128:o_kg + (h + 1) * 128],
                                    wk[:, :, o_vg + h * 256:o_vg + (h + 1) * 256], wk[:, :, o_zg + h * 256:o_zg + (h + 1) * 256]], axis=2) for h in range(4)])
    w_s = np.concatenate([wk[:, :, o_g:o_g + 32], wk[:, :, o_r:o_r + 32]], axis=2)
    wbm, wbg = _kmaj(g("w_branch_mlstm")), _kmaj(g("w_branch_gla"))
    wc1 = np.stack([np.concatenate([wbm[:, :, n * 128:(n + 1) * 128], wbg[:, :, n * 128:(n + 1) * 128],
                                    wk[:, :, o_a + n * 128:o_a + (n + 1) * 128], wk[:, :, o_b + n * 128:o_b + (n + 1) * 128]], axis=2) for n in range(8)])
    wu = _kmaj(g("w_up"))
    wup = np.stack([np.concatenate([wu[:, :, c * 128:(c + 1) * 128], wu[:, :, DFF + c * 128:DFF + (c + 1) * 128]], axis=2) for c in range(NFC)])
    wd = _kmaj(g("w_down"))
    wdn = np.stack([wd[:, :, n * 128:(n + 1) * 128] for n in range(8)])
    rows = np.zeros((8, D), np.float32)
    rows[0], rows[1], rows[2], rows[3] = g("ln1_g"), g("ln1_b"), g("ln2_g"), g("ln2_b")
    bm = g("b_mod")
    rows[4] = bm[2 * D:3 * D]
    rows[6, 0:32] = b_in[o_g:o_g + 32]
    idx = np.arange(128)
    cst = np.zeros((128, 4, 128), np.float32)
    cst[:, 0, :] = np.eye(128)
    cst[:, 1, :] = (idx[:, None] <= idx[None, :])
    cst[:, 2, :] = (idx[:, None] >= idx[None, :])
    cst[:, 3, :] = 1.0
    rst = np.ones((128, T), np.float32)
    rst[:, ::128] = 0.0
    c = lambda a: np.ascontiguousarray(a, dtype=np.float32)
    return dict(wmod=_kmaj(g("w_mod")), vecs=vec, w_m=c(w_m), w_g=c(w_g), w_s=c(w_s),
                gw=c(np.concatenate([g("gla_gate_w_fwd"), g("gla_gate_w_bwd")], axis=1)), rows=rows, wc1=c(wc1),
                wout=_kmaj(g("w_out")), wup=c(wup), wdn=c(wdn), cst=cst, rst=rst)


def prep_core(inp, bs):
    x, ctx, cvec, cctx = (np.asarray(inp[n], np.float32) for n in ("x", "ctx", "c", "c_ctx"))
    nb = len(bs)
    xin = np.ascontiguousarray(np.concatenate([ctx[bs], x[bs]], axis=1))
    cT = np.zeros((128, 8, 4), np.float32)
    for j, b in enumerate(bs):
        cT[:, :, j] = _fm(cvec[b], 8)
    cT[:, :, nb] = _fm(cctx, 8)
    return dict(xin=xin, cT=cT)


def kernel(**inputs):
    n = 8
    NB = 2
    shared = prep_shared(inputs)
    nc = build(NB)
    in_maps = []
    for i in range(n):
        m = dict(shared)
        m.update(prep_core(inputs, list(range(i * NB, (i + 1) * NB))))
        in_maps.append(m)
    res = run_bass_kernel_spmd(nc, in_maps, core_ids=list(range(n)))
    return np.concatenate([r["out"] for r in res.results], axis=0).astype(np.float32)
```

```python
import numpy as np
import concourse.bass as bass
import concourse.mybir as mybir
from concourse.bass_utils import run_bass_kernel_spmd

F32 = mybir.dt.float32
BF16 = mybir.dt.bfloat16
U8 = mybir.dt.uint8
AF = mybir.ActivationFunctionType
ALU = mybir.AluOpType
AX = mybir.AxisListType

D = 1024
TC = 256
TL = 2048
T = TC + TL
NCH = T // 128
DFF = 2816
NFC = DFF // 128
ALPHA = 2.0 ** 0.25
EPS = 1e-5
NSLOT = 24
ATTACH = True

_VSPEC = [("bq_m", 8), ("bk_m", 8), ("bv_m", 8), ("bo_m", 8), ("cw0", 16), ("cw1", 16), ("cw2", 16),
          ("bq_g", 4), ("bk_g", 4), ("bv_g", 8), ("bz_g", 8), ("gb_f", 4), ("gb_b", 4),
          ("gn_m", 8), ("gn_g", 8), ("ba", 8), ("bb", 8), ("bup_g", NFC), ("bup_v", NFC),
          ("fcw", 9 * NFC), ("fcb", NFC), ("bdn", 8), ("bmod", 48), ("br", 2)]
VOFF = {}
_o = 0
for _n, _c in _VSPEC:
    VOFF[_n] = _o
    _o += _c
NV = _o


class Own:
    def __init__(self, name, sem, inc):
        self.name, self.sem, self.inc, self.count = name, sem, inc, 0


class Iss:
    def __init__(self, h, own=None):
        self.h, self.own, self.seen = h, own, {}


class Buf:
    __slots__ = ("ap", "w", "r")

    def __init__(self, ap):
        self.ap, self.w, self.r = ap, None, {}

    def __getitem__(self, k):
        return self.ap[k]


class Pair:
    def __init__(self, ap, bufs):
        self.ap, self.bufs = ap, bufs

    def __getitem__(self, k):
        return self.ap[k]


def _flat(lst):
    out = []
    for t in lst:
        if isinstance(t, Pair):
            out.extend(t.bufs)
        else:
            out.append(t)
    return out


class KB:
    def __init__(self, nc, sbuf_bytes):
        self.nc = nc
        mk = lambda n, inc: Own(n, nc.alloc_semaphore(n), inc)
        self.oPE, self.oACT, self.oDVE, self.oPOOL = mk("pe", 1), mk("act", 1), mk("dve", 1), mk("pool", 1)
        self.PE, self.ACT = Iss(nc.tensor, self.oPE), Iss(nc.scalar, self.oACT)
        self.DVE, self.POOL = Iss(nc.vector, self.oDVE), Iss(nc.gpsimd, self.oPOOL)
        self.SP = Iss(nc.sync, None)
        self.slots = [mk("dma%d" % i, 16) for i in range(NSLOT)]
        self.hw_slots, self.sw_slots = self.slots[:NSLOT // 2], self.slots[NSLOT // 2:]
        self.dma_i = {id(self.hw_slots): 0, id(self.sw_slots): 0}
        self.big = nc.alloc_sbuf_tensor("big", [128, sbuf_bytes], U8)
        self.cap = sbuf_bytes
        self.top = 0

    def alloc(self, free, dt):
        sz = 4 if dt == F32 else 2
        n = 1
        for q in free:
            n *= q
        off = (self.top + 31) // 32 * 32
        self.top = off + n * sz
        assert self.top <= self.cap, ("SBUF overflow", self.top, self.cap)
        ap = self.big[:, off:off + n * sz].bitcast(dt)
        if len(free) == 2:
            ap = ap.rearrange("p (a b) -> p a b", a=free[0])
        elif len(free) == 3:
            ap = ap.rearrange("p (a b c) -> p a b c", a=free[0], b=free[1])
        return Buf(ap)

    def _waits(self, iss, reads, writes, attach=False):
        need = {}
        for t in reads:
            if t.w is not None:
                o, v = t.w
                if need.get(o, 0) < v:
                    need[o] = v
        for t in writes:
            if t.w is not None:
                o, v = t.w
                if need.get(o, 0) < v:
                    need[o] = v
            for o, v in t.r.items():
                if need.get(o, 0) < v:
                    need[o] = v
        todo = []
        for o, v in need.items():
            if o is iss.own and o is self.oPE:
                continue
            if iss.seen.get(o, 0) >= v:
                continue
            todo.append((o, v))
            iss.seen[o] = v
        last = todo.pop() if (attach and todo) else None
        for o, v in todo:
            iss.h.wait_ge(o.sem, v)
        return last

    @staticmethod
    def _record(own, v, reads, writes):
        for t in reads:
            if t.r.get(own, 0) < v:
                t.r[own] = v
        for t in writes:
            t.w = (own, v)
            t.r = {}

    def op(self, iss, fn, reads=(), writes=(), inc=True):
        reads, writes = _flat(reads), _flat(writes)
        last = self._waits(iss, reads, writes, attach=(ATTACH and iss is not self.PE))
        ins = fn(iss.h)
        if last is not None:
            ins._wait_ge(last[0].sem, last[1])
        own = iss.own
        if inc:
            own.count += 1
            ins.then_inc(own.sem, 1)
            v = own.count
        else:
            v = own.count + 1
        self._record(own, v, reads, writes)

    def dma(self, iss, out, in_, reads=(), writes=()):
        reads, writes = _flat(reads), _flat(writes)
        pool = self.sw_slots if iss is self.POOL else self.hw_slots
        slot = pool[self.dma_i[id(pool)] % len(pool)]
        self.dma_i[id(pool)] += 1
        if slot.count > iss.seen.get(slot, 0):
            iss.h.wait_ge(slot.sem, slot.count)
            iss.seen[slot] = slot.count
        last = self._waits(iss, reads, writes, attach=ATTACH)
        ins = iss.h.dma_start(out=out, in_=in_, max_dma_last_dim=4096) if iss is self.POOL else iss.h.dma_start(out=out, in_=in_)
        if last is not None:
            ins._wait_ge(last[0].sem, last[1])
        slot.count += 16
        ins.then_inc(slot.sem, 16)
        self._record(slot, slot.count, reads, writes)

    def barrier(self):
        owners = [self.oPE, self.oACT, self.oDVE, self.oPOOL] + self.slots
        for iss in (self.PE, self.ACT, self.DVE, self.POOL, self.SP):
            for o in owners:
                if o.count > iss.seen.get(o, 0):
                    iss.h.wait_ge(o.sem, o.count)
                    iss.seen[o] = o.count


def bc_last(ap, n):
    sh = list(ap.shape)
    sh[-1] = n
    return ap.broadcast_to(sh)


def build(NB=2, dbg=None, stage=99):
    nc = bass.Bass("TRN2", target_bir_lowering=False)
    dram = lambda n, s, k="ExternalInput": nc.dram_tensor(n, list(s), F32, kind=k).ap()
    xin = dram("xin", [NB, T, D])
    cT_d = dram("cT", [128, 8, 4])
    wmod_d = dram("wmod", [128, 8, 6 * D])
    vecs_d = dram("vecs", [128, NV])
    wm_d = dram("w_m", [8, 128, 8, 512])
    wg_d = dram("w_g", [4, 128, 8, 768])
    ws_d = dram("w_s", [128, 8, 64])
    gw_d = dram("gw", [16, 1024])
    rows_d = dram("rows", [8, D])
    wc1_d = dram("wc1", [8, 128, 8, 512])
    wout_d = dram("wout", [128, 8, D])
    wup_d = dram("wup", [NFC, 128, 8, 256])
    wdn_d = dram("wdn", [8, 128, NFC, 128])
    cst_d = dram("cst", [128, 4, 128])
    rst_d = dram("rst", [128, T])
    out_d = dram("out", [NB, TL, D], "ExternalOutput")
    dbg_d = {}
    if dbg:
        for n, (s, dt_) in dbg.items():
            dbg_d[n] = nc.dram_tensor("dbg_" + n, list(s), dt_, kind="ExternalOutput").ap()

    k = KB(nc, 206 * 1024)
    PE, ACT, DVE, POOL, SP = k.PE, k.ACT, k.DVE, k.POOL, k.SP
    op, dma = k.op, k.dma

    _pt = [nc.alloc_psum_tensor("pp%d" % i, [128, 1024], F32)[:, :] for i in range(4)]
    PB = [[Buf(_pt[i][:, 0:512]), Buf(_pt[i][:, 512:1024])] for i in range(4)]
    PP = [Pair(_pt[i], PB[i]) for i in range(4)]
    pb_rr = [0]

    def next_bank():
        i = pb_rr[0] % 4
        pb_rr[0] += 1
        return PB[i // 2][i % 2]

    def dump(name, buf, ap=None):
        if name in dbg_d:
            dma(SP, dbg_d[name], buf.ap if ap is None else ap, reads=[buf])

    cst = k.alloc([4, 128], F32)
    identf, maskU, maskL, onesf = cst[:, 0, :], cst[:, 1, :], cst[:, 2, :], cst[:, 3, :]
    dma(SP, cst.ap, cst_d, writes=[cst])
    identb = k.alloc([128], BF16)
    op(DVE, lambda e: e.tensor_copy(out=identb.ap, in_=identf), [cst], [identb])
    vecs = k.alloc([NV], F32)
    dma(SP, vecs.ap, vecs_d, writes=[vecs])
    V = lambda name, i=0: vecs[:, VOFF[name] + i:VOFF[name] + i + 1]
    cT = k.alloc([8, 4], F32)
    dma(SP, cT.ap, cT_d, writes=[cT])
    scT = k.alloc([8, 4], BF16)
    op(ACT, lambda e: e.activation(out=scT.ap, in_=cT.ap, func=AF.Silu), [cT], [scT])
    ws = k.alloc([8, 64], BF16)
    dma(POOL, ws.ap, ws_d, writes=[ws])
    gwb = k.alloc([1024], BF16)
    dma(POOL, gwb[0:16, :], gw_d, writes=[gwb])
    ngb = k.alloc([8], F32)
    op(DVE, lambda e: e.tensor_scalar(out=ngb.ap, in0=vecs[:, VOFF["gb_f"]:VOFF["gb_f"] + 8], scalar1=-1.0,
                                      scalar2=None, op0=ALU.mult), [vecs], [ngb])
    MOD = k.alloc([48, 4], F32)
    SC1 = k.alloc([8, 4], F32)
    A2 = k.alloc([8, 4], F32)
    B2 = k.alloc([8, 4], F32)
    GB = k.alloc([8, 4], F32)
    persist_top = k.top

    wpiece = k.alloc([8, 512], BF16)
    modps = PB[3][1]
    for nb in range(12):
        dma(POOL, wpiece.ap, wmod_d[:, :, nb * 512:(nb + 1) * 512], writes=[wpiece])
        for j in range(4):
            n = nb * 4 + j
            for kc in range(8):
                op(PE, lambda e, j=j, kc=kc, n=n: e.matmul(modps[:, n * 4:n * 4 + 4], wpiece[:, kc, j * 128:(j + 1) * 128],
                                                           scT[:, kc, :], start=(kc == 0), stop=(kc == 7)),
                   [wpiece, scT], [modps], inc=(kc == 7))
    bm3 = vecs[:, VOFF["bmod"]:VOFF["bmod"] + 48].rearrange("p (a b) -> p a b", b=1)
    op(DVE, lambda e: e.tensor_tensor(out=MOD.ap, in0=modps[:, 0:192].rearrange("p (a b) -> p a b", b=4),
                                      in1=bc_last(bm3, 4), op=ALU.add), [modps, vecs], [MOD])
    op(DVE, lambda e: e.tensor_scalar(out=SC1.ap, in0=MOD[:, 8:16, :], scalar1=1.0, scalar2=None, op0=ALU.add), [MOD], [SC1])
    bd3 = vecs[:, VOFF["bdn"]:VOFF["bdn"] + 8].rearrange("p (a b) -> p a b", b=1)
    op(DVE, lambda e: e.tensor_tensor(out=GB.ap, in0=MOD[:, 40:48, :], in1=bc_last(bd3, 4), op=ALU.mult), [MOD, vecs], [GB])
    op(DVE, lambda e: e.tensor_scalar(out=A2.ap, in0=MOD[:, 32:40, :], scalar1=1.0, scalar2=1.0 / ALPHA,
                                      op0=ALU.add, op1=ALU.mult), [MOD], [A2])
    op(DVE, lambda e: e.tensor_tensor(out=B2.ap, in0=GB.ap, in1=A2.ap, op=ALU.mult), [GB, A2], [B2])
    op(DVE, lambda e: e.tensor_tensor(out=B2.ap, in0=MOD[:, 24:32, :], in1=B2.ap, op=ALU.subtract), [MOD, B2], [B2])
    k.barrier()
    k.top = persist_top

    R2_off = (k.top + 31) // 32 * 32
    gated = [k.alloc([TL], BF16) for _ in range(16)]
    uT_off = k.top
    uT = k.alloc([8, T], BF16)
    work_top = k.top

    def h1s_view():
        ap = k.big[:, R2_off:R2_off + 8 * TL * 4].bitcast(F32).rearrange("p (a b) -> p a b", a=8)
        return Buf(ap)

    for b in range(NB):
        k.top = work_top
        xts = [k.alloc([D], F32) for _ in range(2)]
        tmpA = [k.alloc([8, 128], F32) for _ in range(2)]
        for tt in range(NCH):
            xt = xts[tt % 2]
            pp = PP[tt % 2]
            col = NB if tt < 2 else b
            dma(SP, xt.ap, xin[b, tt * 128:(tt + 1) * 128, :], writes=[xt])
            for kc in range(8):
                op(PE, lambda e, kc=kc, xt=xt, pp=pp: e.transpose(pp[:, kc * 128:(kc + 1) * 128], xt[:, kc * 128:(kc + 1) * 128], identf),
                   [xt, cst], [pp], inc=(kc == 7))
            tm = tmpA[tt % 2]
            p3 = pp.ap.rearrange("p (a b) -> p a b", a=8)
            op(DVE, lambda e, tm=tm, p3=p3, col=col: e.tensor_tensor(out=tm.ap, in0=p3, in1=bc_last(SC1[:, :, col:col + 1], 128), op=ALU.mult),
               [pp, SC1], [tm])
            op(DVE, lambda e, tm=tm, tt=tt, col=col: e.tensor_tensor(out=uT[:, :, tt * 128:(tt + 1) * 128], in0=tm.ap,
                                                                      in1=bc_last(MOD[:, 0:8, col:col + 1], 128), op=ALU.add),
               [tm, MOD], [uT])
        k.barrier()
        dump("uT%d" % b, uT)
        if stage == 1:
            k.barrier()
            return nc

        k.top = work_top
        TB = [(0, 256)] + [(256 + i * 512, 512) for i in range(4)]
        RSTm = k.alloc([768], BF16)
        dma(POOL, RSTm.ap, rst_d[:, 0:768], writes=[RSTm])
        RT = [k.alloc([T], BF16) for _ in range(2)]
        gla_base = k.top
        WS = k.alloc([NCH, 16], F32)
        TH = k.alloc([NCH, 16], F32)
        EE = k.alloc([NCH, 16], F32)
        mix_top = k.top
        GM = k.alloc([NCH, 32], F32)
        LFN = k.alloc([NCH, 16], F32)
        NBt = k.alloc([NCH, 32], F32)
        bgm = k.alloc([32], F32)
        dma(SP, bgm.ap, rows_d[6:7, 0:32].partition_broadcast(128), writes=[bgm])
        gp = PP[2]
        for tt in range(NCH):
            for kc in range(8):
                op(PE, lambda e, tt=tt, kc=kc: e.matmul(gp[:, tt * 32:(tt + 1) * 32], uT[:, kc, tt * 128:(tt + 1) * 128], ws[:, kc, 0:32],
                                                        start=(kc == 0), stop=(kc == 7)), [uT, ws], [gp], inc=(kc == 7))
        b3 = bgm.ap.rearrange("p (a b) -> p a b", a=1).broadcast_to([128, NCH, 32])
        op(DVE, lambda e: e.tensor_tensor(out=GM.ap, in0=gp[:, 0:NCH * 32].rearrange("p (a b) -> p a b", b=32), in1=b3, op=ALU.add),
           [gp, bgm], [GM])
        op(ACT, lambda e: e.activation(out=LFN.ap, in_=GM[:, :, 16:32], func=AF.Exp, scale=-1.0), [GM], [LFN])
        op(ACT, lambda e: e.activation(out=LFN.ap, in_=LFN.ap, func=AF.Ln, bias=1.0), [LFN], [LFN])
        gp2 = PP[3]
        for c in range(NCH):
            op(PE, lambda e, c=c: e.matmul(gp2[:, c * 32:c * 32 + 8], maskU, LFN[:, c, 0:8], start=True, stop=True), [cst, LFN], [gp2], inc=False)
            op(PE, lambda e, c=c: e.matmul(gp2[:, c * 32 + 8:c * 32 + 16], maskL, LFN[:, c, 8:16], start=True, stop=True), [cst, LFN], [gp2], inc=False)
            op(PE, lambda e, c=c: e.matmul(gp2[:, c * 32 + 16:c * 32 + 32], onesf, LFN[:, c, 0:16], start=True, stop=True), [cst, LFN], [gp2])
        op(DVE, lambda e: e.tensor_copy(out=NBt.ap, in_=gp2[:, 0:NCH * 32].rearrange("p (a b) -> p a b", b=32)), [gp2], [NBt])
        op(DVE, lambda e: e.tensor_tensor(out=WS.ap, in0=GM[:, :, 0:16], in1=NBt[:, :, 0:16], op=ALU.add), [GM, NBt], [WS])
        op(ACT, lambda e: e.activation(out=WS.ap, in_=WS.ap, func=AF.Exp), [WS], [WS])
        lnc = k.alloc([1], F32)
        op(DVE, lambda e: e.memset(lnc.ap, 0.5 * float(np.log(128.0))), [], [lnc])
        op(ACT, lambda e: e.activation(out=TH.ap, in_=NBt[:, :, 0:16], func=AF.Exp, bias=lnc[:, 0:1]), [NBt, lnc], [TH])
        op(ACT, lambda e: e.activation(out=EE.ap, in_=NBt[:, :, 16:32], func=AF.Exp, scale=-1.0), [NBt], [EE])
        for d in range(2):
            for (t0, n) in TB:
                pb = next_bank()
                for kc in range(8):
                    op(PE, lambda e, d=d, kc=kc, t0=t0, n=n, pb=pb: e.matmul(pb[0:16, 0:n], ws[:, kc, 32 + d * 16:48 + d * 16], uT[:, kc, t0:t0 + n],
                                                                             start=(kc == 0), stop=(kc == 7)), [ws, uT], [pb], inc=(kc == 7))
                op(ACT, lambda e, d=d, t0=t0, n=n, pb=pb: e.activation(out=RT[d][0:16, t0:t0 + n], in_=pb[0:16, 0:n], func=AF.Identity,
                                                                       bias=vecs[0:16, VOFF["br"] + d:VOFF["br"] + d + 1]), [pb, vecs], [RT[d]])
        k.barrier()

        cS = [PB[2 + d][1] for d in range(2)]
        cT_ = [PB[2 + d][0] for d in range(2)]
        cO = [PB[0][d] for d in range(2)]
        cC = [PB[1][d] for d in range(2)]

        def chain(qT, kT, vtok, Wv, colscale, decay, emit_out):
            D32 = [k.alloc([Wv], F32) for _ in range(2)]
            Cbf = [k.alloc([Wv], BF16) for _ in range(2)]
            ktok = [[k.alloc([128], BF16) for _ in range(2)] for _ in range(2)]
            SPm = [[k.alloc([128], BF16) for _ in range(2)] for _ in range(2)]
            order = [list(range(NCH)), [1, 0] + list(range(NCH - 1, 1, -1))]
            for d in range(2):
                op(DVE, lambda e, d=d: e.memset(Cbf[d].ap, 0.0), [], [Cbf[d]])

            def stageA(step):
                p = step % 2
                for d in range(2):
                    c = order[d][step]
                    kc_ = kT[d][:, c * 128:(c + 1) * 128]
                    if step != NCH - 1:
                        tr = cT_[d]
                        ptr = tr.ap[:, 0:64].bitcast(BF16)
                        op(PE, lambda e, kc_=kc_, ptr=ptr: e.transpose(ptr, kc_, identb.ap), [kT[d], identb], [tr])
                    if c >= 2:
                        qc = qT[d][:, c * 128:(c + 1) * 128]
                        op(PE, lambda e, kc_=kc_, qc=qc, d=d, p=p: e.matmul(cS[d][:, 0:128], kc_, qc, start=True, stop=True), [kT[d], qT[d]], [cS[d]])
                for d in range(2):
                    c = order[d][step]
                    cs = None if colscale is None else colscale(d, c)
                    if step != NCH - 1:
                        kt = ktok[d][p]
                        ptr = cT_[d].ap[:, 0:64].bitcast(BF16)
                        if cs is None:
                            op(ACT, lambda e, kt=kt, ptr=ptr: e.activation(out=kt.ap, in_=ptr, func=AF.Copy), [cT_[d]], [kt])
                        else:
                            op(ACT, lambda e, kt=kt, ptr=ptr, cs=cs: e.activation(out=kt.ap, in_=ptr, func=AF.Copy, scale=cs[0]), [cT_[d], cs[1]], [kt])
                    if c >= 2:
                        sp = SPm[d][p]
                        msk = maskU if d == 0 else maskL
                        if cs is None:
                            op(DVE, lambda e, sp=sp, d=d, p=p, msk=msk: e.tensor_tensor(out=sp.ap, in0=cS[d][:, 0:128], in1=msk, op=ALU.mult), [cS[d], cst], [sp])
                        else:
                            op(DVE, lambda e, sp=sp, d=d, p=p, msk=msk, cs=cs: e.scalar_tensor_tensor(out=sp.ap, in0=cS[d][:, 0:128], scalar=cs[0], in1=msk,
                                                                                                    op0=ALU.mult, op1=ALU.mult), [cS[d], cst, cs[1]], [sp])

            def stageB(step):
                p = step % 2
                for d in range(2):
                    c = order[d][step]
                    if c >= 2:
                        qc = qT[d][:, c * 128:(c + 1) * 128]
                        sp = SPm[d][p]
                        op(PE, lambda e, sp=sp, c=c, d=d: e.matmul(cO[d][:, 0:Wv], sp.ap, vtok[:, c, 0:Wv], start=True, stop=False), [sp, vtok], [cO[d]], inc=False)
                        op(PE, lambda e, qc=qc, d=d: e.matmul(cO[d][:, 0:Wv], qc, Cbf[d].ap, start=False, stop=True), [qT[d], Cbf[d]], [cO[d]])
                        emit_out(d, c - 2, cO[d], step)
                    if step != NCH - 1:
                        kt = ktok[d][p]
                        dc = decay(d, c)
                        op(PE, lambda e, kt=kt, c=c, d=d: e.matmul(cC[d][:, 0:Wv], kt.ap, vtok[:, c, 0:Wv], start=True, stop=True), [kt, vtok], [cC[d]])
                        if step == 0:
                            op(DVE, lambda e, d=d: e.tensor_copy(out=D32[d].ap, in_=cC[d][:, 0:Wv]), [cC[d]], [D32[d]])
                        else:
                            dp = decay(d, order[d][step - 1])
                            op(DVE, lambda e, d=d, dp=dp: e.scalar_tensor_tensor(out=D32[d].ap, in0=D32[d].ap, scalar=dp[0], in1=cC[d][:, 0:Wv],
                                                                                 op0=ALU.mult, op1=ALU.add), [D32[d], cC[d], dp[1]], [D32[d]])
                        op(ACT, lambda e, d=d, dc=dc: e.activation(out=Cbf[d].ap, in_=D32[d].ap, func=AF.Copy, scale=dc[0]), [D32[d], dc[1]], [Cbf[d]])

            stageA(0)
            for step in range(NCH):
                if step + 1 < NCH:
                    stageA(step + 1)
                stageB(step)

        def load_vtok(W, wcol0, nfeat_chunks, bias_name, bias_i0, vtok, vTb):
            for bi, (t0, n) in enumerate(TB):
                for f in range(nfeat_chunks):
                    pb = next_bank()
                    for kc in range(8):
                        op(PE, lambda e, kc=kc, f=f, t0=t0, n=n, pb=pb: e.matmul(pb[:, 0:n], W[:, kc, wcol0 + f * 128:wcol0 + (f + 1) * 128], uT[:, kc, t0:t0 + n],
                                                                                 start=(kc == 0), stop=(kc == 7)), [W, uT], [pb], inc=(kc == 7))
                    vb = vTb[(bi * nfeat_chunks + f) % 2]
                    op(ACT, lambda e, f=f, n=n, pb=pb, vb=vb: e.activation(out=vb[:, 0:n], in_=pb[:, 0:n], func=AF.Identity, bias=V(bias_name, bias_i0 + f)),
                       [pb, vecs], [vb])
                    pt = next_bank()
                    ptb = pt.ap.bitcast(BF16)
                    nj = n // 128
                    for j in range(nj):
                        op(PE, lambda e, j=j, vb=vb, ptb=ptb: e.transpose(ptb[:, j * 128:(j + 1) * 128], vb[:, j * 128:(j + 1) * 128], identb.ap),
                           [vb, identb], [pt], inc=(j == nj - 1))
                    c0 = t0 // 128
                    op(DVE, lambda e, f=f, nj=nj, c0=c0, ptb=ptb: e.tensor_copy(out=vtok[:, c0:c0 + nj, f * 128:(f + 1) * 128],
                                                                                in_=ptb[:, 0:nj * 128].rearrange("p (a b) -> p a b", a=nj)), [pt], [vtok])

        k.top = mix_top
        Wm = k.alloc([8, 384], BF16)
        Wo = k.alloc([8, 128], BF16)
        Pq = [k.alloc([T + 3], F32) for _ in range(2)]
        Y = k.alloc([T + 1], F32)
        qkT = [k.alloc([T], BF16) for _ in range(2)]
        vtok = k.alloc([NCH, 129], BF16)
        vTb = [k.alloc([512], BF16) for _ in range(2)]
        RAWd = [k.alloc([16, 129], F32) for _ in range(2)]
        RAW = Buf(RAWd[0][:, :, 0:128])
        RAW1 = Buf(RAWd[1][:, :, 0:128])
        dnm = k.alloc([2, 3, 16], F32)
        sig = [k.alloc([512], F32) for _ in range(2)]
        st = k.alloc([16, 8], F32)
        den3 = k.alloc([2, 4], F32)
        for p_ in Pq:
            op(DVE, lambda e, p_=p_: e.memset(p_.ap, 0.0), [], [p_])
        op(DVE, lambda e: e.memset(vtok.ap, 1.0), [], [vtok])
        chain_top = k.top
        for h in range(8):
            k.top = chain_top
            if h == 0:
                dma(POOL, Wm.ap, wm_d[h][:, :, 0:384], writes=[Wm])
            dma(POOL, Wo.ap, wm_d[h][:, :, 384:512], writes=[Wo])
            for kind in range(2):
                P_ = Pq[kind]
                for (t0, n) in TB:
                    pb = next_bank()
                    for kc in range(8):
                        op(PE, lambda e, kc=kc, kind=kind, t0=t0, n=n, pb=pb: e.matmul(pb[:, 0:n], Wm[:, kc, kind * 128:(kind + 1) * 128], uT[:, kc, t0:t0 + n],
                                                                                       start=(kc == 0), stop=(kc == 7)), [Wm, uT], [pb], inc=(kc == 7))
                    pc = 1 if t0 == 0 else t0 + 2
                    op(ACT, lambda e, kind=kind, n=n, pb=pb, pc=pc, P_=P_: e.activation(out=P_[:, pc:pc + n], in_=pb[:, 0:n], func=AF.Identity,
                                                                                         bias=V("bq_m" if kind == 0 else "bk_m", h)), [pb, vecs], [P_])
                ci = kind * 8 + h
                op(DVE, lambda e, P_=P_, ci=ci: e.tensor_scalar(out=Y.ap, in0=P_[:, 1:T + 2], scalar1=V("cw1", ci), scalar2=None, op0=ALU.mult), [P_, vecs], [Y])
                op(DVE, lambda e, P_=P_, ci=ci: e.scalar_tensor_tensor(out=Y.ap, in0=P_[:, 0:T + 1], scalar=V("cw0", ci), in1=Y.ap, op0=ALU.mult, op1=ALU.add),
                   [P_, vecs, Y], [Y])
                op(DVE, lambda e, P_=P_, ci=ci: e.scalar_tensor_tensor(out=Y.ap, in0=P_[:, 2:T + 3], scalar=V("cw2", ci), in1=Y.ap, op0=ALU.mult, op1=ALU.add),
                   [P_, vecs, Y], [Y])
                op(ACT, lambda e, kind=kind: e.activation(out=qkT[kind][:, 0:TC], in_=Y[:, 0:TC], func=AF.Silu), [Y], [qkT[kind]])
                op(ACT, lambda e, kind=kind: e.activation(out=qkT[kind][:, TC:T], in_=Y[:, TC + 1:T + 1], func=AF.Silu), [Y], [qkT[kind]])
            load_vtok(Wm, 256, 1, "bv_m", h, vtok, vTb)
            if h + 1 < 8:
                dma(POOL, Wm.ap, wm_d[h + 1][:, :, 0:384], writes=[Wm])

            def m_out(d, ci, bO, step, h=h):
                op(ACT, lambda e: e.activation(out=RAWd[d][:, ci, :], in_=bO[:, 0:129], func=AF.Copy), [bO], [RAWd[d]])

            chain([qkT[0], qkT[0]], [qkT[1], qkT[1]], vtok, 129,
                  lambda d, c, h=h: (WS[:, c, d * 8 + h:d * 8 + h + 1], WS),
                  lambda d, c, h=h: (EE[:, c, d * 8 + h:d * 8 + h + 1], EE), m_out)
            for d in range(2):
                den = RAWd[d][:, :, 128]
                thd = TH[:, 2:NCH, d * 8 + h]
                op(DVE, lambda e, d=d, den=den: e.tensor_scalar(out=dnm[:, d, 0, :], in0=den, scalar1=-1.0, scalar2=None, op0=ALU.mult), [RAWd[d]], [dnm])
                op(DVE, lambda e, d=d, den=den, thd=thd: e.tensor_tensor(out=dnm[:, d, 1, :], in0=den, in1=thd, op=ALU.max), [RAWd[d], TH], [dnm])
                op(DVE, lambda e, d=d: e.tensor_tensor(out=dnm[:, d, 1, :], in0=dnm[:, d, 1, :], in1=dnm[:, d, 0, :], op=ALU.max), [dnm], [dnm])
                op(DVE, lambda e, d=d: e.reciprocal(out=dnm[:, d, 2, :], in_=dnm[:, d, 1, :]), [dnm], [dnm])
            r0_ = dnm[:, 0, 2, :].rearrange("p (a b) -> p a b", b=1)
            r1_ = dnm[:, 1, 2, :].rearrange("p (a b) -> p a b", b=1)
            op(DVE, lambda e: e.tensor_tensor(out=RAW.ap, in0=RAW.ap, in1=bc_last(r0_, 128), op=ALU.mult), [RAWd[0], dnm], [RAWd[0]])
            op(DVE, lambda e: e.tensor_tensor(out=RAW1.ap, in0=RAW1.ap, in1=bc_last(r1_, 128), op=ALU.mult), [RAWd[1], dnm], [RAWd[1]])
            op(DVE, lambda e: e.tensor_tensor(out=RAW.ap, in0=RAW.ap, in1=RAW1.ap, op=ALU.add), [RAWd[0], RAWd[1]], [RAWd[0]])
            SQ = Y.ap[:, 0:2048].rearrange("p (a b) -> p a b", a=16)
            op(DVE, lambda e: e.tensor_reduce(out=st[:, :, 0], in_=RAW.ap, axis=AX.X, op=ALU.add), [RAWd[0]], [st])
            op(ACT, lambda e: e.activation(out=SQ, in_=RAW.ap, func=AF.Square), [RAWd[0]], [Y])
            op(DVE, lambda e: e.tensor_reduce(out=st[:, :, 1], in_=SQ, axis=AX.X, op=ALU.add), [Y], [st])
            op(DVE, lambda e: e.tensor_scalar(out=st[:, :, 2], in0=st[:, :, 0], scalar1=1.0 / 128, scalar2=None, op0=ALU.mult), [st], [st])
            op(DVE, lambda e: e.tensor_tensor(out=st[:, :, 3], in0=st[:, :, 2], in1=st[:, :, 2], op=ALU.mult), [st], [st])
            op(DVE, lambda e: e.scalar_tensor_tensor(out=st[:, :, 4], in0=st[:, :, 1], scalar=1.0 / 128, in1=st[:, :, 3], op0=ALU.mult, op1=ALU.subtract),
               [st], [st])
            op(DVE, lambda e: e.tensor_scalar(out=st[:, :, 4], in0=st[:, :, 4], scalar1=EPS, scalar2=None, op0=ALU.add), [st], [st])
            op(ACT, lambda e: e.activation(out=st[:, :, 5], in_=st[:, :, 4], func=AF.Sqrt), [st], [st])
            op(DVE, lambda e: e.reciprocal(out=st[:, :, 6], in_=st[:, :, 5]), [st], [st])
            op(DVE, lambda e: e.tensor_tensor(out=RAW.ap, in0=RAW.ap, in1=bc_last(st[:, :, 2:3], 128), op=ALU.subtract), [RAWd[0], st], [RAWd[0]])
            op(DVE, lambda e: e.tensor_tensor(out=RAW.ap, in0=RAW.ap, in1=bc_last(st[:, :, 6:7], 128), op=ALU.mult), [RAWd[0], st], [RAWd[0]])
            for tb in range(4):
                t0 = 256 + tb * 512
                pb = next_bank()
                for kc in range(8):
                    op(PE, lambda e, kc=kc, t0=t0, pb=pb: e.matmul(pb.ap, Wo[:, kc, :], uT[:, kc, t0:t0 + 512], start=(kc == 0), stop=(kc == 7)),
                       [Wo, uT], [pb], inc=(kc == 7))
                sg = sig[tb % 2]
                op(ACT, lambda e, pb=pb, sg=sg: e.activation(out=sg.ap, in_=pb.ap, func=AF.Sigmoid, bias=V("bo_m", h)), [pb, vecs], [sg])
                pt = next_bank()
                for j in range(4):
                    op(PE, lambda e, j=j, tb=tb, pt=pt: e.transpose(pt[:, j * 128:(j + 1) * 128], RAW[:, tb * 4 + j, :], identf), [RAWd[0], cst], [pt], inc=(j == 3))
                op(DVE, lambda e, tb=tb, pt=pt, sg=sg: e.scalar_tensor_tensor(out=gated[h][:, tb * 512:(tb + 1) * 512], in0=pt.ap, scalar=V("gn_m", h), in1=sg.ap,
                                                                              op0=ALU.mult, op1=ALU.mult), [pt, vecs, sg], [gated[h]])
        k.barrier()
        if b == 0:
            for h in (0, 7):
                dump("gm%d" % h, gated[h])
        if stage == 2:
            k.barrier()
            return nc

        k.top = gla_base
        Wg = k.alloc([8, 512], BF16)
        Wz = k.alloc([8, 256], BF16)
        HT = 768
        Qr = k.alloc([HT], F32)
        Kr = k.alloc([HT], F32)
        T1 = k.alloc([HT], F32)
        T2 = k.alloc([HT], F32)
        qt = [k.alloc([T], BF16) for _ in range(2)]
        ktl = [k.alloc([T], BF16) for _ in range(2)]
        vtg = k.alloc([NCH, 256], BF16)
        vTg = [k.alloc([512], BF16) for _ in range(2)]
        RAWg = k.alloc([16, 256], F32)
        DEC = [k.alloc([NCH], F32) for _ in range(2)]
        TOT = k.alloc([6], F32)
        sg2 = [k.alloc([512], F32) for _ in range(2)]
        stg = k.alloc([16, 4], F32)
        chain_top = k.top
        HALF = [(0, 768, [(0, 256), (256, 512)]), (768, 768, [(768, 512), (1280, 256)]), (1536, 768, [(1536, 512), (2048, 256)])]
        for g in range(4):
            k.top = chain_top
            if g == 0:
                dma(POOL, Wg.ap, wg_d[g][:, :, 0:512], writes=[Wg])
            dma(POOL, Wz.ap, wg_d[g][:, :, 512:768], writes=[Wz])
            for (h0, hn, blocks) in HALF:
                nch = hn // 128
                c0 = h0 // 128
                for kind, dst in ((0, Qr), (1, Kr)):
                    for (t0, n) in blocks:
                        pb = next_bank()
                        for kc in range(8):
                            op(PE, lambda e, kc=kc, kind=kind, t0=t0, n=n, pb=pb: e.matmul(pb[:, 0:n], Wg[:, kc, kind * 128:(kind + 1) * 128], uT[:, kc, t0:t0 + n],
                                                                                           start=(kc == 0), stop=(kc == 7)), [Wg, uT], [pb], inc=(kc == 7))
                        op(ACT, lambda e, kind=kind, t0=t0, n=n, pb=pb, dst=dst: e.activation(out=dst[:, t0 - h0:t0 - h0 + n], in_=pb[:, 0:n], func=AF.Identity,
                                                                                               bias=V("bq_g" if kind == 0 else "bk_g", g)), [pb, vecs], [dst])
                for d in range(2):
                    for (t0, n) in blocks:
                        pb = next_bank()
                        op(PE, lambda e, d=d, t0=t0, n=n, pb=pb: e.matmul(pb[:, 0:n], gwb[0:16, d * 512 + g * 128:d * 512 + (g + 1) * 128], RT[d][0:16, t0:t0 + n],
                                                                          start=True, stop=True), [gwb, RT[d]], [pb])
                        op(ACT, lambda e, d=d, t0=t0, n=n, pb=pb: e.activation(out=T1[:, t0 - h0:t0 - h0 + n], in_=pb[:, 0:n], func=AF.Exp, scale=-1.0,
                                                                               bias=ngb[:, d * 4 + g:d * 4 + g + 1]), [pb, ngb], [T1])
                    op(ACT, lambda e: e.activation(out=T1[:, 0:hn], in_=T1[:, 0:hn], func=AF.Ln, bias=1.0), [T1], [T1])
                    op(DVE, lambda e: e.tensor_tensor_scan(out=T2[:, 0:hn], data0=RSTm[:, 0:hn], data1=T1[:, 0:hn], initial=0.0, op0=ALU.mult, op1=ALU.add),
                       [RSTm, T1], [T2])
                    t23 = T2[:, 0:hn].rearrange("p (a b) -> p a b", b=128)
                    if d == 1:
                        op(DVE, lambda e: e.tensor_copy(out=TOT[:, 0:nch], in_=t23[:, :, 127]), [T2], [TOT])
                        op(DVE, lambda e: e.tensor_tensor(out=T2[:, 0:hn], in0=T1[:, 0:hn], in1=T2[:, 0:hn], op=ALU.subtract), [T1, T2], [T2])
                        tot3 = TOT[:, 0:nch].rearrange("p (a b) -> p a b", b=1)
                        op(DVE, lambda e: e.tensor_tensor(out=t23, in0=t23, in1=bc_last(tot3, 128), op=ALU.add), [T2, TOT], [T2])
                    op(ACT, lambda e: e.activation(out=T1[:, 0:hn], in_=T2[:, 0:hn], func=AF.Exp, scale=-1.0 / 16), [T2], [T1])
                    t13 = T1[:, 0:hn].rearrange("p (a b) -> p a b", b=128)
                    op(DVE, lambda e, d=d: e.tensor_copy(out=DEC[d][:, c0:c0 + nch], in_=t13[:, :, 127 if d == 0 else 0]), [T1], [DEC[d]])
                    op(DVE, lambda e, d=d: e.tensor_tensor(out=qt[d][:, h0:h0 + hn], in0=Qr[:, 0:hn], in1=T1[:, 0:hn], op=ALU.mult), [Qr, T1], [qt[d]])
                    op(ACT, lambda e: e.activation(out=T1[:, 0:hn], in_=T2[:, 0:hn], func=AF.Exp, scale=1.0 / 16), [T2], [T1])
                    op(DVE, lambda e, d=d: e.tensor_tensor(out=ktl[d][:, h0:h0 + hn], in0=Kr[:, 0:hn], in1=T1[:, 0:hn], op=ALU.mult), [Kr, T1], [ktl[d]])
            load_vtok(Wg, 256, 2, "bv_g", g * 2, vtg, vTg)
            if g + 1 < 4:
                dma(POOL, Wg.ap, wg_d[g + 1][:, :, 0:512], writes=[Wg])

            def g_out(d, ci, bO, step):
                mine = ci + 2 if d == 0 else NCH - 1 - ci
                other = NCH - 1 - ci if d == 0 else ci + 2
                first = mine < other or (mine == other and d == 0)
                if first:
                    op(ACT, lambda e: e.activation(out=RAWg[:, ci, :], in_=bO[:, 0:256], func=AF.Copy), [bO], [RAWg])
                else:
                    op(DVE, lambda e: e.tensor_tensor(out=RAWg[:, ci, :], in0=bO[:, 0:256], in1=RAWg[:, ci, :], op=ALU.add), [bO, RAWg], [RAWg])

            chain(qt, ktl, vtg, 256, None, lambda d, c: (DEC[d][:, c:c + 1], DEC[d]), g_out)
            for ci in range(16):
                op(ACT, lambda e, ci=ci: e.activation(out=T1[:, 0:256], in_=RAWg[:, ci, :], func=AF.Square, accum_out=stg[:, ci, 0:1]), [RAWg], [T1, stg])
            op(DVE, lambda e: e.tensor_scalar(out=stg[:, :, 1], in0=stg[:, :, 0], scalar1=1.0 / 256, scalar2=EPS * 128.0, op0=ALU.mult, op1=ALU.add), [stg], [stg])
            op(ACT, lambda e: e.activation(out=stg[:, :, 2], in_=stg[:, :, 1], func=AF.Sqrt), [stg], [stg])
            op(DVE, lambda e: e.reciprocal(out=stg[:, :, 3], in_=stg[:, :, 2]), [stg], [stg])
            op(DVE, lambda e: e.tensor_tensor(out=RAWg.ap, in0=RAWg.ap, in1=bc_last(stg[:, :, 3:4], 256), op=ALU.mult), [RAWg, stg], [RAWg])
            for tb in range(4):
                t0 = 256 + tb * 512
                for fh in range(2):
                    pb = next_bank()
                    for kc in range(8):
                        op(PE, lambda e, kc=kc, t0=t0, fh=fh, pb=pb: e.matmul(pb.ap, Wz[:, kc, fh * 128:(fh + 1) * 128], uT[:, kc, t0:t0 + 512],
                                                                              start=(kc == 0), stop=(kc == 7)), [Wz, uT], [pb], inc=(kc == 7))
                    sg = sg2[fh]
                    op(ACT, lambda e, pb=pb, sg=sg, fh=fh: e.activation(out=sg.ap, in_=pb.ap, func=AF.Silu, bias=V("bz_g", g * 2 + fh)), [pb, vecs], [sg])
                    pt = next_bank()
                    for j in range(4):
                        op(PE, lambda e, j=j, tb=tb, fh=fh, pt=pt: e.transpose(pt[:, j * 128:(j + 1) * 128], RAWg[:, tb * 4 + j, fh * 128:(fh + 1) * 128], identf),
                           [RAWg, cst], [pt], inc=(j == 3))
                    gi = 8 + g * 2 + fh
                    op(DVE, lambda e, tb=tb, pt=pt, sg=sg, gi=gi, fh=fh: e.scalar_tensor_tensor(out=gated[gi][:, tb * 512:(tb + 1) * 512], in0=pt.ap,
                                                                                                scalar=V("gn_g", g * 2 + fh), in1=sg.ap, op0=ALU.mult, op1=ALU.mult),
                       [pt, vecs, sg], [gated[gi]])
        k.barrier()
        if b == 0:
            for gi in (8, 15):
                dump("gg%d" % gi, gated[gi])
        if stage == 3:
            k.barrier()
            return nc

        k.top = work_top
        mT = [k.alloc([TL], BF16) for _ in range(8)]
        c2_top = k.top
        Wc = [k.alloc([8, 512], BF16) for _ in range(2)]
        SA = k.alloc([512], F32)
        SBb = k.alloc([512], F32)
        M1 = k.alloc([512], F32)
        M2 = k.alloc([512], F32)
        for n in range(8):
            W = Wc[n % 2]
            dma(POOL, W.ap, wc1_d[n], writes=[W])
            for tb in range(4):
                l0 = tb * 512
                pa, pbk, pm, pg = next_bank(), next_bank(), next_bank(), next_bank()
                for kc in range(8):
                    op(PE, lambda e, kc=kc, pa=pa, l0=l0, W=W: e.matmul(pa.ap, W[:, kc, 256:384], uT[:, kc, TC + l0:TC + l0 + 512], start=(kc == 0), stop=(kc == 7)),
                       [W, uT], [pa], inc=(kc == 7))
                op(ACT, lambda e, pa=pa: e.activation(out=SA.ap, in_=pa.ap, func=AF.Sigmoid, bias=V("ba", n)), [pa, vecs], [SA])
                for kc in range(8):
                    op(PE, lambda e, kc=kc, pbk=pbk, l0=l0, W=W: e.matmul(pbk.ap, W[:, kc, 384:512], uT[:, kc, TC + l0:TC + l0 + 512], start=(kc == 0), stop=(kc == 7)),
                       [W, uT], [pbk], inc=(kc == 7))
                op(ACT, lambda e, pbk=pbk: e.activation(out=SBb.ap, in_=pbk.ap, func=AF.Sigmoid, bias=V("bb", n)), [pbk, vecs], [SBb])
                for kc in range(8):
                    op(PE, lambda e, kc=kc, pm=pm, l0=l0, W=W: e.matmul(pm.ap, W[:, kc, 0:128], gated[kc][:, l0:l0 + 512], start=(kc == 0), stop=(kc == 7)),
                       [W, gated[kc]], [pm], inc=(kc == 7))
                op(DVE, lambda e, pm=pm: e.tensor_tensor(out=M1.ap, in0=pm.ap, in1=SA.ap, op=ALU.mult), [pm, SA], [M1])
                for kc in range(8):
                    op(PE, lambda e, kc=kc, pg=pg, l0=l0, W=W: e.matmul(pg.ap, W[:, kc, 128:256], gated[8 + kc][:, l0:l0 + 512], start=(kc == 0), stop=(kc == 7)),
                       [W, gated[8 + kc]], [pg], inc=(kc == 7))
                op(DVE, lambda e, pg=pg: e.tensor_tensor(out=M2.ap, in0=pg.ap, in1=SBb.ap, op=ALU.mult), [pg, SBb], [M2])
                op(DVE, lambda e, n=n, l0=l0: e.tensor_tensor(out=mT[n][:, l0:l0 + 512], in0=M1.ap, in1=M2.ap, op=ALU.add), [M1, M2], [mT[n]])
        k.barrier()
        if b == 0:
            dump("mT0", mT[0])
        if stage == 4:
            k.barrier()
            return nc

        k.top = c2_top
        h1s = h1s_view()
        wout = k.alloc([8, D], BF16)
        dma(POOL, wout.ap, wout_d, writes=[wout])
        G1bc = k.alloc([D], F32)
        LN1G = k.alloc([D], F32)
        LN1B = k.alloc([D], F32)
        dma(SP, LN1G.ap, rows_d[0:1, :].partition_broadcast(128), writes=[LN1G])
        dma(SP, LN1B.ap, rows_d[1:2, :].partition_broadcast(128), writes=[LN1B])
        dma(SP, G1bc.ap, rows_d[4:5, :].partition_broadcast(128), writes=[G1bc])
        xts = [k.alloc([D], F32) for _ in range(2)]
        Rb = [k.alloc([D], F32) for _ in range(2)]
        stc = k.alloc([2, 6], F32)
        mv = k.alloc([8], F32)
        c2w_top = k.top
        wg1 = k.alloc([8, D], BF16)
        screp = k.alloc([8, 128], BF16)
        dma(POOL, wg1.ap, wmod_d[:, :, 2 * D:3 * D], writes=[wg1])
        op(DVE, lambda e: e.tensor_copy(out=screp.ap, in_=bc_last(scT[:, :, b:b + 1], 128)), [scT], [screp])
        pg1 = PP[0]
        for hf in range(2):
            for kc in range(8):
                op(PE, lambda e, hf=hf, kc=kc: e.matmul(pg1[:, hf * 512:(hf + 1) * 512], screp[:, kc, :], wg1[:, kc, hf * 512:(hf + 1) * 512],
                                                        start=(kc == 0), stop=(kc == 7)), [screp, wg1], [pg1], inc=(kc == 7 and hf == 1))
        op(DVE, lambda e: e.tensor_tensor(out=G1bc.ap, in0=pg1.ap, in1=G1bc.ap, op=ALU.add), [pg1, G1bc], [G1bc])
        k.barrier()
        k.top = c2w_top
        for tt in range(16):
            xt, R = xts[tt % 2], Rb[tt % 2]
            pm, ptr = PP[tt % 2], PP[2 + tt % 2]
            dma(SP, xt.ap, xin[b, TC + tt * 128:TC + (tt + 1) * 128, :], writes=[xt])
            for hf in range(2):
                for kc in range(8):
                    op(PE, lambda e, hf=hf, kc=kc, tt=tt, pm=pm: e.matmul(pm[:, hf * 512:(hf + 1) * 512], mT[kc][:, tt * 128:(tt + 1) * 128], wout[:, kc, hf * 512:(hf + 1) * 512],
                                                                          start=(kc == 0), stop=(kc == 7)), [mT[kc], wout], [pm], inc=(kc == 7 and hf == 1))
            op(DVE, lambda e, R=R, pm=pm: e.tensor_tensor(out=R.ap, in0=pm.ap, in1=G1bc.ap, op=ALU.mult), [pm, G1bc], [R])
            op(DVE, lambda e, R=R, xt=xt: e.scalar_tensor_tensor(out=R.ap, in0=xt.ap, scalar=ALPHA, in1=R.ap, op0=ALU.mult, op1=ALU.add), [xt, R], [R])
            for hf in range(2):
                op(DVE, lambda e, R=R, hf=hf: e.bn_stats(out=stc[:, hf, :], in_=R[:, hf * 512:(hf + 1) * 512]), [R], [stc])
            op(DVE, lambda e: e.bn_aggr(out=mv[:, 0:2], in_=stc.ap.rearrange("p a b -> p (a b)")), [stc], [mv])
            op(DVE, lambda e: e.tensor_scalar(out=mv[:, 2:3], in0=mv[:, 1:2], scalar1=EPS, scalar2=None, op0=ALU.add), [mv], [mv])
            op(ACT, lambda e: e.activation(out=mv[:, 3:4], in_=mv[:, 2:3], func=AF.Sqrt), [mv], [mv])
            op(DVE, lambda e: e.reciprocal(out=mv[:, 4:5], in_=mv[:, 3:4]), [mv], [mv])
            op(DVE, lambda e: e.tensor_scalar(out=mv[:, 5:6], in0=mv[:, 0:1], scalar1=-1.0, scalar2=mv[:, 4:5], op0=ALU.mult, op1=ALU.mult), [mv], [mv])
            op(DVE, lambda e, R=R: e.tensor_scalar(out=R.ap, in0=R.ap, scalar1=mv[:, 4:5], scalar2=mv[:, 5:6], op0=ALU.mult, op1=ALU.add), [R, mv], [R])
            op(DVE, lambda e, R=R: e.tensor_tensor(out=R.ap, in0=R.ap, in1=LN1G.ap, op=ALU.mult), [R, LN1G], [R])
            op(DVE, lambda e, R=R: e.tensor_tensor(out=R.ap, in0=R.ap, in1=LN1B.ap, op=ALU.add), [R, LN1B], [R])
            if b == 0 and tt == 0:
                dump("h1t0", R)
            for kc in range(8):
                op(PE, lambda e, kc=kc, R=R, ptr=ptr: e.transpose(ptr[:, kc * 128:(kc + 1) * 128], R[:, kc * 128:(kc + 1) * 128], identf), [R, cst], [ptr], inc=(kc == 7))
            op(DVE, lambda e, tt=tt, ptr=ptr: e.scalar_tensor_tensor(out=h1s[:, :, tt * 128:(tt + 1) * 128], in0=ptr.ap.rearrange("p (a b) -> p a b", a=8),
                                                                      scalar=ALPHA, in1=bc_last(GB[:, :, b:b + 1], 128), op0=ALU.mult, op1=ALU.add),
               [ptr, GB], [h1s])
        k.barrier()

        k.top = uT_off
        stc = k.alloc([2, 6], F32)
        mv = k.alloc([8], F32)
        aT = [k.alloc([1024], BF16) for _ in range(NFC)]
        U2 = k.alloc([8, 1088], BF16)
        G = [k.alloc([18, 66], BF16) for _ in range(2)]
        DG = [k.alloc([9, 128], BF16) for _ in range(2)]
        Wu = [k.alloc([8, 256], BF16) for _ in range(2)]
        GL = [k.alloc([512], F32) for _ in range(2)]
        Wd = [k.alloc([NFC, 128], BF16) for _ in range(2)]
        R2T = k.alloc([8, 512], F32)
        LN2G = k.alloc([D], F32)
        LN2B = k.alloc([D], F32)
        OT = [k.alloc([D], F32) for _ in range(2)]
        dma(SP, LN2G.ap, rows_d[2:3, :].partition_broadcast(128), writes=[LN2G])
        dma(SP, LN2B.ap, rows_d[3:4, :].partition_broadcast(128), writes=[LN2B])
        for hf in range(2):
            t0 = hf * 1024
            lo = 0 if hf == 0 else 960
            grow0 = 1 if hf == 0 else 0
            voff = t0 - lo
            for kc in range(8):
                op(DVE, lambda e, kc=kc, lo=lo: e.tensor_scalar(out=U2[:, kc, :], in0=h1s[:, kc, lo:lo + 1088], scalar1=A2[:, kc, b:b + 1], scalar2=B2[:, kc, b:b + 1],
                                                                op0=ALU.mult, op1=ALU.add), [h1s, A2, B2], [U2])
            for gb_ in G:
                op(DVE, lambda e, gb_=gb_: e.memset(gb_.ap, 0.0), [], [gb_])
            for c in range(NFC):
                W, Gc, Dg = Wu[c % 2], G[c % 2], DG[c % 2]
                dma(POOL, W.ap, wup_d[c], writes=[W])
                for tap in range(9):
                    op(DVE, lambda e, tap=tap, c=c, Dg=Dg: e.tensor_scalar(out=Dg[:, tap, :], in0=identb.ap, scalar1=V("fcw", tap * NFC + c), scalar2=None, op0=ALU.mult), [identb, vecs], [Dg])
                for (p0, n) in ((0, 512), (512, 512), (1024, 64)):
                    pb = next_bank()
                    for kc in range(8):
                        op(PE, lambda e, kc=kc, p0=p0, n=n, pb=pb, W=W: e.matmul(pb[:, 0:n], W[:, kc, 0:128], U2[:, kc, p0:p0 + n], start=(kc == 0), stop=(kc == 7)),
                           [W, U2], [pb], inc=(kc == 7))
                    r_ = grow0 + p0 // 64
                    nr = n // 64
                    op(ACT, lambda e, pb=pb, n=n, r_=r_, nr=nr, Gc=Gc, c=c: e.activation(out=Gc[:, r_:r_ + nr, 1:65], in_=pb[:, 0:n].rearrange("p (a b) -> p a b", b=64),
                                                                                         func=AF.Identity, bias=V("bup_g", c)), [pb, vecs], [Gc])
                for blk in range(2):
                    pc = next_bank()
                    for tap in range(9):
                        dr, dcol = tap // 3, tap % 3
                        op(PE, lambda e, tap=tap, dr=dr, dcol=dcol, blk=blk, pc=pc, Gc=Gc, Dg=Dg: e.matmul(pc.ap, Dg[:, tap, :], Gc[:, blk * 8 + dr:blk * 8 + dr + 8, dcol:dcol + 64],
                                                                                                            start=(tap == 0), stop=(tap == 8)), [Dg, Gc], [pc], inc=(tap == 8))
                    gl = GL[blk]
                    op(ACT, lambda e, pc=pc, gl=gl, c=c: e.activation(out=gl.ap, in_=pc.ap, func=AF.Gelu_apprx_tanh, bias=V("fcb", c)), [pc, vecs], [gl])
                    pv = next_bank()
                    for kc in range(8):
                        op(PE, lambda e, kc=kc, blk=blk, pv=pv, W=W: e.matmul(pv.ap, W[:, kc, 128:256], U2[:, kc, voff + blk * 512:voff + (blk + 1) * 512],
                                                                              start=(kc == 0), stop=(kc == 7)), [W, U2], [pv], inc=(kc == 7))
                    op(DVE, lambda e, pv=pv, gl=gl, c=c, blk=blk: e.scalar_tensor_tensor(out=aT[c][:, blk * 512:(blk + 1) * 512], in0=pv.ap, scalar=V("bup_v", c), in1=gl.ap,
                                                                                         op0=ALU.add, op1=ALU.mult), [pv, vecs, gl], [aT[c]])
            for tb in range(2):
                for n in range(8):
                    W = Wd[n % 2]
                    dma(POOL, W.ap, wdn_d[n], writes=[W])
                    pb = next_bank()
                    for kc in range(NFC):
                        op(PE, lambda e, kc=kc, pb=pb, W=W, tb=tb: e.matmul(pb.ap, W[:, kc, :], aT[kc][:, tb * 512:(tb + 1) * 512], start=(kc == 0), stop=(kc == NFC - 1)),
                           [W, aT[kc]], [pb], inc=(kc == NFC - 1))
                    tg = t0 + tb * 512
                    op(DVE, lambda e, n=n, pb=pb, tg=tg: e.scalar_tensor_tensor(out=R2T[:, n, :], in0=pb.ap, scalar=MOD[:, 40 + n, b:b + 1], in1=h1s[:, n, tg:tg + 512],
                                                                                op0=ALU.mult, op1=ALU.add), [pb, MOD, h1s], [R2T])
                for tt in range(4):
                    ptr = PP[2 + tt % 2]
                    ot = OT[tt % 2]
                    for n in range(8):
                        op(PE, lambda e, n=n, tt=tt, ptr=ptr: e.transpose(ptr[:, n * 128:(n + 1) * 128], R2T[:, n, tt * 128:(tt + 1) * 128], identf), [R2T, cst], [ptr], inc=(n == 7))
                    for q in range(2):
                        op(DVE, lambda e, q=q, ptr=ptr: e.bn_stats(out=stc[:, q, :], in_=ptr[:, q * 512:(q + 1) * 512]), [ptr], [stc])
                    op(DVE, lambda e: e.bn_aggr(out=mv[:, 0:2], in_=stc.ap.rearrange("p a b -> p (a b)")), [stc], [mv])
                    op(DVE, lambda e: e.tensor_scalar(out=mv[:, 2:3], in0=mv[:, 1:2], scalar1=EPS, scalar2=None, op0=ALU.add), [mv], [mv])
                    op(ACT, lambda e: e.activation(out=mv[:, 3:4], in_=mv[:, 2:3], func=AF.Sqrt), [mv], [mv])
                    op(DVE, lambda e: e.reciprocal(out=mv[:, 4:5], in_=mv[:, 3:4]), [mv], [mv])
                    op(DVE, lambda e: e.tensor_scalar(out=mv[:, 5:6], in0=mv[:, 0:1], scalar1=-1.0, scalar2=mv[:, 4:5], op0=ALU.mult, op1=ALU.mult), [mv], [mv])
                    op(DVE, lambda e, ot=ot, ptr=ptr: e.tensor_scalar(out=ot.ap, in0=ptr.ap, scalar1=mv[:, 4:5], scalar2=mv[:, 5:6], op0=ALU.mult, op1=ALU.add), [ptr, mv], [ot])
                    op(DVE, lambda e, ot=ot: e.tensor_tensor(out=ot.ap, in0=ot.ap, in1=LN2G.ap, op=ALU.mult), [ot, LN2G], [ot])
                    op(DVE, lambda e, ot=ot: e.tensor_tensor(out=ot.ap, in0=ot.ap, in1=LN2B.ap, op=ALU.add), [ot, LN2B], [ot])
                    r0 = t0 + tb * 512 + tt * 128
                    dma(SP, out_d[b, r0:r0 + 128, :], ot.ap, reads=[ot])
        k.barrier()
    k.barrier()
    return nc


def _fm(v, nchunk):
    return np.ascontiguousarray(np.asarray(v, np.float32).reshape(nchunk, 128).T)


def _kmaj(w):
    K, N = w.shape
    return np.ascontiguousarray(np.asarray(w, np.float32).reshape(K // 128, 128, N).transpose(1, 0, 2))


def prep_shared(inp):
    g = lambda n: np.asarray(inp[n], np.float32)[0]
    w_in, b_in = g("w_in"), g("b_in")
    o_q, o_k, o_v, o_o, o_g = 0, 1024, 2048, 3072, 4096
    o_qg = 4128
    o_kg, o_vg, o_zg, o_r = o_qg + 512, o_qg + 1024, o_qg + 2048, o_qg + 3072
    o_a = o_r + 32
    o_b = o_a + 1024
    vec = np.zeros((128, NV), np.float32)

    def put(name, arr):
        vec[:, VOFF[name]:VOFF[name] + arr.shape[1]] = arr

    put("bq_m", _fm(b_in[o_q:o_q + 1024], 8)); put("bk_m", _fm(b_in[o_k:o_k + 1024], 8))
    put("bv_m", _fm(b_in[o_v:o_v + 1024], 8)); put("bo_m", _fm(b_in[o_o:o_o + 1024], 8))
    cq = g("conv_qk")
    for j in range(3):
        put("cw%d" % j, _fm(cq[j], 16))
    put("bq_g", _fm(b_in[o_qg:o_qg + 512], 4)); put("bk_g", _fm(b_in[o_kg:o_kg + 512], 4))
    put("bv_g", _fm(b_in[o_vg:o_vg + 1024], 8)); put("bz_g", _fm(b_in[o_zg:o_zg + 1024], 8))
    put("gb_f", _fm(g("gla_gate_b_fwd"), 4)); put("gb_b", _fm(g("gla_gate_b_bwd"), 4))
    put("gn_m", _fm(g("mlstm_norm_g"), 8)); put("gn_g", _fm(g("gla_norm_g"), 8))
    put("ba", _fm(b_in[o_a:o_a + 1024], 8)); put("bb", _fm(b_in[o_b:o_b + 1024], 8))
    bup = g("b_up")
    put("bup_g", _fm(bup[:DFF], NFC)); put("bup_v", _fm(bup[DFF:], NFC))
    fcw = g("ffn_conv_w").reshape(9, DFF)
    put("fcw", np.concatenate([_fm(fcw[t], NFC) for t in range(9)], axis=1))
    put("fcb", _fm(g("ffn_conv_b"), NFC)); put("bdn", _fm(g("b_down"), 8)); put("bmod", _fm(g("b_mod"), 48))
    br = np.zeros((128, 2), np.float32)
    br[0:16, 0] = b_in[o_r:o_r + 16]
    br[0:16, 1] = b_in[o_r + 16:o_r + 32]
    put("br", br)
    wk = _kmaj(w_in)
    w_m = np.stack([np.concatenate([wk[:, :, o + h * 128:o + (h + 1) * 128] for o in (o_q, o_k, o_v, o_o)], axis=2) for h in range(8)])
    w_g = np.stack([np.concatenate([wk[:, :, o_qg + h * 128:o_qg + (h + 1) * 128], wk[:, :, o_kg + h * 128:o_kg + (h + 1) * 128],
                                    wk[:, :, o_vg + h * 256:o_vg + (h + 1) * 256], wk[:, :, o_zg + h * 256:o_zg + (h + 1) * 256]], axis=2) for h in range(4)])
    w_s = np.concatenate([wk[:, :, o_g:o_g + 32], wk[:, :, o_r:o_r + 32]], axis=2)
    wbm, wbg = _kmaj(g("w_branch_mlstm")), _kmaj(g("w_branch_gla"))
    wc1 = np.stack([np.concatenate([wbm[:, :, n * 128:(n + 1) * 128], wbg[:, :, n * 128:(n + 1) * 128],
                                    wk[:, :, o_a + n * 128:o_a + (n + 1) * 128], wk[:, :, o_b + n * 128:o_b + (n + 1) * 128]], axis=2) for n in range(8)])
    wu = _kmaj(g("w_up"))
    wup = np.stack([np.concatenate([wu[:, :, c * 128:(c + 1) * 128], wu[:, :, DFF + c * 128:DFF + (c + 1) * 128]], axis=2) for c in range(NFC)])
    wd = _kmaj(g("w_down"))
    wdn = np.stack([wd[:, :, n * 128:(n + 1) * 128] for n in range(8)])
    rows = np.zeros((8, D), np.float32)
    rows[0], rows[1], rows[2], rows[3] = g("ln1_g"), g("ln1_b"), g("ln2_g"), g("ln2_b")
    bm = g("b_mod")
    rows[4] = bm[2 * D:3 * D]
    rows[6, 0:32] = b_in[o_g:o_g + 32]
    idx = np.arange(128)
    cst = np.zeros((128, 4, 128), np.float32)
    cst[:, 0, :] = np.eye(128)
    cst[:, 1, :] = (idx[:, None] <= idx[None, :])
    cst[:, 2, :] = (idx[:, None] >= idx[None, :])
    cst[:, 3, :] = 1.0
    rst = np.ones((128, T), np.float32)
    rst[:, ::128] = 0.0
    c = lambda a: np.ascontiguousarray(a, dtype=np.float32)
    return dict(wmod=_kmaj(g("w_mod")), vecs=vec, w_m=c(w_m), w_g=c(w_g), w_s=c(w_s),
                gw=c(np.concatenate([g("gla_gate_w_fwd"), g("gla_gate_w_bwd")], axis=1)), rows=rows, wc1=c(wc1),
                wout=_kmaj(g("w_out")), wup=c(wup), wdn=c(wdn), cst=cst, rst=rst)


def prep_core(inp, bs):
    x, ctx, cvec, cctx = (np.asarray(inp[n], np.float32) for n in ("x", "ctx", "c", "c_ctx"))
    nb = len(bs)
    xin = np.ascontiguousarray(np.concatenate([ctx[bs], x[bs]], axis=1))
    cT = np.zeros((128, 8, 4), np.float32)
    for j, b in enumerate(bs):
        cT[:, :, j] = _fm(cvec[b], 8)
    cT[:, :, nb] = _fm(cctx, 8)
    return dict(xin=xin, cT=cT)


def kernel(**inputs):
    n = 8
    NB = 2
    shared = prep_shared(inputs)
    nc = build(NB)
    in_maps = []
    for i in range(n):
        m = dict(shared)
        m.update(prep_core(inputs, list(range(i * NB, (i + 1) * NB))))
        in_maps.append(m)
    res = run_bass_kernel_spmd(nc, in_maps, core_ids=list(range(n)))
    return np.concatenate([r["out"] for r in res.results], axis=0).astype(np.float32)
```

```python
import numpy as np
import concourse.bass as bass
import concourse.mybir as mybir
from concourse.bass_utils import run_bass_kernel_spmd

F32 = mybir.dt.float32
BF16 = mybir.dt.bfloat16
U8 = mybir.dt.uint8
AF = mybir.ActivationFunctionType
ALU = mybir.AluOpType
AX = mybir.AxisListType

D = 1024
TC = 256
TL = 2048
T = TC + TL
NCH = T // 128
DFF = 2816
NFC = DFF // 128
ALPHA = 2.0 ** 0.25
EPS = 1e-5
NSLOT = 24
ATTACH = True

_VSPEC = [("bq_m", 8), ("bk_m", 8), ("bv_m", 8), ("bo_m", 8), ("cw0", 16), ("cw1", 16), ("cw2", 16),
          ("bq_g", 4), ("bk_g", 4), ("bv_g", 8), ("bz_g", 8), ("gb_f", 4), ("gb_b", 4),
          ("gn_m", 8), ("gn_g", 8), ("ba", 8), ("bb", 8), ("bup_g", NFC), ("bup_v", NFC),
          ("fcw", 9 * NFC), ("fcb", NFC), ("bdn", 8), ("bmod", 48), ("br", 2)]
VOFF = {}
_o = 0
for _n, _c in _VSPEC:
    VOFF[_n] = _o
    _o += _c
NV = _o


class Own:
    def __init__(self, name, sem, inc):
        self.name, self.sem, self.inc, self.count = name, sem, inc, 0


class Iss:
    def __init__(self, h, own=None):
        self.h, self.own, self.seen = h, own, {}


class Buf:
    __slots__ = ("ap", "w", "r")

    def __init__(self, ap):
        self.ap, self.w, self.r = ap, None, {}

    def __getitem__(self, k):
        return self.ap[k]


class Pair:
    def __init__(self, ap, bufs):
        self.ap, self.bufs = ap, bufs

    def __getitem__(self, k):
        return self.ap[k]


def _flat(lst):
    out = []
    for t in lst:
        if isinstance(t, Pair):
            out.extend(t.bufs)
        else:
            out.append(t)
    return out


class KB:
    def __init__(self, nc, sbuf_bytes):
        self.nc = nc
        mk = lambda n, inc: Own(n, nc.alloc_semaphore(n), inc)
        self.oPE, self.oACT, self.oDVE, self.oPOOL = mk("pe", 1), mk("act", 1), mk("dve", 1), mk("pool", 1)
        self.PE, self.ACT = Iss(nc.tensor, self.oPE), Iss(nc.scalar, self.oACT)
        self.DVE, self.POOL = Iss(nc.vector, self.oDVE), Iss(nc.gpsimd, self.oPOOL)
        self.SP = Iss(nc.sync, None)
        self.slots = [mk("dma%d" % i, 16) for i in range(NSLOT)]
        self.hw_slots, self.sw_slots = self.slots[:NSLOT // 2], self.slots[NSLOT // 2:]
        self.dma_i = {id(self.hw_slots): 0, id(self.sw_slots): 0}
        self.big = nc.alloc_sbuf_tensor("big", [128, sbuf_bytes], U8)
        self.cap = sbuf_bytes
        self.top = 0

    def alloc(self, free, dt):
        sz = 4 if dt == F32 else 2
        n = 1
        for q in free:
            n *= q
        off = (self.top + 31) // 32 * 32
        self.top = off + n * sz
        assert self.top <= self.cap, ("SBUF overflow", self.top, self.cap)
        ap = self.big[:, off:off + n * sz].bitcast(dt)
        if len(free) == 2:
            ap = ap.rearrange("p (a b) -> p a b", a=free[0])
        elif len(free) == 3:
            ap = ap.rearrange("p (a b c) -> p a b c", a=free[0], b=free[1])
        return Buf(ap)

    def _waits(self, iss, reads, writes, attach=False):
        need = {}
        for t in reads:
            if t.w is not None:
                o, v = t.w
                if need.get(o, 0) < v:
                    need[o] = v
        for t in writes:
            if t.w is not None:
                o, v = t.w
                if need.get(o, 0) < v:
                    need[o] = v
            for o, v in t.r.items():
                if need.get(o, 0) < v:
                    need[o] = v
        todo = []
        for o, v in need.items():
            if o is iss.own and o is self.oPE:
                continue
            if iss.seen.get(o, 0) >= v:
                continue
            todo.append((o, v))
            iss.seen[o] = v
        last = todo.pop() if (attach and todo) else None
        for o, v in todo:
            iss.h.wait_ge(o.sem, v)
        return last

    @staticmethod
    def _record(own, v, reads, writes):
        for t in reads:
            if t.r.get(own, 0) < v:
                t.r[own] = v
        for t in writes:
            t.w = (own, v)
            t.r = {}

    def op(self, iss, fn, reads=(), writes=(), inc=True):
        reads, writes = _flat(reads), _flat(writes)
        last = self._waits(iss, reads, writes, attach=(ATTACH and iss is not self.PE))
        ins = fn(iss.h)
        if last is not None:
            ins._wait_ge(last[0].sem, last[1])
        own = iss.own
        if inc:
            own.count += 1
            ins.then_inc(own.sem, 1)
            v = own.count
        else:
            v = own.count + 1
        self._record(own, v, reads, writes)

    def dma(self, iss, out, in_, reads=(), writes=()):
        reads, writes = _flat(reads), _flat(writes)
        pool = self.sw_slots if iss is self.POOL else self.hw_slots
        slot = pool[self.dma_i[id(pool)] % len(pool)]
        self.dma_i[id(pool)] += 1
        if slot.count > iss.seen.get(slot, 0):
            iss.h.wait_ge(slot.sem, slot.count)
            iss.seen[slot] = slot.count
        last = self._waits(iss, reads, writes, attach=ATTACH)
        ins = iss.h.dma_start(out=out, in_=in_, max_dma_last_dim=4096) if iss is self.POOL else iss.h.dma_start(out=out, in_=in_)
        if last is not None:
            ins._wait_ge(last[0].sem, last[1])
        slot.count += 16
        ins.then_inc(slot.sem, 16)
        self._record(slot, slot.count, reads, writes)

    def barrier(self):
        owners = [self.oPE, self.oACT, self.oDVE, self.oPOOL] + self.slots
        for iss in (self.PE, self.ACT, self.DVE, self.POOL, self.SP):
            for o in owners:
                if o.count > iss.seen.get(o, 0):
                    iss.h.wait_ge(o.sem, o.count)
                    iss.seen[o] = o.count


def bc_last(ap, n):
    sh = list(ap.shape)
    sh[-1] = n
    return ap.broadcast_to(sh)


def build(NB=2, dbg=None, stage=99):
    nc = bass.Bass("TRN2", target_bir_lowering=False)
    dram = lambda n, s, k="ExternalInput": nc.dram_tensor(n, list(s), F32, kind=k).ap()
    xin = dram("xin", [NB, T, D])
    cT_d = dram("cT", [128, 8, 4])
    wmod_d = dram("wmod", [128, 8, 6 * D])
    vecs_d = dram("vecs", [128, NV])
    wm_d = dram("w_m", [8, 128, 8, 512])
    wg_d = dram("w_g", [4, 128, 8, 768])
    ws_d = dram("w_s", [128, 8, 64])
    gw_d = dram("gw", [16, 1024])
    rows_d = dram("rows", [8, D])
    wc1_d = dram("wc1", [8, 128, 8, 512])
    wout_d = dram("wout", [128, 8, D])
    wup_d = dram("wup", [NFC, 128, 8, 256])
    wdn_d = dram("wdn", [8, 128, NFC, 128])
    cst_d = dram("cst", [128, 4, 128])
    rst_d = dram("rst", [128, T])
    out_d = dram("out", [NB, TL, D], "ExternalOutput")
    dbg_d = {}
    if dbg:
        for n, (s, dt_) in dbg.items():
            dbg_d[n] = nc.dram_tensor("dbg_" + n, list(s), dt_, kind="ExternalOutput").ap()

    k = KB(nc, 206 * 1024)
    PE, ACT, DVE, POOL, SP = k.PE, k.ACT, k.DVE, k.POOL, k.SP
    op, dma = k.op, k.dma

    _pt = [nc.alloc_psum_tensor("pp%d" % i, [128, 1024], F32)[:, :] for i in range(4)]
    PB = [[Buf(_pt[i][:, 0:512]), Buf(_pt[i][:, 512:1024])] for i in range(4)]
    PP = [Pair(_pt[i], PB[i]) for i in range(4)]
    pb_rr = [0]

    def next_bank():
        i = pb_rr[0] % 4
        pb_rr[0] += 1
        return PB[i // 2][i % 2]

    def dump(name, buf, ap=None):
        if name in dbg_d:
            dma(SP, dbg_d[name], buf.ap if ap is None else ap, reads=[buf])

    cst = k.alloc([4, 128], F32)
    identf, maskU, maskL, onesf = cst[:, 0, :], cst[:, 1, :], cst[:, 2, :], cst[:, 3, :]
    dma(SP, cst.ap, cst_d, writes=[cst])
    identb = k.alloc([128], BF16)
    op(DVE, lambda e: e.tensor_copy(out=identb.ap, in_=identf), [cst], [identb])
    vecs = k.alloc([NV], F32)
    dma(SP, vecs.ap, vecs_d, writes=[vecs])
    V = lambda name, i=0: vecs[:, VOFF[name] + i:VOFF[name] + i + 1]
    cT = k.alloc([8, 4], F32)
    dma(SP, cT.ap, cT_d, writes=[cT])
    scT = k.alloc([8, 4], BF16)
    op(ACT, lambda e: e.activation(out=scT.ap, in_=cT.ap, func=AF.Silu), [cT], [scT])
    ws = k.alloc([8, 64], BF16)
    dma(POOL, ws.ap, ws_d, writes=[ws])
    gwb = k.alloc([1024], BF16)
    dma(POOL, gwb[0:16, :], gw_d, writes=[gwb])
    ngb = k.alloc([8], F32)
    op(DVE, lambda e: e.tensor_scalar(out=ngb.ap, in0=vecs[:, VOFF["gb_f"]:VOFF["gb_f"] + 8], scalar1=-1.0,
                                      scalar2=None, op0=ALU.mult), [vecs], [ngb])
    MOD = k.alloc([48, 4], F32)
    SC1 = k.alloc([8, 4], F32)
    A2 = k.alloc([8, 4], F32)
    B2 = k.alloc([8, 4], F32)
    GB = k.alloc([8, 4], F32)
    persist_top = k.top

    wpiece = k.alloc([8, 512], BF16)
    modps = PB[3][1]
    for nb in range(12):
        dma(POOL, wpiece.ap, wmod_d[:, :, nb * 512:(nb + 1) * 512], writes=[wpiece])
        for j in range(4):
            n = nb * 4 + j
            for kc in range(8):
                op(PE, lambda e, j=j, kc=kc, n=n: e.matmul(modps[:, n * 4:n * 4 + 4], wpiece[:, kc, j * 128:(j + 1) * 128],
                                                           scT[:, kc, :], start=(kc == 0), stop=(kc == 7)),
                   [wpiece, scT], [modps], inc=(kc == 7))
    bm3 = vecs[:, VOFF["bmod"]:VOFF["bmod"] + 48].rearrange("p (a b) -> p a b", b=1)
    op(DVE, lambda e: e.tensor_tensor(out=MOD.ap, in0=modps[:, 0:192].rearrange("p (a b) -> p a b", b=4),
                                      in1=bc_last(bm3, 4), op=ALU.add), [modps, vecs], [MOD])
    op(DVE, lambda e: e.tensor_scalar(out=SC1.ap, in0=MOD[:, 8:16, :], scalar1=1.0, scalar2=None, op0=ALU.add), [MOD], [SC1])
    bd3 = vecs[:, VOFF["bdn"]:VOFF["bdn"] + 8].rearrange("p (a b) -> p a b", b=1)
    op(DVE, lambda e: e.tensor_tensor(out=GB.ap, in0=MOD[:, 40:48, :], in1=bc_last(bd3, 4), op=ALU.mult), [MOD, vecs], [GB])
    op(DVE, lambda e: e.tensor_scalar(out=A2.ap, in0=MOD[:, 32:40, :], scalar1=1.0, scalar2=1.0 / ALPHA,
                                      op0=ALU.add, op1=ALU.mult), [MOD], [A2])
    op(DVE, lambda e: e.tensor_tensor(out=B2.ap, in0=GB.ap, in1=A2.ap, op=ALU.mult), [GB, A2], [B2])
    op(DVE, lambda e: e.tensor_tensor(out=B2.ap, in0=MOD[:, 24:32, :], in1=B2.ap, op=ALU.subtract), [MOD, B2], [B2])
    k.barrier()
    k.top = persist_top

    R2_off = (k.top + 31) // 32 * 32
    gated = [k.alloc([TL], BF16) for _ in range(16)]
    uT_off = k.top
    uT = k.alloc([8, T], BF16)
    work_top = k.top

    def h1s_view():
        ap = k.big[:, R2_off:R2_off + 8 * TL * 4].bitcast(F32).rearrange("p (a b) -> p a b", a=8)
        return Buf(ap)

    for b in range(NB):
        k.top = work_top
        xts = [k.alloc([D], F32) for _ in range(6)]
        tmpA = [k.alloc([8, 128], F32) for _ in range(2)]
        for tt in range(NCH):
            xt = xts[tt % 6]
            pp = PP[tt % 2]
            col = NB if tt < 2 else b
            dma(SP, xt.ap, xin[b, tt * 128:(tt + 1) * 128, :], writes=[xt])
            for kc in range(8):
                op(PE, lambda e, kc=kc, xt=xt, pp=pp: e.transpose(pp[:, kc * 128:(kc + 1) * 128], xt[:, kc * 128:(kc + 1) * 128], identf),
                   [xt, cst], [pp], inc=(kc == 7))
            tm = tmpA[tt % 2]
            p3 = pp.ap.rearrange("p (a b) -> p a b", a=8)
            op(DVE, lambda e, tm=tm, p3=p3, col=col: e.tensor_tensor(out=tm.ap, in0=p3, in1=bc_last(SC1[:, :, col:col + 1], 128), op=ALU.mult),
               [pp, SC1], [tm])
            op(DVE, lambda e, tm=tm, tt=tt, col=col: e.tensor_tensor(out=uT[:, :, tt * 128:(tt + 1) * 128], in0=tm.ap,
                                                                      in1=bc_last(MOD[:, 0:8, col:col + 1], 128), op=ALU.add),
               [tm, MOD], [uT])
        k.barrier()
        dump("uT%d" % b, uT)
        if stage == 1:
            k.barrier()
            return nc

        k.top = work_top
        TB = [(0, 256)] + [(256 + i * 512, 512) for i in range(4)]
        RSTm = k.alloc([768], BF16)
        dma(POOL, RSTm.ap, rst_d[:, 0:768], writes=[RSTm])
        RT = [k.alloc([T], BF16) for _ in range(2)]
        gla_base = k.top
        WS = k.alloc([NCH, 16], F32)
        TH = k.alloc([NCH, 16], F32)
        EE = k.alloc([NCH, 16], F32)
        mix_top = k.top
        GM = k.alloc([NCH, 32], F32)
        LFN = k.alloc([NCH, 16], F32)
        NBt = k.alloc([NCH, 32], F32)
        bgm = k.alloc([32], F32)
        dma(SP, bgm.ap, rows_d[6:7, 0:32].partition_broadcast(128), writes=[bgm])
        gp = PP[2]
        for tt in range(NCH):
            for kc in range(8):
                op(PE, lambda e, tt=tt, kc=kc: e.matmul(gp[:, tt * 32:(tt + 1) * 32], uT[:, kc, tt * 128:(tt + 1) * 128], ws[:, kc, 0:32],
                                                        start=(kc == 0), stop=(kc == 7)), [uT, ws], [gp], inc=(kc == 7))
        b3 = bgm.ap.rearrange("p (a b) -> p a b", a=1).broadcast_to([128, NCH, 32])
        op(DVE, lambda e: e.tensor_tensor(out=GM.ap, in0=gp[:, 0:NCH * 32].rearrange("p (a b) -> p a b", b=32), in1=b3, op=ALU.add),
           [gp, bgm], [GM])
        op(ACT, lambda e: e.activation(out=LFN.ap, in_=GM[:, :, 16:32], func=AF.Exp, scale=-1.0), [GM], [LFN])
        op(ACT, lambda e: e.activation(out=LFN.ap, in_=LFN.ap, func=AF.Ln, bias=1.0), [LFN], [LFN])
        gp2 = PP[3]
        for c in range(NCH):
            op(PE, lambda e, c=c: e.matmul(gp2[:, c * 32:c * 32 + 8], maskU, LFN[:, c, 0:8], start=True, stop=True), [cst, LFN], [gp2], inc=False)
            op(PE, lambda e, c=c: e.matmul(gp2[:, c * 32 + 8:c * 32 + 16], maskL, LFN[:, c, 8:16], start=True, stop=True), [cst, LFN], [gp2], inc=False)
            op(PE, lambda e, c=c: e.matmul(gp2[:, c * 32 + 16:c * 32 + 32], onesf, LFN[:, c, 0:16], start=True, stop=True), [cst, LFN], [gp2])
        op(DVE, lambda e: e.tensor_copy(out=NBt.ap, in_=gp2[:, 0:NCH * 32].rearrange("p (a b) -> p a b", b=32)), [gp2], [NBt])
        op(DVE, lambda e: e.tensor_tensor(out=WS.ap, in0=GM[:, :, 0:16], in1=NBt[:, :, 0:16], op=ALU.add), [GM, NBt], [WS])
        op(ACT, lambda e: e.activation(out=WS.ap, in_=WS.ap, func=AF.Exp), [WS], [WS])
        lnc = k.alloc([1], F32)
        op(DVE, lambda e: e.memset(lnc.ap, 0.5 * float(np.log(128.0))), [], [lnc])
        op(ACT, lambda e: e.activation(out=TH.ap, in_=NBt[:, :, 0:16], func=AF.Exp, bias=lnc[:, 0:1]), [NBt, lnc], [TH])
        op(ACT, lambda e: e.activation(out=EE.ap, in_=NBt[:, :, 16:32], func=AF.Exp, scale=-1.0), [NBt], [EE])
        for d in range(2):
            for (t0, n) in TB:
                pb = next_bank()
                for kc in range(8):
                    op(PE, lambda e, d=d, kc=kc, t0=t0, n=n, pb=pb: e.matmul(pb[0:16, 0:n], ws[:, kc, 32 + d * 16:48 + d * 16], uT[:, kc, t0:t0 + n],
                                                                             start=(kc == 0), stop=(kc == 7)), [ws, uT], [pb], inc=(kc == 7))
                op(ACT, lambda e, d=d, t0=t0, n=n, pb=pb: e.activation(out=RT[d][0:16, t0:t0 + n], in_=pb[0:16, 0:n], func=AF.Identity,
                                                                       bias=vecs[0:16, VOFF["br"] + d:VOFF["br"] + d + 1]), [pb, vecs], [RT[d]])
        k.barrier()

        cS = [PB[2 + d][1] for d in range(2)]
        cT_ = [PB[2 + d][0] for d in range(2)]
        cO = [PB[0][d] for d in range(2)]
        cC = [PB[1][d] for d in range(2)]

        def chain_bufs(Wv):
            return dict(D32=[k.alloc([Wv], F32) for _ in range(2)], Cbf=[k.alloc([Wv], BF16) for _ in range(2)],
                        ktok=[[k.alloc([128], BF16) for _ in range(2)] for _ in range(2)],
                        SPm=[[k.alloc([128], BF16) for _ in range(2)] for _ in range(2)])

        def chain(cb, qT, kT, vtok, Wv, colscale, decay, emit_out):
            D32, Cbf, ktok, SPm = cb["D32"], cb["Cbf"], cb["ktok"], cb["SPm"]
            order = [list(range(NCH)), [1, 0] + list(range(NCH - 1, 1, -1))]
            for d in range(2):
                op(DVE, lambda e, d=d: e.memset(Cbf[d].ap, 0.0), [], [Cbf[d]])

            def stageA(step):
                p = step % 2
                for d in range(2):
                    c = order[d][step]
                    kc_ = kT[d][:, c * 128:(c + 1) * 128]
                    if step != NCH - 1:
                        tr = cT_[d]
                        ptr = tr.ap[:, 0:64].bitcast(BF16)
                        op(PE, lambda e, kc_=kc_, ptr=ptr: e.transpose(ptr, kc_, identb.ap), [kT[d], identb], [tr])
                    if c >= 2:
                        qc = qT[d][:, c * 128:(c + 1) * 128]
                        op(PE, lambda e, kc_=kc_, qc=qc, d=d, p=p: e.matmul(cS[d][:, 0:128], kc_, qc, start=True, stop=True), [kT[d], qT[d]], [cS[d]])
                for d in range(2):
                    c = order[d][step]
                    cs = None if colscale is None else colscale(d, c)
                    if step != NCH - 1:
                        kt = ktok[d][p]
                        ptr = cT_[d].ap[:, 0:64].bitcast(BF16)
                        if cs is None:
                            op(ACT, lambda e, kt=kt, ptr=ptr: e.activation(out=kt.ap, in_=ptr, func=AF.Copy), [cT_[d]], [kt])
                        else:
                            op(ACT, lambda e, kt=kt, ptr=ptr, cs=cs: e.activation(out=kt.ap, in_=ptr, func=AF.Copy, scale=cs[0]), [cT_[d], cs[1]], [kt])
                    if c >= 2:
                        sp = SPm[d][p]
                        msk = maskU if d == 0 else maskL
                        if cs is None:
                            op(DVE, lambda e, sp=sp, d=d, p=p, msk=msk: e.tensor_tensor(out=sp.ap, in0=cS[d][:, 0:128], in1=msk, op=ALU.mult), [cS[d], cst], [sp])
                        else:
                            op(DVE, lambda e, sp=sp, d=d, p=p, msk=msk, cs=cs: e.scalar_tensor_tensor(out=sp.ap, in0=cS[d][:, 0:128], scalar=cs[0], in1=msk,
                                                                                                    op0=ALU.mult, op1=ALU.mult), [cS[d], cst, cs[1]], [sp])

            def stageB(step):
                p = step % 2
                for d in range(2):
                    c = order[d][step]
                    if c >= 2:
                        qc = qT[d][:, c * 128:(c + 1) * 128]
                        sp = SPm[d][p]
                        op(PE, lambda e, sp=sp, c=c, d=d: e.matmul(cO[d][:, 0:Wv], sp.ap, vtok[:, c, 0:Wv], start=True, stop=False), [sp, vtok], [cO[d]], inc=False)
                        op(PE, lambda e, qc=qc, d=d: e.matmul(cO[d][:, 0:Wv], qc, Cbf[d].ap, start=False, stop=True), [qT[d], Cbf[d]], [cO[d]])
                        emit_out(d, c - 2, cO[d], step)
                    if step != NCH - 1:
                        kt = ktok[d][p]
                        dc = decay(d, c)
                        op(PE, lambda e, kt=kt, c=c, d=d: e.matmul(cC[d][:, 0:Wv], kt.ap, vtok[:, c, 0:Wv], start=True, stop=True), [kt, vtok], [cC[d]])
                        if step == 0:
                            op(DVE, lambda e, d=d: e.tensor_copy(out=D32[d].ap, in_=cC[d][:, 0:Wv]), [cC[d]], [D32[d]])
                        else:
                            dp = decay(d, order[d][step - 1])
                            op(DVE, lambda e, d=d, dp=dp: e.scalar_tensor_tensor(out=D32[d].ap, in0=D32[d].ap, scalar=dp[0], in1=cC[d][:, 0:Wv],
                                                                                 op0=ALU.mult, op1=ALU.add), [D32[d], cC[d], dp[1]], [D32[d]])
                        op(DVE, lambda e, d=d, dc=dc: e.tensor_scalar(out=Cbf[d].ap, in0=D32[d].ap, scalar1=dc[0], scalar2=None, op0=ALU.mult), [D32[d], dc[1]], [Cbf[d]])

            stageA(0)
            for step in range(NCH):
                if step + 1 < NCH:
                    stageA(step + 1)
                stageB(step)

        def load_vtok(W, wcol0, nfeat_chunks, bias_name, bias_i0, vtok, vTb):
            for bi, (t0, n) in enumerate(TB):
                for f in range(nfeat_chunks):
                    pb = next_bank()
                    for kc in range(8):
                        op(PE, lambda e, kc=kc, f=f, t0=t0, n=n, pb=pb: e.matmul(pb[:, 0:n], W[:, kc, wcol0 + f * 128:wcol0 + (f + 1) * 128], uT[:, kc, t0:t0 + n],
                                                                                 start=(kc == 0), stop=(kc == 7)), [W, uT], [pb], inc=(kc == 7))
                    vb = vTb[(bi * nfeat_chunks + f) % 2]
                    op(ACT, lambda e, f=f, n=n, pb=pb, vb=vb: e.activation(out=vb[:, 0:n], in_=pb[:, 0:n], func=AF.Identity, bias=V(bias_name, bias_i0 + f)),
                       [pb, vecs], [vb])
                    pt = next_bank()
                    ptb = pt.ap.bitcast(BF16)
                    nj = n // 128
                    for j in range(nj):
                        op(PE, lambda e, j=j, vb=vb, ptb=ptb: e.transpose(ptb[:, j * 128:(j + 1) * 128], vb[:, j * 128:(j + 1) * 128], identb.ap),
                           [vb, identb], [pt], inc=(j == nj - 1))
                    c0 = t0 // 128
                    op(DVE, lambda e, f=f, nj=nj, c0=c0, ptb=ptb: e.tensor_copy(out=vtok[:, c0:c0 + nj, f * 128:(f + 1) * 128],
                                                                                in_=ptb[:, 0:nj * 128].rearrange("p (a b) -> p a b", a=nj)), [pt], [vtok])

        k.top = mix_top
        Wm = k.alloc([8, 384], BF16)
        Wo = k.alloc([8, 128], BF16)
        Pq = [k.alloc([T + 3], F32) for _ in range(2)]
        Y = k.alloc([T + 1], F32)
        qkT = [k.alloc([T], BF16) for _ in range(2)]
        vtok = k.alloc([NCH, 129], BF16)
        vTb = [k.alloc([512], BF16) for _ in range(2)]
        RAWd = [k.alloc([16, 129], F32) for _ in range(2)]
        RAW = Buf(RAWd[0][:, :, 0:128])
        RAW1 = Buf(RAWd[1][:, :, 0:128])
        dnm = k.alloc([2, 3, 16], F32)
        sig = [k.alloc([512], F32) for _ in range(2)]
        st = k.alloc([16, 8], F32)
        den3 = k.alloc([2, 4], F32)
        for p_ in Pq:
            op(DVE, lambda e, p_=p_: e.memset(p_.ap, 0.0), [], [p_])
        op(DVE, lambda e: e.memset(vtok.ap, 1.0), [], [vtok])
        cbm = chain_bufs(129)
        dma(POOL, Wm.ap, wm_d[0][:, :, 0:384], writes=[Wm])

        def m_proj(h):
            for kind in range(2):
                P_ = Pq[kind]
                for (t0, n) in TB:
                    pb = next_bank()
                    for kc in range(8):
                        op(PE, lambda e, kc=kc, kind=kind, t0=t0, n=n, pb=pb: e.matmul(pb[:, 0:n], Wm[:, kc, kind * 128:(kind + 1) * 128], uT[:, kc, t0:t0 + n],
                                                                                       start=(kc == 0), stop=(kc == 7)), [Wm, uT], [pb], inc=(kc == 7))
                    pc = 1 if t0 == 0 else t0 + 2
                    op(ACT, lambda e, kind=kind, n=n, pb=pb, pc=pc, P_=P_: e.activation(out=P_[:, pc:pc + n], in_=pb[:, 0:n], func=AF.Identity,
                                                                                         bias=V("bq_m" if kind == 0 else "bk_m", h)), [pb, vecs], [P_])
                ci = kind * 8 + h
                op(DVE, lambda e, P_=P_, ci=ci: e.tensor_scalar(out=Y.ap, in0=P_[:, 1:T + 2], scalar1=V("cw1", ci), scalar2=None, op0=ALU.mult), [P_, vecs], [Y])
                op(DVE, lambda e, P_=P_, ci=ci: e.scalar_tensor_tensor(out=Y.ap, in0=P_[:, 0:T + 1], scalar=V("cw0", ci), in1=Y.ap, op0=ALU.mult, op1=ALU.add),
                   [P_, vecs, Y], [Y])
                op(DVE, lambda e, P_=P_, ci=ci: e.scalar_tensor_tensor(out=Y.ap, in0=P_[:, 2:T + 3], scalar=V("cw2", ci), in1=Y.ap, op0=ALU.mult, op1=ALU.add),
                   [P_, vecs, Y], [Y])
                op(ACT, lambda e, kind=kind: e.activation(out=qkT[kind][:, 0:TC], in_=Y[:, 0:TC], func=AF.Silu), [Y], [qkT[kind]])
                op(ACT, lambda e, kind=kind: e.activation(out=qkT[kind][:, TC:T], in_=Y[:, TC + 1:T + 1], func=AF.Silu), [Y], [qkT[kind]])
            load_vtok(Wm, 256, 1, "bv_m", h, vtok, vTb)
            if h + 1 < 8:
                dma(POOL, Wm.ap, wm_d[h + 1][:, :, 0:384], writes=[Wm])

        def m_chain(h):
            def m_out(d, ci, bO, step):
                op(ACT, lambda e: e.activation(out=RAWd[d][:, ci, :], in_=bO[:, 0:129], func=AF.Copy), [bO], [RAWd[d]])

            chain(cbm, [qkT[0], qkT[0]], [qkT[1], qkT[1]], vtok, 129,
                  lambda d, c: (WS[:, c, d * 8 + h:d * 8 + h + 1], WS),
                  lambda d, c: (EE[:, c, d * 8 + h:d * 8 + h + 1], EE), m_out)

        def m_post_stats(h):
            for d in range(2):
                den = RAWd[d][:, :, 128]
                thd = TH[:, 2:NCH, d * 8 + h]
                op(DVE, lambda e, d=d, den=den: e.tensor_scalar(out=dnm[:, d, 0, :], in0=den, scalar1=-1.0, scalar2=None, op0=ALU.mult), [RAWd[d]], [dnm])
                op(DVE, lambda e, d=d, den=den, thd=thd: e.tensor_tensor(out=dnm[:, d, 1, :], in0=den, in1=thd, op=ALU.max), [RAWd[d], TH], [dnm])
                op(DVE, lambda e, d=d: e.tensor_tensor(out=dnm[:, d, 1, :], in0=dnm[:, d, 1, :], in1=dnm[:, d, 0, :], op=ALU.max), [dnm], [dnm])
                op(DVE, lambda e, d=d: e.reciprocal(out=dnm[:, d, 2, :], in_=dnm[:, d, 1, :]), [dnm], [dnm])
            r0_ = dnm[:, 0, 2, :].rearrange("p (a b) -> p a b", b=1)
            r1_ = dnm[:, 1, 2, :].rearrange("p (a b) -> p a b", b=1)
            op(DVE, lambda e: e.tensor_tensor(out=RAW.ap, in0=RAW.ap, in1=bc_last(r0_, 128), op=ALU.mult), [RAWd[0], dnm], [RAWd[0]])
            op(DVE, lambda e: e.tensor_tensor(out=RAW1.ap, in0=RAW1.ap, in1=bc_last(r1_, 128), op=ALU.mult), [RAWd[1], dnm], [RAWd[1]])
            op(DVE, lambda e: e.tensor_tensor(out=RAW.ap, in0=RAW.ap, in1=RAW1.ap, op=ALU.add), [RAWd[0], RAWd[1]], [RAWd[0]])
            SQ = Y.ap[:, 0:2048].rearrange("p (a b) -> p a b", a=16)
            op(DVE, lambda e: e.tensor_reduce(out=st[:, :, 0], in_=RAW.ap, axis=AX.X, op=ALU.add), [RAWd[0]], [st])
            op(ACT, lambda e: e.activation(out=SQ, in_=RAW.ap, func=AF.Square), [RAWd[0]], [Y])
            op(DVE, lambda e: e.tensor_reduce(out=st[:, :, 1], in_=SQ, axis=AX.X, op=ALU.add), [Y], [st])
            op(DVE, lambda e: e.tensor_scalar(out=st[:, :, 2], in0=st[:, :, 0], scalar1=1.0 / 128, scalar2=None, op0=ALU.mult), [st], [st])
            op(DVE, lambda e: e.tensor_tensor(out=st[:, :, 3], in0=st[:, :, 2], in1=st[:, :, 2], op=ALU.mult), [st], [st])
            op(DVE, lambda e: e.scalar_tensor_tensor(out=st[:, :, 4], in0=st[:, :, 1], scalar=1.0 / 128, in1=st[:, :, 3], op0=ALU.mult, op1=ALU.subtract),
               [st], [st])
            op(DVE, lambda e: e.tensor_scalar(out=st[:, :, 4], in0=st[:, :, 4], scalar1=EPS, scalar2=None, op0=ALU.add), [st], [st])
            op(ACT, lambda e: e.activation(out=st[:, :, 5], in_=st[:, :, 4], func=AF.Sqrt), [st], [st])
            op(DVE, lambda e: e.reciprocal(out=st[:, :, 6], in_=st[:, :, 5]), [st], [st])
            op(DVE, lambda e: e.tensor_tensor(out=RAW.ap, in0=RAW.ap, in1=bc_last(st[:, :, 2:3], 128), op=ALU.subtract), [RAWd[0], st], [RAWd[0]])
            op(DVE, lambda e: e.tensor_tensor(out=RAW.ap, in0=RAW.ap, in1=bc_last(st[:, :, 6:7], 128), op=ALU.mult), [RAWd[0], st], [RAWd[0]])
        def m_post_final(h):
            for tb in range(4):
                t0 = 256 + tb * 512
                pb = next_bank()
                for kc in range(8):
                    op(PE, lambda e, kc=kc, t0=t0, pb=pb: e.matmul(pb.ap, Wo[:, kc, :], uT[:, kc, t0:t0 + 512], start=(kc == 0), stop=(kc == 7)),
                       [Wo, uT], [pb], inc=(kc == 7))
                sg = sig[tb % 2]
                op(ACT, lambda e, pb=pb, sg=sg: e.activation(out=sg.ap, in_=pb.ap, func=AF.Sigmoid, bias=V("bo_m", h)), [pb, vecs], [sg])
                pt = next_bank()
                for j in range(4):
                    op(PE, lambda e, j=j, tb=tb, pt=pt: e.transpose(pt[:, j * 128:(j + 1) * 128], RAW[:, tb * 4 + j, :], identf), [RAWd[0], cst], [pt], inc=(j == 3))
                op(DVE, lambda e, tb=tb, pt=pt, sg=sg: e.scalar_tensor_tensor(out=gated[h][:, tb * 512:(tb + 1) * 512], in0=pt.ap, scalar=V("gn_m", h), in1=sg.ap,
                                                                              op0=ALU.mult, op1=ALU.mult), [pt, vecs, sg], [gated[h]])
        dma(POOL, Wo.ap, wm_d[0][:, :, 384:512], writes=[Wo])
        m_proj(0)
        for h in range(8):
            m_chain(h)
            m_post_stats(h)
            if h + 1 < 8:
                m_proj(h + 1)
            m_post_final(h)
            if h + 1 < 8:
                dma(POOL, Wo.ap, wm_d[h + 1][:, :, 384:512], writes=[Wo])
        k.barrier()
        if b == 0:
            for h in (0, 7):
                dump("gm%d" % h, gated[h])
        if stage == 2:
            k.barrier()
            return nc

        k.top = gla_base
        Wg = k.alloc([8, 512], BF16)
        Wz = k.alloc([8, 256], BF16)
        HT = 768
        Qr = k.alloc([HT], F32)
        Kr = k.alloc([HT], F32)
        T1s = [k.alloc([HT], F32) for _ in range(2)]
        T2s = [k.alloc([HT], F32) for _ in range(2)]
        T1 = T1s[0]
        qt = [k.alloc([T], BF16) for _ in range(2)]
        ktl = [k.alloc([T], BF16) for _ in range(2)]
        vtg = k.alloc([NCH, 256], BF16)
        vTg = [k.alloc([512], BF16) for _ in range(2)]
        RAWg = k.alloc([16, 256], F32)
        DEC = [k.alloc([NCH], F32) for _ in range(2)]
        TOT = k.alloc([6], F32)
        sg2 = [k.alloc([512], F32) for _ in range(2)]
        stg = k.alloc([16, 4], F32)
        junkg = k.alloc([256], F32)
        cbg = chain_bufs(256)
        chain_top = k.top
        HALF = [(0, 768, [(0, 256), (256, 512)]), (768, 768, [(768, 512), (1280, 256)]), (1536, 768, [(1536, 512), (2048, 256)])]
        dma(POOL, Wg.ap, wg_d[0][:, :, 0:512], writes=[Wg])
        dma(POOL, Wz.ap, wg_d[0][:, :, 512:768], writes=[Wz])

        def g_pre(g):
            for (h0, hn, blocks) in HALF:
                nch = hn // 128
                c0 = h0 // 128
                for kind, dst in ((0, Qr), (1, Kr)):
                    for (t0, n) in blocks:
                        pb = next_bank()
                        for kc in range(8):
                            op(PE, lambda e, kc=kc, kind=kind, t0=t0, n=n, pb=pb: e.matmul(pb[:, 0:n], Wg[:, kc, kind * 128:(kind + 1) * 128], uT[:, kc, t0:t0 + n],
                                                                                           start=(kc == 0), stop=(kc == 7)), [Wg, uT], [pb], inc=(kc == 7))
                        op(ACT, lambda e, kind=kind, t0=t0, n=n, pb=pb, dst=dst: e.activation(out=dst[:, t0 - h0:t0 - h0 + n], in_=pb[:, 0:n], func=AF.Identity,
                                                                                               bias=V("bq_g" if kind == 0 else "bk_g", g)), [pb, vecs], [dst])
                for d in range(2):
                    T1 = T1s[d]
                    for (t0, n) in blocks:
                        pb = next_bank()
                        op(PE, lambda e, d=d, t0=t0, n=n, pb=pb: e.matmul(pb[:, 0:n], gwb[0:16, d * 512 + g * 128:d * 512 + (g + 1) * 128], RT[d][0:16, t0:t0 + n],
                                                                          start=True, stop=True), [gwb, RT[d]], [pb])
                        op(ACT, lambda e, d=d, t0=t0, n=n, pb=pb, T1=T1: e.activation(out=T1[:, t0 - h0:t0 - h0 + n], in_=pb[:, 0:n], func=AF.Exp, scale=-1.0,
                                                                                      bias=ngb[:, d * 4 + g:d * 4 + g + 1]), [pb, ngb], [T1])
                    op(ACT, lambda e, T1=T1: e.activation(out=T1[:, 0:hn], in_=T1[:, 0:hn], func=AF.Ln, bias=1.0), [T1], [T1])
                for d in range(2):
                    T1, T2 = T1s[d], T2s[d]
                    op(DVE, lambda e, T1=T1, T2=T2: e.tensor_tensor_scan(out=T2[:, 0:hn], data0=RSTm[:, 0:hn], data1=T1[:, 0:hn], initial=0.0, op0=ALU.mult, op1=ALU.add),
                       [RSTm, T1], [T2])
                    if d == 1:
                        t23 = T2[:, 0:hn].rearrange("p (a b) -> p a b", b=128)
                        op(DVE, lambda e, t23=t23: e.tensor_copy(out=TOT[:, 0:nch], in_=t23[:, :, 127]), [T2], [TOT])
                        op(DVE, lambda e, T1=T1, T2=T2: e.tensor_tensor(out=T2[:, 0:hn], in0=T1[:, 0:hn], in1=T2[:, 0:hn], op=ALU.subtract), [T1, T2], [T2])
                        tot3 = TOT[:, 0:nch].rearrange("p (a b) -> p a b", b=1)
                        op(DVE, lambda e, t23=t23, tot3=tot3: e.tensor_tensor(out=t23, in0=t23, in1=bc_last(tot3, 128), op=ALU.add), [T2, TOT], [T2])
                for d in range(2):
                    T1, T2 = T1s[d], T2s[d]
                    op(ACT, lambda e, T1=T1, T2=T2: e.activation(out=T1[:, 0:hn], in_=T2[:, 0:hn], func=AF.Exp, scale=-1.0 / 16), [T2], [T1])
                for d in range(2):
                    T1 = T1s[d]
                    t13 = T1[:, 0:hn].rearrange("p (a b) -> p a b", b=128)
                    op(DVE, lambda e, d=d, t13=t13: e.tensor_copy(out=DEC[d][:, c0:c0 + nch], in_=t13[:, :, 127 if d == 0 else 0]), [T1], [DEC[d]])
                    op(DVE, lambda e, d=d, T1=T1: e.tensor_tensor(out=qt[d][:, h0:h0 + hn], in0=Qr[:, 0:hn], in1=T1[:, 0:hn], op=ALU.mult), [Qr, T1], [qt[d]])
                for d in range(2):
                    T1, T2 = T1s[d], T2s[d]
                    op(ACT, lambda e, T1=T1, T2=T2: e.activation(out=T1[:, 0:hn], in_=T2[:, 0:hn], func=AF.Exp, scale=1.0 / 16), [T2], [T1])
                for d in range(2):
                    T1 = T1s[d]
                    op(DVE, lambda e, d=d, T1=T1: e.tensor_tensor(out=ktl[d][:, h0:h0 + hn], in0=Kr[:, 0:hn], in1=T1[:, 0:hn], op=ALU.mult), [Kr, T1], [ktl[d]])
            load_vtok(Wg, 256, 2, "bv_g", g * 2, vtg, vTg)
            if g + 1 < 4:
                dma(POOL, Wg.ap, wg_d[g + 1][:, :, 0:512], writes=[Wg])

        def g_chain(g):
            def g_out(d, ci, bO, step):
                mine = ci + 2 if d == 0 else NCH - 1 - ci
                other = NCH - 1 - ci if d == 0 else ci + 2
                first = mine < other or (mine == other and d == 0)
                if first:
                    op(ACT, lambda e: e.activation(out=RAWg[:, ci, :], in_=bO[:, 0:256], func=AF.Copy), [bO], [RAWg])
                else:
                    op(DVE, lambda e: e.tensor_tensor(out=RAWg[:, ci, :], in0=bO[:, 0:256], in1=RAWg[:, ci, :], op=ALU.add), [bO, RAWg], [RAWg])

            chain(cbg, qt, ktl, vtg, 256, None, lambda d, c: (DEC[d][:, c:c + 1], DEC[d]), g_out)

        def g_post_stats(g):
            for ci in range(16):
                op(ACT, lambda e, ci=ci: e.activation(out=junkg.ap, in_=RAWg[:, ci, :], func=AF.Square, accum_out=stg[:, ci, 0:1]), [RAWg], [junkg, stg])
            op(DVE, lambda e: e.tensor_scalar(out=stg[:, :, 1], in0=stg[:, :, 0], scalar1=1.0 / 256, scalar2=EPS * 128.0, op0=ALU.mult, op1=ALU.add), [stg], [stg])
            op(ACT, lambda e: e.activation(out=stg[:, :, 2], in_=stg[:, :, 1], func=AF.Sqrt), [stg], [stg])
            op(DVE, lambda e: e.reciprocal(out=stg[:, :, 3], in_=stg[:, :, 2]), [stg], [stg])
            op(DVE, lambda e: e.tensor_tensor(out=RAWg.ap, in0=RAWg.ap, in1=bc_last(stg[:, :, 3:4], 256), op=ALU.mult), [RAWg, stg], [RAWg])

        def g_post_final(g):
            for tb in range(4):
                t0 = 256 + tb * 512
                for fh in range(2):
                    pb = next_bank()
                    for kc in range(8):
                        op(PE, lambda e, kc=kc, t0=t0, fh=fh, pb=pb: e.matmul(pb.ap, Wz[:, kc, fh * 128:(fh + 1) * 128], uT[:, kc, t0:t0 + 512],
                                                                              start=(kc == 0), stop=(kc == 7)), [Wz, uT], [pb], inc=(kc == 7))
                    sg = sg2[fh]
                    op(ACT, lambda e, pb=pb, sg=sg, fh=fh: e.activation(out=sg.ap, in_=pb.ap, func=AF.Silu, bias=V("bz_g", g * 2 + fh)), [pb, vecs], [sg])
                    pt = next_bank()
                    for j in range(4):
                        op(PE, lambda e, j=j, tb=tb, fh=fh, pt=pt: e.transpose(pt[:, j * 128:(j + 1) * 128], RAWg[:, tb * 4 + j, fh * 128:(fh + 1) * 128], identf),
                           [RAWg, cst], [pt], inc=(j == 3))
                    gi = 8 + g * 2 + fh
                    op(DVE, lambda e, tb=tb, pt=pt, sg=sg, gi=gi, fh=fh: e.scalar_tensor_tensor(out=gated[gi][:, tb * 512:(tb + 1) * 512], in0=pt.ap,
                                                                                                scalar=V("gn_g", g * 2 + fh), in1=sg.ap, op0=ALU.mult, op1=ALU.mult),
                       [pt, vecs, sg], [gated[gi]])

        g_pre(0)
        for g in range(4):
            g_chain(g)
            g_post_stats(g)
            if g + 1 < 4:
                g_pre(g + 1)
            g_post_final(g)
            if g + 1 < 4:
                dma(POOL, Wz.ap, wg_d[g + 1][:, :, 512:768], writes=[Wz])
        k.barrier()
        if b == 0:
            for gi in (8, 15):
                dump("gg%d" % gi, gated[gi])
        if stage == 3:
            k.barrier()
            return nc

        k.top = work_top
        mT = [k.alloc([TL], BF16) for _ in range(8)]
        c2_top = k.top
        Wc = [k.alloc([8, 512], BF16) for _ in range(2)]
        SA = k.alloc([512], F32)
        SBb = k.alloc([512], F32)
        M1 = k.alloc([512], F32)
        M2 = k.alloc([512], F32)
        for n in range(8):
            W = Wc[n % 2]
            dma(POOL, W.ap, wc1_d[n], writes=[W])
            for tb in range(4):
                l0 = tb * 512
                pa, pbk, pm, pg = next_bank(), next_bank(), next_bank(), next_bank()
                for kc in range(8):
                    op(PE, lambda e, kc=kc, pa=pa, l0=l0, W=W: e.matmul(pa.ap, W[:, kc, 256:384], uT[:, kc, TC + l0:TC + l0 + 512], start=(kc == 0), stop=(kc == 7)),
                       [W, uT], [pa], inc=(kc == 7))
                op(ACT, lambda e, pa=pa: e.activation(out=SA.ap, in_=pa.ap, func=AF.Sigmoid, bias=V("ba", n)), [pa, vecs], [SA])
                for kc in range(8):
                    op(PE, lambda e, kc=kc, pbk=pbk, l0=l0, W=W: e.matmul(pbk.ap, W[:, kc, 384:512], uT[:, kc, TC + l0:TC + l0 + 512], start=(kc == 0), stop=(kc == 7)),
                       [W, uT], [pbk], inc=(kc == 7))
                op(ACT, lambda e, pbk=pbk: e.activation(out=SBb.ap, in_=pbk.ap, func=AF.Sigmoid, bias=V("bb", n)), [pbk, vecs], [SBb])
                for kc in range(8):
                    op(PE, lambda e, kc=kc, pm=pm, l0=l0, W=W: e.matmul(pm.ap, W[:, kc, 0:128], gated[kc][:, l0:l0 + 512], start=(kc == 0), stop=(kc == 7)),
                       [W, gated[kc]], [pm], inc=(kc == 7))
                op(DVE, lambda e, pm=pm: e.tensor_tensor(out=M1.ap, in0=pm.ap, in1=SA.ap, op=ALU.mult), [pm, SA], [M1])
                for kc in range(8):
                    op(PE, lambda e, kc=kc, pg=pg, l0=l0, W=W: e.matmul(pg.ap, W[:, kc, 128:256], gated[8 + kc][:, l0:l0 + 512], start=(kc == 0), stop=(kc == 7)),
                       [W, gated[8 + kc]], [pg], inc=(kc == 7))
                op(DVE, lambda e, pg=pg: e.tensor_tensor(out=M2.ap, in0=pg.ap, in1=SBb.ap, op=ALU.mult), [pg, SBb], [M2])
                op(DVE, lambda e, n=n, l0=l0: e.tensor_tensor(out=mT[n][:, l0:l0 + 512], in0=M1.ap, in1=M2.ap, op=ALU.add), [M1, M2], [mT[n]])
        k.barrier()
        if b == 0:
            dump("mT0", mT[0])
        if stage == 4:
            k.barrier()
            return nc

        k.top = c2_top
        h1s = h1s_view()
        wout = k.alloc([8, D], BF16)
        dma(POOL, wout.ap, wout_d, writes=[wout])
        G1bc = k.alloc([D], F32)
        LN1G = k.alloc([D], F32)
        LN1B = k.alloc([D], F32)
        dma(SP, LN1G.ap, rows_d[0:1, :].partition_broadcast(128), writes=[LN1G])
        dma(SP, LN1B.ap, rows_d[1:2, :].partition_broadcast(128), writes=[LN1B])
        dma(SP, G1bc.ap, rows_d[4:5, :].partition_broadcast(128), writes=[G1bc])
        xts = [k.alloc([D], F32) for _ in range(2)]
        Rb = [k.alloc([D], F32) for _ in range(2)]
        stc = k.alloc([2, 6], F32)
        mv = k.alloc([8], F32)
        c2w_top = k.top
        wg1 = k.alloc([8, D], BF16)
        screp = k.alloc([8, 128], BF16)
        dma(POOL, wg1.ap, wmod_d[:, :, 2 * D:3 * D], writes=[wg1])
        op(DVE, lambda e: e.tensor_copy(out=screp.ap, in_=bc_last(scT[:, :, b:b + 1], 128)), [scT], [screp])
        pg1 = PP[0]
        for hf in range(2):
            for kc in range(8):
                op(PE, lambda e, hf=hf, kc=kc: e.matmul(pg1[:, hf * 512:(hf + 1) * 512], screp[:, kc, :], wg1[:, kc, hf * 512:(hf + 1) * 512],
                                                        start=(kc == 0), stop=(kc == 7)), [screp, wg1], [pg1], inc=(kc == 7 and hf == 1))
        op(DVE, lambda e: e.tensor_tensor(out=G1bc.ap, in0=pg1.ap, in1=G1bc.ap, op=ALU.add), [pg1, G1bc], [G1bc])
        k.barrier()
        k.top = c2w_top
        xts = xts + [k.alloc([D], F32) for _ in range(3)]
        for tt in range(16):
            xt, R = xts[tt % 5], Rb[tt % 2]
            pm, ptr = PP[tt % 2], PP[2 + tt % 2]
            dma(SP, xt.ap, xin[b, TC + tt * 128:TC + (tt + 1) * 128, :], writes=[xt])
            for hf in range(2):
                for kc in range(8):
                    op(PE, lambda e, hf=hf, kc=kc, tt=tt, pm=pm: e.matmul(pm[:, hf * 512:(hf + 1) * 512], mT[kc][:, tt * 128:(tt + 1) * 128], wout[:, kc, hf * 512:(hf + 1) * 512],
                                                                          start=(kc == 0), stop=(kc == 7)), [mT[kc], wout], [pm], inc=(kc == 7 and hf == 1))
            op(DVE, lambda e, R=R, pm=pm: e.tensor_tensor(out=R.ap, in0=pm.ap, in1=G1bc.ap, op=ALU.mult), [pm, G1bc], [R])
            op(DVE, lambda e, R=R, xt=xt: e.scalar_tensor_tensor(out=R.ap, in0=xt.ap, scalar=ALPHA, in1=R.ap, op0=ALU.mult, op1=ALU.add), [xt, R], [R])
            for hf in range(2):
                op(DVE, lambda e, R=R, hf=hf: e.bn_stats(out=stc[:, hf, :], in_=R[:, hf * 512:(hf + 1) * 512]), [R], [stc])
            op(DVE, lambda e: e.bn_aggr(out=mv[:, 0:2], in_=stc.ap.rearrange("p a b -> p (a b)")), [stc], [mv])
            op(DVE, lambda e: e.tensor_scalar(out=mv[:, 2:3], in0=mv[:, 1:2], scalar1=EPS, scalar2=None, op0=ALU.add), [mv], [mv])
            op(ACT, lambda e: e.activation(out=mv[:, 3:4], in_=mv[:, 2:3], func=AF.Sqrt), [mv], [mv])
            op(DVE, lambda e: e.reciprocal(out=mv[:, 4:5], in_=mv[:, 3:4]), [mv], [mv])
            op(DVE, lambda e: e.tensor_scalar(out=mv[:, 5:6], in0=mv[:, 0:1], scalar1=-1.0, scalar2=mv[:, 4:5], op0=ALU.mult, op1=ALU.mult), [mv], [mv])
            op(ACT, lambda e, R=R: e.activation(out=R.ap, in_=R.ap, func=AF.Identity, scale=mv[:, 4:5], bias=mv[:, 5:6]), [R, mv], [R])
            op(DVE, lambda e, R=R: e.tensor_tensor(out=R.ap, in0=R.ap, in1=LN1G.ap, op=ALU.mult), [R, LN1G], [R])
            op(DVE, lambda e, R=R: e.tensor_tensor(out=R.ap, in0=R.ap, in1=LN1B.ap, op=ALU.add), [R, LN1B], [R])
            if b == 0 and tt == 0:
                dump("h1t0", R)
            for kc in range(8):
                op(PE, lambda e, kc=kc, R=R, ptr=ptr: e.transpose(ptr[:, kc * 128:(kc + 1) * 128], R[:, kc * 128:(kc + 1) * 128], identf), [R, cst], [ptr], inc=(kc == 7))
            op(DVE, lambda e, tt=tt, ptr=ptr: e.scalar_tensor_tensor(out=h1s[:, :, tt * 128:(tt + 1) * 128], in0=ptr.ap.rearrange("p (a b) -> p a b", a=8),
                                                                      scalar=ALPHA, in1=bc_last(GB[:, :, b:b + 1], 128), op0=ALU.mult, op1=ALU.add),
               [ptr, GB], [h1s])
        k.barrier()

        k.top = uT_off
        stc = k.alloc([2, 6], F32)
        mv = k.alloc([8], F32)
        aT = [k.alloc([1024], BF16) for _ in range(NFC)]
        U2 = k.alloc([8, 1088], BF16)
        G = [k.alloc([18, 66], BF16) for _ in range(2)]
        DG = [k.alloc([9, 128], BF16) for _ in range(2)]
        Wu = [k.alloc([8, 256], BF16) for _ in range(2)]
        GL = [k.alloc([512], F32) for _ in range(2)]
        Wd = [k.alloc([NFC, 128], BF16) for _ in range(2)]
        R2T = k.alloc([8, 512], F32)
        LN2G = k.alloc([D], F32)
        LN2B = k.alloc([D], F32)
        OT = [k.alloc([D], F32) for _ in range(2)]
        dma(SP, LN2G.ap, rows_d[2:3, :].partition_broadcast(128), writes=[LN2G])
        dma(SP, LN2B.ap, rows_d[3:4, :].partition_broadcast(128), writes=[LN2B])
        for hf in range(2):
            t0 = hf * 1024
            lo = 0 if hf == 0 else 960
            grow0 = 1 if hf == 0 else 0
            voff = t0 - lo
            for kc in range(8):
                op(DVE, lambda e, kc=kc, lo=lo: e.tensor_scalar(out=U2[:, kc, :], in0=h1s[:, kc, lo:lo + 1088], scalar1=A2[:, kc, b:b + 1], scalar2=B2[:, kc, b:b + 1],
                                                                op0=ALU.mult, op1=ALU.add), [h1s, A2, B2], [U2])
            for gb_ in G:
                op(DVE, lambda e, gb_=gb_: e.memset(gb_.ap, 0.0), [], [gb_])
            for c in range(NFC):
                W, Gc, Dg = Wu[c % 2], G[c % 2], DG[c % 2]
                dma(POOL, W.ap, wup_d[c], writes=[W])
                for tap in range(9):
                    op(DVE, lambda e, tap=tap, c=c, Dg=Dg: e.tensor_scalar(out=Dg[:, tap, :], in0=identb.ap, scalar1=V("fcw", tap * NFC + c), scalar2=None, op0=ALU.mult), [identb, vecs], [Dg])
                for (p0, n) in ((0, 512), (512, 512), (1024, 64)):
                    pb = next_bank()
                    for kc in range(8):
                        op(PE, lambda e, kc=kc, p0=p0, n=n, pb=pb, W=W: e.matmul(pb[:, 0:n], W[:, kc, 0:128], U2[:, kc, p0:p0 + n], start=(kc == 0), stop=(kc == 7)),
                           [W, U2], [pb], inc=(kc == 7))
                    r_ = grow0 + p0 // 64
                    nr = n // 64
                    op(ACT, lambda e, pb=pb, n=n, r_=r_, nr=nr, Gc=Gc, c=c: e.activation(out=Gc[:, r_:r_ + nr, 1:65], in_=pb[:, 0:n].rearrange("p (a b) -> p a b", b=64),
                                                                                         func=AF.Identity, bias=V("bup_g", c)), [pb, vecs], [Gc])
                for blk in range(2):
                    pc = next_bank()
                    for tap in range(9):
                        dr, dcol = tap // 3, tap % 3
                        op(PE, lambda e, tap=tap, dr=dr, dcol=dcol, blk=blk, pc=pc, Gc=Gc, Dg=Dg: e.matmul(pc.ap, Dg[:, tap, :], Gc[:, blk * 8 + dr:blk * 8 + dr + 8, dcol:dcol + 64],
                                                                                                            start=(tap == 0), stop=(tap == 8)), [Dg, Gc], [pc], inc=(tap == 8))
                    gl = GL[blk]
                    op(ACT, lambda e, pc=pc, gl=gl, c=c: e.activation(out=gl.ap, in_=pc.ap, func=AF.Gelu_apprx_tanh, bias=V("fcb", c)), [pc, vecs], [gl])
                    pv = next_bank()
                    for kc in range(8):
                        op(PE, lambda e, kc=kc, blk=blk, pv=pv, W=W: e.matmul(pv.ap, W[:, kc, 128:256], U2[:, kc, voff + blk * 512:voff + (blk + 1) * 512],
                                                                              start=(kc == 0), stop=(kc == 7)), [W, U2], [pv], inc=(kc == 7))
                    op(DVE, lambda e, pv=pv, gl=gl, c=c, blk=blk: e.scalar_tensor_tensor(out=aT[c][:, blk * 512:(blk + 1) * 512], in0=pv.ap, scalar=V("bup_v", c), in1=gl.ap,
                                                                                         op0=ALU.add, op1=ALU.mult), [pv, vecs, gl], [aT[c]])
            for tb in range(2):
                for n in range(8):
                    W = Wd[n % 2]
                    dma(POOL, W.ap, wdn_d[n], writes=[W])
                    pb = next_bank()
                    for kc in range(NFC):
                        op(PE, lambda e, kc=kc, pb=pb, W=W, tb=tb: e.matmul(pb.ap, W[:, kc, :], aT[kc][:, tb * 512:(tb + 1) * 512], start=(kc == 0), stop=(kc == NFC - 1)),
                           [W, aT[kc]], [pb], inc=(kc == NFC - 1))
                    tg = t0 + tb * 512
                    op(DVE, lambda e, n=n, pb=pb, tg=tg: e.scalar_tensor_tensor(out=R2T[:, n, :], in0=pb.ap, scalar=MOD[:, 40 + n, b:b + 1], in1=h1s[:, n, tg:tg + 512],
                                                                                op0=ALU.mult, op1=ALU.add), [pb, MOD, h1s], [R2T])
                for tt in range(4):
                    ptr = PP[2 + tt % 2]
                    ot = OT[tt % 2]
                    for n in range(8):
                        op(PE, lambda e, n=n, tt=tt, ptr=ptr: e.transpose(ptr[:, n * 128:(n + 1) * 128], R2T[:, n, tt * 128:(tt + 1) * 128], identf), [R2T, cst], [ptr], inc=(n == 7))
                    for q in range(2):
                        op(DVE, lambda e, q=q, ptr=ptr: e.bn_stats(out=stc[:, q, :], in_=ptr[:, q * 512:(q + 1) * 512]), [ptr], [stc])
                    op(DVE, lambda e: e.bn_aggr(out=mv[:, 0:2], in_=stc.ap.rearrange("p a b -> p (a b)")), [stc], [mv])
                    op(DVE, lambda e: e.tensor_scalar(out=mv[:, 2:3], in0=mv[:, 1:2], scalar1=EPS, scalar2=None, op0=ALU.add), [mv], [mv])
                    op(ACT, lambda e: e.activation(out=mv[:, 3:4], in_=mv[:, 2:3], func=AF.Sqrt), [mv], [mv])
                    op(DVE, lambda e: e.reciprocal(out=mv[:, 4:5], in_=mv[:, 3:4]), [mv], [mv])
                    op(DVE, lambda e: e.tensor_scalar(out=mv[:, 5:6], in0=mv[:, 0:1], scalar1=-1.0, scalar2=mv[:, 4:5], op0=ALU.mult, op1=ALU.mult), [mv], [mv])
                    op(DVE, lambda e, ot=ot, ptr=ptr: e.tensor_scalar(out=ot.ap, in0=ptr.ap, scalar1=mv[:, 4:5], scalar2=mv[:, 5:6], op0=ALU.mult, op1=ALU.add), [ptr, mv], [ot])
                    op(DVE, lambda e, ot=ot: e.tensor_tensor(out=ot.ap, in0=ot.ap, in1=LN2G.ap, op=ALU.mult), [ot, LN2G], [ot])
                    op(DVE, lambda e, ot=ot: e.tensor_tensor(out=ot.ap, in0=ot.ap, in1=LN2B.ap, op=ALU.add), [ot, LN2B], [ot])
                    r0 = t0 + tb * 512 + tt * 128
                    dma(SP, out_d[b, r0:r0 + 128, :], ot.ap, reads=[ot])
        k.barrier()
    k.barrier()
    return nc


def _fm(v, nchunk):
    return np.ascontiguousarray(np.asarray(v, np.float32).reshape(nchunk, 128).T)


def _kmaj(w):
    K, N = w.shape
    return np.ascontiguousarray(np.asarray(w, np.float32).reshape(K // 128, 128, N).transpose(1, 0, 2))


def prep_shared(inp):
    g = lambda n: np.asarray(inp[n], np.float32)[0]
    w_in, b_in = g("w_in"), g("b_in")
    o_q, o_k, o_v, o_o, o_g = 0, 1024, 2048, 3072, 4096
    o_qg = 4128
    o_kg, o_vg, o_zg, o_r = o_qg + 512, o_qg + 1024, o_qg + 2048, o_qg + 3072
    o_a = o_r + 32
    o_b = o_a + 1024
    vec = np.zeros((128, NV), np.float32)

    def put(name, arr):
        vec[:, VOFF[name]:VOFF[name] + arr.shape[1]] = arr

    put("bq_m", _fm(b_in[o_q:o_q + 1024], 8)); put("bk_m", _fm(b_in[o_k:o_k + 1024], 8))
    put("bv_m", _fm(b_in[o_v:o_v + 1024], 8)); put("bo_m", _fm(b_in[o_o:o_o + 1024], 8))
    cq = g("conv_qk")
    for j in range(3):
        put("cw%d" % j, _fm(cq[j], 16))
    put("bq_g", _fm(b_in[o_qg:o_qg + 512], 4)); put("bk_g", _fm(b_in[o_kg:o_kg + 512], 4))
    put("bv_g", _fm(b_in[o_vg:o_vg + 1024], 8)); put("bz_g", _fm(b_in[o_zg:o_zg + 1024], 8))
    put("gb_f", _fm(g("gla_gate_b_fwd"), 4)); put("gb_b", _fm(g("gla_gate_b_bwd"), 4))
    put("gn_m", _fm(g("mlstm_norm_g"), 8)); put("gn_g", _fm(g("gla_norm_g"), 8))
    put("ba", _fm(b_in[o_a:o_a + 1024], 8)); put("bb", _fm(b_in[o_b:o_b + 1024], 8))
    bup = g("b_up")
    put("bup_g", _fm(bup[:DFF], NFC)); put("bup_v", _fm(bup[DFF:], NFC))
    fcw = g("ffn_conv_w").reshape(9, DFF)
    put("fcw", np.concatenate([_fm(fcw[t], NFC) for t in range(9)], axis=1))
    put("fcb", _fm(g("ffn_conv_b"), NFC)); put("bdn", _fm(g("b_down"), 8)); put("bmod", _fm(g("b_mod"), 48))
    br = np.zeros((128, 2), np.float32)
    br[0:16, 0] = b_in[o_r:o_r + 16]
    br[0:16, 1] = b_in[o_r + 16:o_r + 32]
    put("br", br)
    wk = _kmaj(w_in)
    w_m = np.stack([np.concatenate([wk[:, :, o + h * 128:o + (h + 1) * 128] for o in (o_q, o_k, o_v, o_o)], axis=2) for h in range(8)])
    w_g = np.stack([np.concatenate([wk[:, :, o_qg + h * 128:o_qg + (h + 1) * 128], wk[:, :, o_kg + h * 128:o_kg + (h + 1) * 128],
                                    wk[:, :, o_vg + h * 256:o_vg + (h + 1) * 256], wk[:, :, o_zg + h * 256:o_zg + (h + 1) * 256]], axis=2) for h in range(4)])
    w_s = np.concatenate([wk[:, :, o_g:o_g + 32], wk[:, :, o_r:o_r + 32]], axis=2)
    wbm, wbg = _kmaj(g("w_branch_mlstm")), _kmaj(g("w_branch_gla"))
    wc1 = np.stack([np.concatenate([wbm[:, :, n * 128:(n + 1) * 128], wbg[:, :, n * 128:(n + 1) * 128],
                                    wk[:, :, o_a + n * 128:o_a + (n + 1) * 128], wk[:, :, o_b + n * 128:o_b + (n + 1) * 128]], axis=2) for n in range(8)])
    wu = _kmaj(g("w_up"))
    wup = np.stack([np.concatenate([wu[:, :, c * 128:(c + 1) * 128], wu[:, :, DFF + c * 128:DFF + (c + 1) * 128]], axis=2) for c in range(NFC)])
    wd = _kmaj(g("w_down"))
    wdn = np.stack([wd[:, :, n * 128:(n + 1) * 128] for n in range(8)])
    rows = np.zeros((8, D), np.float32)
    rows[0], rows[1], rows[2], rows[3] = g("ln1_g"), g("ln1_b"), g("ln2_g"), g("ln2_b")
    bm = g("b_mod")
    rows[4] = bm[2 * D:3 * D]
    rows[6, 0:32] = b_in[o_g:o_g + 32]
    idx = np.arange(128)
    cst = np.zeros((128, 4, 128), np.float32)
    cst[:, 0, :] = np.eye(128)
    cst[:, 1, :] = (idx[:, None] <= idx[None, :])
    cst[:, 2, :] = (idx[:, None] >= idx[None, :])
    cst[:, 3, :] = 1.0
    rst = np.ones((128, T), np.float32)
    rst[:, ::128] = 0.0
    c = lambda a: np.ascontiguousarray(a, dtype=np.float32)
    return dict(wmod=_kmaj(g("w_mod")), vecs=vec, w_m=c(w_m), w_g=c(w_g), w_s=c(w_s),
                gw=c(np.concatenate([g("gla_gate_w_fwd"), g("gla_gate_w_bwd")], axis=1)), rows=rows, wc1=c(wc1),
                wout=_kmaj(g("w_out")), wup=c(wup), wdn=c(wdn), cst=cst, rst=rst)


def prep_core(inp, bs):
    x, ctx, cvec, cctx = (np.asarray(inp[n], np.float32) for n in ("x", "ctx", "c", "c_ctx"))
    nb = len(bs)
    xin = np.ascontiguousarray(np.concatenate([ctx[bs], x[bs]], axis=1))
    cT = np.zeros((128, 8, 4), np.float32)
    for j, b in enumerate(bs):
        cT[:, :, j] = _fm(cvec[b], 8)
    cT[:, :, nb] = _fm(cctx, 8)
    return dict(xin=xin, cT=cT)


def kernel(**inputs):
    n = 8
    NB = 2
    shared = prep_shared(inputs)
    nc = build(NB)
    in_maps = []
    for i in range(n):
        m = dict(shared)
        m.update(prep_core(inputs, list(range(i * NB, (i + 1) * NB))))
        in_maps.append(m)
    res = run_bass_kernel_spmd(nc, in_maps, core_ids=list(range(n)))
    return np.concatenate([r["out"] for r in res.results], axis=0).astype(np.float32)
```

```python
import numpy as np
import concourse.bass as bass
import concourse.mybir as mybir
from concourse.bass_utils import run_bass_kernel_spmd

F32 = mybir.dt.float32
BF16 = mybir.dt.bfloat16
U8 = mybir.dt.uint8
AF = mybir.ActivationFunctionType
ALU = mybir.AluOpType
AX = mybir.AxisListType

D = 1024
TC = 256
TL = 2048
T = TC + TL
NCH = T // 128
DFF = 2816
NFC = DFF // 128
ALPHA = 2.0 ** 0.25
EPS = 1e-5
NSLOT = 24
ATTACH = True

_VSPEC = [("bq_m", 8), ("bk_m", 8), ("bv_m", 8), ("bo_m", 8), ("cw0", 16), ("cw1", 16), ("cw2", 16),
          ("bq_g", 4), ("bk_g", 4), ("bv_g", 8), ("bz_g", 8), ("gb_f", 4), ("gb_b", 4),
          ("gn_m", 8), ("gn_g", 8), ("ba", 8), ("bb", 8), ("bup_g", NFC), ("bup_v", NFC),
          ("fcw", 9 * NFC), ("fcb", NFC), ("bdn", 8), ("bmod", 48), ("br", 2)]
VOFF = {}
_o = 0
for _n, _c in _VSPEC:
    VOFF[_n] = _o
    _o += _c
NV = _o


class Own:
    def __init__(self, name, sem, inc):
        self.name, self.sem, self.inc, self.count = name, sem, inc, 0


class Iss:
    def __init__(self, h, own=None):
        self.h, self.own, self.seen = h, own, {}


class Buf:
    __slots__ = ("ap", "w", "r")

    def __init__(self, ap):
        self.ap, self.w, self.r = ap, None, {}

    def __getitem__(self, k):
        return self.ap[k]


class Pair:
    def __init__(self, ap, bufs):
        self.ap, self.bufs = ap, bufs

    def __getitem__(self, k):
        return self.ap[k]


def _flat(lst):
    out = []
    for t in lst:
        if isinstance(t, Pair):
            out.extend(t.bufs)
        else:
            out.append(t)
    return out


class KB:
    def __init__(self, nc, sbuf_bytes):
        self.nc = nc
        mk = lambda n, inc: Own(n, nc.alloc_semaphore(n), inc)
        self.oPE, self.oACT, self.oDVE, self.oPOOL = mk("pe", 1), mk("act", 1), mk("dve", 1), mk("pool", 1)
        self.PE, self.ACT = Iss(nc.tensor, self.oPE), Iss(nc.scalar, self.oACT)
        self.DVE, self.POOL = Iss(nc.vector, self.oDVE), Iss(nc.gpsimd, self.oPOOL)
        self.SP = Iss(nc.sync, None)
        self.slots = [mk("dma%d" % i, 16) for i in range(NSLOT)]
        self.hw_slots, self.sw_slots = self.slots[:NSLOT // 2], self.slots[NSLOT // 2:]
        self.dma_i = {id(self.hw_slots): 0, id(self.sw_slots): 0}
        self.big = nc.alloc_sbuf_tensor("big", [128, sbuf_bytes], U8)
        self.cap = sbuf_bytes
        self.top = 0

    def alloc(self, free, dt):
        sz = 4 if dt == F32 else 2
        n = 1
        for q in free:
            n *= q
        off = (self.top + 31) // 32 * 32
        self.top = off + n * sz
        assert self.top <= self.cap, ("SBUF overflow", self.top, self.cap)
        ap = self.big[:, off:off + n * sz].bitcast(dt)
        if len(free) == 2:
            ap = ap.rearrange("p (a b) -> p a b", a=free[0])
        elif len(free) == 3:
            ap = ap.rearrange("p (a b c) -> p a b c", a=free[0], b=free[1])
        return Buf(ap)

    def _waits(self, iss, reads, writes, attach=False):
        need = {}
        for t in reads:
            if t.w is not None:
                o, v = t.w
                if need.get(o, 0) < v:
                    need[o] = v
        for t in writes:
            if t.w is not None:
                o, v = t.w
                if need.get(o, 0) < v:
                    need[o] = v
            for o, v in t.r.items():
                if need.get(o, 0) < v:
                    need[o] = v
        todo = []
        for o, v in need.items():
            if o is iss.own and o is self.oPE:
                continue
            if iss.seen.get(o, 0) >= v:
                continue
            todo.append((o, v))
            iss.seen[o] = v
        last = todo.pop() if (attach and todo) else None
        for o, v in todo:
            iss.h.wait_ge(o.sem, v)
        return last

    @staticmethod
    def _record(own, v, reads, writes):
        for t in reads:
            if t.r.get(own, 0) < v:
                t.r[own] = v
        for t in writes:
            t.w = (own, v)
            t.r = {}

    def op(self, iss, fn, reads=(), writes=(), inc=True):
        reads, writes = _flat(reads), _flat(writes)
        last = self._waits(iss, reads, writes, attach=(ATTACH and iss is not self.PE))
        ins = fn(iss.h)
        if last is not None:
            ins._wait_ge(last[0].sem, last[1])
        own = iss.own
        if inc:
            own.count += 1
            ins.then_inc(own.sem, 1)
            v = own.count
        else:
            v = own.count + 1
        self._record(own, v, reads, writes)

    def dma(self, iss, out, in_, reads=(), writes=()):
        reads, writes = _flat(reads), _flat(writes)
        pool = self.sw_slots if iss is self.POOL else self.hw_slots
        slot = pool[self.dma_i[id(pool)] % len(pool)]
        self.dma_i[id(pool)] += 1
        if slot.count > iss.seen.get(slot, 0):
            iss.h.wait_ge(slot.sem, slot.count)
            iss.seen[slot] = slot.count
        last = self._waits(iss, reads, writes, attach=ATTACH)
        ins = iss.h.dma_start(out=out, in_=in_, max_dma_last_dim=4096) if iss is self.POOL else iss.h.dma_start(out=out, in_=in_)
        if last is not None:
            ins._wait_ge(last[0].sem, last[1])
        slot.count += 16
        ins.then_inc(slot.sem, 16)
        self._record(slot, slot.count, reads, writes)

    def barrier(self):
        owners = [self.oPE, self.oACT, self.oDVE, self.oPOOL] + self.slots
        for iss in (self.PE, self.ACT, self.DVE, self.POOL, self.SP):
            for o in owners:
                if o.count > iss.seen.get(o, 0):
                    iss.h.wait_ge(o.sem, o.count)
                    iss.seen[o] = o.count


def bc_last(ap, n):
    sh = list(ap.shape)
    sh[-1] = n
    return ap.broadcast_to(sh)


def build(NB=2, dbg=None, stage=99):
    nc = bass.Bass("TRN2", target_bir_lowering=False)
    dram = lambda n, s, k="ExternalInput": nc.dram_tensor(n, list(s), F32, kind=k).ap()
    xin = dram("xin", [NB, T, D])
    cT_d = dram("cT", [128, 8, 4])
    wmod_d = dram("wmod", [128, 8, 6 * D])
    vecs_d = dram("vecs", [128, NV])
    wm_d = dram("w_m", [8, 128, 8, 512])
    wg_d = dram("w_g", [4, 128, 8, 768])
    ws_d = dram("w_s", [128, 8, 64])
    gw_d = dram("gw", [16, 1024])
    rows_d = dram("rows", [8, D])
    wc1_d = dram("wc1", [8, 128, 8, 512])
    wout_d = dram("wout", [128, 8, D])
    wup_d = dram("wup", [NFC, 128, 8, 256])
    wdn_d = dram("wdn", [8, 128, NFC, 128])
    cst_d = dram("cst", [128, 4, 128])
    rst_d = dram("rst", [128, T])
    out_d = dram("out", [NB, TL, D], "ExternalOutput")
    dbg_d = {}
    if dbg:
        for n, (s, dt_) in dbg.items():
            dbg_d[n] = nc.dram_tensor("dbg_" + n, list(s), dt_, kind="ExternalOutput").ap()

    k = KB(nc, 206 * 1024)
    PE, ACT, DVE, POOL, SP = k.PE, k.ACT, k.DVE, k.POOL, k.SP
    op, dma = k.op, k.dma

    _pt = [nc.alloc_psum_tensor("pp%d" % i, [128, 1024], F32)[:, :] for i in range(4)]
    PB = [[Buf(_pt[i][:, 0:512]), Buf(_pt[i][:, 512:1024])] for i in range(4)]
    PP = [Pair(_pt[i], PB[i]) for i in range(4)]
    pb_rr = [0]

    def next_bank():
        i = pb_rr[0] % 4
        pb_rr[0] += 1
        return PB[i // 2][i % 2]

    def dump(name, buf, ap=None):
        if name in dbg_d:
            dma(SP, dbg_d[name], buf.ap if ap is None else ap, reads=[buf])

    cst = k.alloc([4, 128], F32)
    identf, maskU, maskL, onesf = cst[:, 0, :], cst[:, 1, :], cst[:, 2, :], cst[:, 3, :]
    dma(SP, cst.ap, cst_d, writes=[cst])
    identb = k.alloc([128], BF16)
    op(DVE, lambda e: e.tensor_copy(out=identb.ap, in_=identf), [cst], [identb])
    vecs = k.alloc([NV], F32)
    dma(SP, vecs.ap, vecs_d, writes=[vecs])
    V = lambda name, i=0: vecs[:, VOFF[name] + i:VOFF[name] + i + 1]
    cT = k.alloc([8, 4], F32)
    dma(SP, cT.ap, cT_d, writes=[cT])
    scT = k.alloc([8, 4], BF16)
    op(ACT, lambda e: e.activation(out=scT.ap, in_=cT.ap, func=AF.Silu), [cT], [scT])
    ws = k.alloc([8, 64], BF16)
    dma(POOL, ws.ap, ws_d, writes=[ws])
    gwb = k.alloc([1024], BF16)
    dma(POOL, gwb[0:16, :], gw_d, writes=[gwb])
    ngb = k.alloc([8], F32)
    op(DVE, lambda e: e.tensor_scalar(out=ngb.ap, in0=vecs[:, VOFF["gb_f"]:VOFF["gb_f"] + 8], scalar1=-1.0,
                                      scalar2=None, op0=ALU.mult), [vecs], [ngb])
    MOD = k.alloc([48, 4], F32)
    SC1 = k.alloc([8, 4], F32)
    A2 = k.alloc([8, 4], F32)
    B2 = k.alloc([8, 4], F32)
    GB = k.alloc([8, 4], F32)
    persist_top = k.top

    wpiece = k.alloc([8, 512], BF16)
    modps = PB[3][1]
    for nb in range(12):
        dma(POOL, wpiece.ap, wmod_d[:, :, nb * 512:(nb + 1) * 512], writes=[wpiece])
        for j in range(4):
            n = nb * 4 + j
            for kc in range(8):
                op(PE, lambda e, j=j, kc=kc, n=n: e.matmul(modps[:, n * 4:n * 4 + 4], wpiece[:, kc, j * 128:(j + 1) * 128],
                                                           scT[:, kc, :], start=(kc == 0), stop=(kc == 7)),
                   [wpiece, scT], [modps], inc=(kc == 7))
    bm3 = vecs[:, VOFF["bmod"]:VOFF["bmod"] + 48].rearrange("p (a b) -> p a b", b=1)
    op(DVE, lambda e: e.tensor_tensor(out=MOD.ap, in0=modps[:, 0:192].rearrange("p (a b) -> p a b", b=4),
                                      in1=bc_last(bm3, 4), op=ALU.add), [modps, vecs], [MOD])
    op(DVE, lambda e: e.tensor_scalar(out=SC1.ap, in0=MOD[:, 8:16, :], scalar1=1.0, scalar2=None, op0=ALU.add), [MOD], [SC1])
    bd3 = vecs[:, VOFF["bdn"]:VOFF["bdn"] + 8].rearrange("p (a b) -> p a b", b=1)
    op(DVE, lambda e: e.tensor_tensor(out=GB.ap, in0=MOD[:, 40:48, :], in1=bc_last(bd3, 4), op=ALU.mult), [MOD, vecs], [GB])
    op(DVE, lambda e: e.tensor_scalar(out=A2.ap, in0=MOD[:, 32:40, :], scalar1=1.0, scalar2=1.0 / ALPHA,
                                      op0=ALU.add, op1=ALU.mult), [MOD], [A2])
    op(DVE, lambda e: e.tensor_tensor(out=B2.ap, in0=GB.ap, in1=A2.ap, op=ALU.mult), [GB, A2], [B2])
    op(DVE, lambda e: e.tensor_tensor(out=B2.ap, in0=MOD[:, 24:32, :], in1=B2.ap, op=ALU.subtract), [MOD, B2], [B2])
    k.barrier()
    k.top = persist_top

    R2_off = (k.top + 31) // 32 * 32
    gated = [k.alloc([TL], BF16) for _ in range(16)]
    uT_off = k.top
    uT = k.alloc([8, T], BF16)
    work_top = k.top

    def h1s_view():
        ap = k.big[:, R2_off:R2_off + 8 * TL * 4].bitcast(F32).rearrange("p (a b) -> p a b", a=8)
        return Buf(ap)

    for b in range(NB):
        k.top = work_top
        xts = [k.alloc([D], F32) for _ in range(6)]
        tmpA = [k.alloc([8, 128], F32) for _ in range(2)]
        for tt in range(NCH):
            xt = xts[tt % 6]
            pp = PP[tt % 2]
            col = NB if tt < 2 else b
            dma(SP, xt.ap, xin[b, tt * 128:(tt + 1) * 128, :], writes=[xt])
            for kc in range(8):
                op(PE, lambda e, kc=kc, xt=xt, pp=pp: e.transpose(pp[:, kc * 128:(kc + 1) * 128], xt[:, kc * 128:(kc + 1) * 128], identf),
                   [xt, cst], [pp], inc=(kc == 7))
            tm = tmpA[tt % 2]
            p3 = pp.ap.rearrange("p (a b) -> p a b", a=8)
            op(DVE, lambda e, tm=tm, p3=p3, col=col: e.tensor_tensor(out=tm.ap, in0=p3, in1=bc_last(SC1[:, :, col:col + 1], 128), op=ALU.mult),
               [pp, SC1], [tm])
            op(DVE, lambda e, tm=tm, tt=tt, col=col: e.tensor_tensor(out=uT[:, :, tt * 128:(tt + 1) * 128], in0=tm.ap,
                                                                      in1=bc_last(MOD[:, 0:8, col:col + 1], 128), op=ALU.add),
               [tm, MOD], [uT])
        k.barrier()
        dump("uT%d" % b, uT)
        if stage == 1:
            k.barrier()
            return nc

        k.top = work_top
        TB = [(0, 256)] + [(256 + i * 512, 512) for i in range(4)]
        RSTm = k.alloc([768], BF16)
        dma(POOL, RSTm.ap, rst_d[:, 0:768], writes=[RSTm])
        RT = [k.alloc([T], BF16) for _ in range(2)]
        gla_base = k.top
        WS = k.alloc([NCH, 16], F32)
        TH = k.alloc([NCH, 16], F32)
        EE = k.alloc([NCH, 16], F32)
        mix_top = k.top
        GM = k.alloc([NCH, 32], F32)
        LFN = k.alloc([NCH, 16], F32)
        NBt = k.alloc([NCH, 32], F32)
        bgm = k.alloc([32], F32)
        dma(SP, bgm.ap, rows_d[6:7, 0:32].partition_broadcast(128), writes=[bgm])
        gp = PP[2]
        for tt in range(NCH):
            for kc in range(8):
                op(PE, lambda e, tt=tt, kc=kc: e.matmul(gp[:, tt * 32:(tt + 1) * 32], uT[:, kc, tt * 128:(tt + 1) * 128], ws[:, kc, 0:32],
                                                        start=(kc == 0), stop=(kc == 7)), [uT, ws], [gp], inc=(kc == 7))
        b3 = bgm.ap.rearrange("p (a b) -> p a b", a=1).broadcast_to([128, NCH, 32])
        op(DVE, lambda e: e.tensor_tensor(out=GM.ap, in0=gp[:, 0:NCH * 32].rearrange("p (a b) -> p a b", b=32), in1=b3, op=ALU.add),
           [gp, bgm], [GM])
        op(ACT, lambda e: e.activation(out=LFN.ap, in_=GM[:, :, 16:32], func=AF.Exp, scale=-1.0), [GM], [LFN])
        op(ACT, lambda e: e.activation(out=LFN.ap, in_=LFN.ap, func=AF.Ln, bias=1.0), [LFN], [LFN])
        gp2 = PP[3]
        for c in range(NCH):
            op(PE, lambda e, c=c: e.matmul(gp2[:, c * 32:c * 32 + 8], maskU, LFN[:, c, 0:8], start=True, stop=True), [cst, LFN], [gp2], inc=False)
            op(PE, lambda e, c=c: e.matmul(gp2[:, c * 32 + 8:c * 32 + 16], maskL, LFN[:, c, 8:16], start=True, stop=True), [cst, LFN], [gp2], inc=False)
            op(PE, lambda e, c=c: e.matmul(gp2[:, c * 32 + 16:c * 32 + 32], onesf, LFN[:, c, 0:16], start=True, stop=True), [cst, LFN], [gp2])
        op(DVE, lambda e: e.tensor_copy(out=NBt.ap, in_=gp2[:, 0:NCH * 32].rearrange("p (a b) -> p a b", b=32)), [gp2], [NBt])
        op(DVE, lambda e: e.tensor_tensor(out=WS.ap, in0=GM[:, :, 0:16], in1=NBt[:, :, 0:16], op=ALU.add), [GM, NBt], [WS])
        op(ACT, lambda e: e.activation(out=WS.ap, in_=WS.ap, func=AF.Exp), [WS], [WS])
        lnc = k.alloc([1], F32)
        op(DVE, lambda e: e.memset(lnc.ap, 0.5 * float(np.log(128.0))), [], [lnc])
        op(ACT, lambda e: e.activation(out=TH.ap, in_=NBt[:, :, 0:16], func=AF.Exp, bias=lnc[:, 0:1]), [NBt, lnc], [TH])
        op(ACT, lambda e: e.activation(out=EE.ap, in_=NBt[:, :, 16:32], func=AF.Exp, scale=-1.0), [NBt], [EE])
        for d in range(2):
            for (t0, n) in TB:
                pb = next_bank()
                for kc in range(8):
                    op(PE, lambda e, d=d, kc=kc, t0=t0, n=n, pb=pb: e.matmul(pb[0:16, 0:n], ws[:, kc, 32 + d * 16:48 + d * 16], uT[:, kc, t0:t0 + n],
                                                                             start=(kc == 0), stop=(kc == 7)), [ws, uT], [pb], inc=(kc == 7))
                op(ACT, lambda e, d=d, t0=t0, n=n, pb=pb: e.activation(out=RT[d][0:16, t0:t0 + n], in_=pb[0:16, 0:n], func=AF.Identity,
                                                                       bias=vecs[0:16, VOFF["br"] + d:VOFF["br"] + d + 1]), [pb, vecs], [RT[d]])
        k.barrier()

        cS = [PB[2 + d][1] for d in range(2)]
        cT_ = [PB[2 + d][0] for d in range(2)]
        cO = [PB[0][d] for d in range(2)]
        cC = [PB[1][d] for d in range(2)]

        def chain_bufs(Wv):
            return dict(D32=[k.alloc([Wv], F32) for _ in range(2)], Cbf=[k.alloc([Wv], BF16) for _ in range(2)],
                        ktok=[[k.alloc([128], BF16) for _ in range(2)] for _ in range(2)],
                        SPm=[[k.alloc([128], BF16) for _ in range(2)] for _ in range(2)])

        def chain(cb, qT, kT, vtok, Wv, colscale, decay, emit_out):
            D32, Cbf, ktok, SPm = cb["D32"], cb["Cbf"], cb["ktok"], cb["SPm"]
            order = [list(range(NCH)), [1, 0] + list(range(NCH - 1, 1, -1))]
            for d in range(2):
                op(DVE, lambda e, d=d: e.memset(Cbf[d].ap, 0.0), [], [Cbf[d]])

            def stageA(step):
                p = step % 2
                for d in range(2):
                    c = order[d][step]
                    kc_ = kT[d][:, c * 128:(c + 1) * 128]
                    if step != NCH - 1:
                        tr = cT_[d]
                        ptr = tr.ap[:, 0:64].bitcast(BF16)
                        op(PE, lambda e, kc_=kc_, ptr=ptr: e.transpose(ptr, kc_, identb.ap), [kT[d], identb], [tr])
                    if c >= 2:
                        qc = qT[d][:, c * 128:(c + 1) * 128]
                        op(PE, lambda e, kc_=kc_, qc=qc, d=d, p=p: e.matmul(cS[d][:, 0:128], kc_, qc, start=True, stop=True), [kT[d], qT[d]], [cS[d]])
                for d in range(2):
                    c = order[d][step]
                    cs = None if colscale is None else colscale(d, c)
                    if step != NCH - 1:
                        kt = ktok[d][p]
                        ptr = cT_[d].ap[:, 0:64].bitcast(BF16)
                        if cs is None:
                            op(ACT, lambda e, kt=kt, ptr=ptr: e.activation(out=kt.ap, in_=ptr, func=AF.Copy), [cT_[d]], [kt])
                        else:
                            op(ACT, lambda e, kt=kt, ptr=ptr, cs=cs: e.activation(out=kt.ap, in_=ptr, func=AF.Copy, scale=cs[0]), [cT_[d], cs[1]], [kt])
                    if c >= 2:
                        sp = SPm[d][p]
                        msk = maskU if d == 0 else maskL
                        if cs is None:
                            op(DVE, lambda e, sp=sp, d=d, p=p, msk=msk: e.tensor_tensor(out=sp.ap, in0=cS[d][:, 0:128], in1=msk, op=ALU.mult), [cS[d], cst], [sp])
                        else:
                            op(DVE, lambda e, sp=sp, d=d, p=p, msk=msk, cs=cs: e.scalar_tensor_tensor(out=sp.ap, in0=cS[d][:, 0:128], scalar=cs[0], in1=msk,
                                                                                                    op0=ALU.mult, op1=ALU.mult), [cS[d], cst, cs[1]], [sp])

            def stageB(step):
                p = step % 2
                for d in range(2):
                    c = order[d][step]
                    if c >= 2:
                        qc = qT[d][:, c * 128:(c + 1) * 128]
                        sp = SPm[d][p]
                        op(PE, lambda e, sp=sp, c=c, d=d: e.matmul(cO[d][:, 0:Wv], sp.ap, vtok[:, c, 0:Wv], start=True, stop=False), [sp, vtok], [cO[d]], inc=False)
                        op(PE, lambda e, qc=qc, d=d: e.matmul(cO[d][:, 0:Wv], qc, Cbf[d].ap, start=False, stop=True), [qT[d], Cbf[d]], [cO[d]])
                        emit_out(d, c - 2, cO[d], step)
                    if step != NCH - 1:
                        kt = ktok[d][p]
                        dc = decay(d, c)
                        op(PE, lambda e, kt=kt, c=c, d=d: e.matmul(cC[d][:, 0:Wv], kt.ap, vtok[:, c, 0:Wv], start=True, stop=True), [kt, vtok], [cC[d]])
                        if step == 0:
                            op(DVE, lambda e, d=d: e.tensor_copy(out=D32[d].ap, in_=cC[d][:, 0:Wv]), [cC[d]], [D32[d]])
                        else:
                            dp = decay(d, order[d][step - 1])
                            op(DVE, lambda e, d=d, dp=dp: e.scalar_tensor_tensor(out=D32[d].ap, in0=D32[d].ap, scalar=dp[0], in1=cC[d][:, 0:Wv],
                                                                                 op0=ALU.mult, op1=ALU.add), [D32[d], cC[d], dp[1]], [D32[d]])
                        op(POOL, lambda e, d=d, dc=dc: e.tensor_scalar(out=Cbf[d].ap, in0=D32[d].ap, scalar1=dc[0], scalar2=1.0, op0=ALU.mult, op1=ALU.mult), [D32[d], dc[1]], [Cbf[d]])

            stageA(0)
            for step in range(NCH):
                if step + 1 < NCH:
                    stageA(step + 1)
                stageB(step)

        def load_vtok(W, wcol0, nfeat_chunks, bias_name, bias_i0, vtok, vTb):
            for bi, (t0, n) in enumerate(TB):
                for f in range(nfeat_chunks):
                    pb = next_bank()
                    for kc in range(8):
                        op(PE, lambda e, kc=kc, f=f, t0=t0, n=n, pb=pb: e.matmul(pb[:, 0:n], W[:, kc, wcol0 + f * 128:wcol0 + (f + 1) * 128], uT[:, kc, t0:t0 + n],
                                                                                 start=(kc == 0), stop=(kc == 7)), [W, uT], [pb], inc=(kc == 7))
                    vb = vTb[(bi * nfeat_chunks + f) % 2]
                    op(ACT, lambda e, f=f, n=n, pb=pb, vb=vb: e.activation(out=vb[:, 0:n], in_=pb[:, 0:n], func=AF.Identity, bias=V(bias_name, bias_i0 + f)),
                       [pb, vecs], [vb])
                    pt = next_bank()
                    ptb = pt.ap.bitcast(BF16)
                    nj = n // 128
                    for j in range(nj):
                        op(PE, lambda e, j=j, vb=vb, ptb=ptb: e.transpose(ptb[:, j * 128:(j + 1) * 128], vb[:, j * 128:(j + 1) * 128], identb.ap),
                           [vb, identb], [pt], inc=(j == nj - 1))
                    c0 = t0 // 128
                    op(DVE, lambda e, f=f, nj=nj, c0=c0, ptb=ptb: e.tensor_copy(out=vtok[:, c0:c0 + nj, f * 128:(f + 1) * 128],
                                                                                in_=ptb[:, 0:nj * 128].rearrange("p (a b) -> p a b", a=nj)), [pt], [vtok])

        k.top = mix_top
        Wm = k.alloc([8, 384], BF16)
        Wo = k.alloc([8, 128], BF16)
        Pq = [k.alloc([T + 3], F32) for _ in range(2)]
        Y = k.alloc([T + 1], F32)
        qkT = [k.alloc([T], BF16) for _ in range(2)]
        vtok = k.alloc([NCH, 129], BF16)
        vTb = [k.alloc([512], BF16) for _ in range(2)]
        RAWd = [k.alloc([16, 129], F32) for _ in range(2)]
        RAW = Buf(RAWd[0][:, :, 0:128])
        RAW1 = Buf(RAWd[1][:, :, 0:128])
        dnm = k.alloc([2, 3, 16], F32)
        sig = [k.alloc([512], F32) for _ in range(2)]
        st = k.alloc([16, 8], F32)
        den3 = k.alloc([2, 4], F32)
        for p_ in Pq:
            op(DVE, lambda e, p_=p_: e.memset(p_.ap, 0.0), [], [p_])
        op(DVE, lambda e: e.memset(vtok.ap, 1.0), [], [vtok])
        cbm = chain_bufs(129)
        dma(POOL, Wm.ap, wm_d[0][:, :, 0:384], writes=[Wm])

        def m_proj(h):
            for kind in range(2):
                P_ = Pq[kind]
                for (t0, n) in TB:
                    pb = next_bank()
                    for kc in range(8):
                        op(PE, lambda e, kc=kc, kind=kind, t0=t0, n=n, pb=pb: e.matmul(pb[:, 0:n], Wm[:, kc, kind * 128:(kind + 1) * 128], uT[:, kc, t0:t0 + n],
                                                                                       start=(kc == 0), stop=(kc == 7)), [Wm, uT], [pb], inc=(kc == 7))
                    pc = 1 if t0 == 0 else t0 + 2
                    op(ACT, lambda e, kind=kind, n=n, pb=pb, pc=pc, P_=P_: e.activation(out=P_[:, pc:pc + n], in_=pb[:, 0:n], func=AF.Identity,
                                                                                         bias=V("bq_m" if kind == 0 else "bk_m", h)), [pb, vecs], [P_])
                ci = kind * 8 + h
                op(DVE, lambda e, P_=P_, ci=ci: e.tensor_scalar(out=Y.ap, in0=P_[:, 1:T + 2], scalar1=V("cw1", ci), scalar2=None, op0=ALU.mult), [P_, vecs], [Y])
                op(DVE, lambda e, P_=P_, ci=ci: e.scalar_tensor_tensor(out=Y.ap, in0=P_[:, 0:T + 1], scalar=V("cw0", ci), in1=Y.ap, op0=ALU.mult, op1=ALU.add),
                   [P_, vecs, Y], [Y])
                op(DVE, lambda e, P_=P_, ci=ci: e.scalar_tensor_tensor(out=Y.ap, in0=P_[:, 2:T + 3], scalar=V("cw2", ci), in1=Y.ap, op0=ALU.mult, op1=ALU.add),
                   [P_, vecs, Y], [Y])
                op(ACT, lambda e, kind=kind: e.activation(out=qkT[kind][:, 0:TC], in_=Y[:, 0:TC], func=AF.Silu), [Y], [qkT[kind]])
                op(ACT, lambda e, kind=kind: e.activation(out=qkT[kind][:, TC:T], in_=Y[:, TC + 1:T + 1], func=AF.Silu), [Y], [qkT[kind]])
            load_vtok(Wm, 256, 1, "bv_m", h, vtok, vTb)
            if h + 1 < 8:
                dma(POOL, Wm.ap, wm_d[h + 1][:, :, 0:384], writes=[Wm])

        def m_chain(h):
            def m_out(d, ci, bO, step):
                op(ACT, lambda e: e.activation(out=RAWd[d][:, ci, :], in_=bO[:, 0:129], func=AF.Copy), [bO], [RAWd[d]])

            chain(cbm, [qkT[0], qkT[0]], [qkT[1], qkT[1]], vtok, 129,
                  lambda d, c: (WS[:, c, d * 8 + h:d * 8 + h + 1], WS),
                  lambda d, c: (EE[:, c, d * 8 + h:d * 8 + h + 1], EE), m_out)

        def m_post_stats(h):
            for d in range(2):
                den = RAWd[d][:, :, 128]
                thd = TH[:, 2:NCH, d * 8 + h]
                op(DVE, lambda e, d=d, den=den: e.tensor_scalar(out=dnm[:, d, 0, :], in0=den, scalar1=-1.0, scalar2=None, op0=ALU.mult), [RAWd[d]], [dnm])
                op(DVE, lambda e, d=d, den=den, thd=thd: e.tensor_tensor(out=dnm[:, d, 1, :], in0=den, in1=thd, op=ALU.max), [RAWd[d], TH], [dnm])
                op(DVE, lambda e, d=d: e.tensor_tensor(out=dnm[:, d, 1, :], in0=dnm[:, d, 1, :], in1=dnm[:, d, 0, :], op=ALU.max), [dnm], [dnm])
                op(DVE, lambda e, d=d: e.reciprocal(out=dnm[:, d, 2, :], in_=dnm[:, d, 1, :]), [dnm], [dnm])
            r0_ = dnm[:, 0, 2, :].rearrange("p (a b) -> p a b", b=1)
            r1_ = dnm[:, 1, 2, :].rearrange("p (a b) -> p a b", b=1)
            op(DVE, lambda e: e.tensor_tensor(out=RAW.ap, in0=RAW.ap, in1=bc_last(r0_, 128), op=ALU.mult), [RAWd[0], dnm], [RAWd[0]])
            op(DVE, lambda e: e.tensor_tensor(out=RAW1.ap, in0=RAW1.ap, in1=bc_last(r1_, 128), op=ALU.mult), [RAWd[1], dnm], [RAWd[1]])
            op(DVE, lambda e: e.tensor_tensor(out=RAW.ap, in0=RAW.ap, in1=RAW1.ap, op=ALU.add), [RAWd[0], RAWd[1]], [RAWd[0]])
            SQ = Y.ap[:, 0:2048].rearrange("p (a b) -> p a b", a=16)
            op(DVE, lambda e: e.tensor_reduce(out=st[:, :, 0], in_=RAW.ap, axis=AX.X, op=ALU.add), [RAWd[0]], [st])
            op(ACT, lambda e: e.activation(out=SQ, in_=RAW.ap, func=AF.Square), [RAWd[0]], [Y])
            op(DVE, lambda e: e.tensor_reduce(out=st[:, :, 1], in_=SQ, axis=AX.X, op=ALU.add), [Y], [st])
            op(DVE, lambda e: e.tensor_scalar(out=st[:, :, 2], in0=st[:, :, 0], scalar1=1.0 / 128, scalar2=None, op0=ALU.mult), [st], [st])
            op(DVE, lambda e: e.tensor_tensor(out=st[:, :, 3], in0=st[:, :, 2], in1=st[:, :, 2], op=ALU.mult), [st], [st])
            op(DVE, lambda e: e.scalar_tensor_tensor(out=st[:, :, 4], in0=st[:, :, 1], scalar=1.0 / 128, in1=st[:, :, 3], op0=ALU.mult, op1=ALU.subtract),
               [st], [st])
            op(DVE, lambda e: e.tensor_scalar(out=st[:, :, 4], in0=st[:, :, 4], scalar1=EPS, scalar2=None, op0=ALU.add), [st], [st])
            op(ACT, lambda e: e.activation(out=st[:, :, 5], in_=st[:, :, 4], func=AF.Sqrt), [st], [st])
            op(DVE, lambda e: e.reciprocal(out=st[:, :, 6], in_=st[:, :, 5]), [st], [st])
            op(DVE, lambda e: e.tensor_tensor(out=RAW.ap, in0=RAW.ap, in1=bc_last(st[:, :, 2:3], 128), op=ALU.subtract), [RAWd[0], st], [RAWd[0]])
            op(DVE, lambda e: e.tensor_tensor(out=RAW.ap, in0=RAW.ap, in1=bc_last(st[:, :, 6:7], 128), op=ALU.mult), [RAWd[0], st], [RAWd[0]])
        def m_post_final(h):
            for tb in range(4):
                t0 = 256 + tb * 512
                pb = next_bank()
                for kc in range(8):
                    op(PE, lambda e, kc=kc, t0=t0, pb=pb: e.matmul(pb.ap, Wo[:, kc, :], uT[:, kc, t0:t0 + 512], start=(kc == 0), stop=(kc == 7)),
                       [Wo, uT], [pb], inc=(kc == 7))
                sg = sig[tb % 2]
                op(ACT, lambda e, pb=pb, sg=sg: e.activation(out=sg.ap, in_=pb.ap, func=AF.Sigmoid, bias=V("bo_m", h)), [pb, vecs], [sg])
                pt = next_bank()
                for j in range(4):
                    op(PE, lambda e, j=j, tb=tb, pt=pt: e.transpose(pt[:, j * 128:(j + 1) * 128], RAW[:, tb * 4 + j, :], identf), [RAWd[0], cst], [pt], inc=(j == 3))
                op(DVE, lambda e, tb=tb, pt=pt, sg=sg: e.scalar_tensor_tensor(out=gated[h][:, tb * 512:(tb + 1) * 512], in0=pt.ap, scalar=V("gn_m", h), in1=sg.ap,
                                                                              op0=ALU.mult, op1=ALU.mult), [pt, vecs, sg], [gated[h]])
        dma(POOL, Wo.ap, wm_d[0][:, :, 384:512], writes=[Wo])
        m_proj(0)
        for h in range(8):
            m_chain(h)
            m_post_stats(h)
            if h + 1 < 8:
                m_proj(h + 1)
            m_post_final(h)
            if h + 1 < 8:
                dma(POOL, Wo.ap, wm_d[h + 1][:, :, 384:512], writes=[Wo])
        k.barrier()
        if b == 0:
            for h in (0, 7):
                dump("gm%d" % h, gated[h])
        if stage == 2:
            k.barrier()
            return nc

        k.top = gla_base
        Wg = k.alloc([8, 512], BF16)
        Wz = k.alloc([8, 256], BF16)
        HT = 768
        Qr = k.alloc([HT], F32)
        Kr = k.alloc([HT], F32)
        T1s = [k.alloc([HT], F32) for _ in range(2)]
        T2s = [k.alloc([HT], F32) for _ in range(2)]
        T1 = T1s[0]
        qt = [k.alloc([T], BF16) for _ in range(2)]
        ktl = [k.alloc([T], BF16) for _ in range(2)]
        vtg = k.alloc([NCH, 256], BF16)
        vTg = [k.alloc([512], BF16) for _ in range(2)]
        RAWg = k.alloc([16, 256], F32)
        DEC = [k.alloc([NCH], F32) for _ in range(2)]
        TOT = k.alloc([6], F32)
        sg2 = [k.alloc([512], F32) for _ in range(2)]
        stg = k.alloc([16, 4], F32)
        junkg = k.alloc([256], F32)
        cbg = chain_bufs(256)
        chain_top = k.top
        HALF = [(0, 768, [(0, 256), (256, 512)]), (768, 768, [(768, 512), (1280, 256)]), (1536, 768, [(1536, 512), (2048, 256)])]
        dma(POOL, Wg.ap, wg_d[0][:, :, 0:512], writes=[Wg])
        dma(POOL, Wz.ap, wg_d[0][:, :, 512:768], writes=[Wz])

        def g_pre(g):
            for (h0, hn, blocks) in HALF:
                nch = hn // 128
                c0 = h0 // 128
                for kind, dst in ((0, Qr), (1, Kr)):
                    for (t0, n) in blocks:
                        pb = next_bank()
                        for kc in range(8):
                            op(PE, lambda e, kc=kc, kind=kind, t0=t0, n=n, pb=pb: e.matmul(pb[:, 0:n], Wg[:, kc, kind * 128:(kind + 1) * 128], uT[:, kc, t0:t0 + n],
                                                                                           start=(kc == 0), stop=(kc == 7)), [Wg, uT], [pb], inc=(kc == 7))
                        op(ACT, lambda e, kind=kind, t0=t0, n=n, pb=pb, dst=dst: e.activation(out=dst[:, t0 - h0:t0 - h0 + n], in_=pb[:, 0:n], func=AF.Identity,
                                                                                               bias=V("bq_g" if kind == 0 else "bk_g", g)), [pb, vecs], [dst])
                for d in range(2):
                    T1 = T1s[d]
                    for (t0, n) in blocks:
                        pb = next_bank()
                        op(PE, lambda e, d=d, t0=t0, n=n, pb=pb: e.matmul(pb[:, 0:n], gwb[0:16, d * 512 + g * 128:d * 512 + (g + 1) * 128], RT[d][0:16, t0:t0 + n],
                                                                          start=True, stop=True), [gwb, RT[d]], [pb])
                        op(ACT, lambda e, d=d, t0=t0, n=n, pb=pb, T1=T1: e.activation(out=T1[:, t0 - h0:t0 - h0 + n], in_=pb[:, 0:n], func=AF.Exp, scale=-1.0,
                                                                                      bias=ngb[:, d * 4 + g:d * 4 + g + 1]), [pb, ngb], [T1])
                    op(ACT, lambda e, T1=T1: e.activation(out=T1[:, 0:hn], in_=T1[:, 0:hn], func=AF.Ln, bias=1.0), [T1], [T1])
                for d in range(2):
                    T1, T2 = T1s[d], T2s[d]
                    op(DVE, lambda e, T1=T1, T2=T2: e.tensor_tensor_scan(out=T2[:, 0:hn], data0=RSTm[:, 0:hn], data1=T1[:, 0:hn], initial=0.0, op0=ALU.mult, op1=ALU.add),
                       [RSTm, T1], [T2])
                    if d == 1:
                        t23 = T2[:, 0:hn].rearrange("p (a b) -> p a b", b=128)
                        op(DVE, lambda e, t23=t23: e.tensor_copy(out=TOT[:, 0:nch], in_=t23[:, :, 127]), [T2], [TOT])
                        op(DVE, lambda e, T1=T1, T2=T2: e.tensor_tensor(out=T2[:, 0:hn], in0=T1[:, 0:hn], in1=T2[:, 0:hn], op=ALU.subtract), [T1, T2], [T2])
                        tot3 = TOT[:, 0:nch].rearrange("p (a b) -> p a b", b=1)
                        op(DVE, lambda e, t23=t23, tot3=tot3: e.tensor_tensor(out=t23, in0=t23, in1=bc_last(tot3, 128), op=ALU.add), [T2, TOT], [T2])
                for d in range(2):
                    T1, T2 = T1s[d], T2s[d]
                    op(ACT, lambda e, T1=T1, T2=T2: e.activation(out=T1[:, 0:hn], in_=T2[:, 0:hn], func=AF.Exp, scale=-1.0 / 16), [T2], [T1])
                for d in range(2):
                    T1 = T1s[d]
                    t13 = T1[:, 0:hn].rearrange("p (a b) -> p a b", b=128)
                    op(DVE, lambda e, d=d, t13=t13: e.tensor_copy(out=DEC[d][:, c0:c0 + nch], in_=t13[:, :, 127 if d == 0 else 0]), [T1], [DEC[d]])
                    op(DVE, lambda e, d=d, T1=T1: e.tensor_tensor(out=qt[d][:, h0:h0 + hn], in0=Qr[:, 0:hn], in1=T1[:, 0:hn], op=ALU.mult), [Qr, T1], [qt[d]])
                for d in range(2):
                    T1, T2 = T1s[d], T2s[d]
                    op(ACT, lambda e, T1=T1, T2=T2: e.activation(out=T1[:, 0:hn], in_=T2[:, 0:hn], func=AF.Exp, scale=1.0 / 16), [T2], [T1])
                for d in range(2):
                    T1 = T1s[d]
                    op(DVE, lambda e, d=d, T1=T1: e.tensor_tensor(out=ktl[d][:, h0:h0 + hn], in0=Kr[:, 0:hn], in1=T1[:, 0:hn], op=ALU.mult), [Kr, T1], [ktl[d]])
            load_vtok(Wg, 256, 2, "bv_g", g * 2, vtg, vTg)
            if g + 1 < 4:
                dma(POOL, Wg.ap, wg_d[g + 1][:, :, 0:512], writes=[Wg])

        def g_chain(g):
            def g_out(d, ci, bO, step):
                mine = ci + 2 if d == 0 else NCH - 1 - ci
                other = NCH - 1 - ci if d == 0 else ci + 2
                first = mine < other or (mine == other and d == 0)
                if first:
                    op(ACT, lambda e: e.activation(out=RAWg[:, ci, :], in_=bO[:, 0:256], func=AF.Copy), [bO], [RAWg])
                else:
                    op(DVE, lambda e: e.tensor_tensor(out=RAWg[:, ci, :], in0=bO[:, 0:256], in1=RAWg[:, ci, :], op=ALU.add), [bO, RAWg], [RAWg])

            chain(cbg, qt, ktl, vtg, 256, None, lambda d, c: (DEC[d][:, c:c + 1], DEC[d]), g_out)

        def g_post_stats(g):
            for ci in range(16):
                op(ACT, lambda e, ci=ci: e.activation(out=junkg.ap, in_=RAWg[:, ci, :], func=AF.Square, accum_out=stg[:, ci, 0:1]), [RAWg], [junkg, stg])
            op(DVE, lambda e: e.tensor_scalar(out=stg[:, :, 1], in0=stg[:, :, 0], scalar1=1.0 / 256, scalar2=EPS * 128.0, op0=ALU.mult, op1=ALU.add), [stg], [stg])
            op(ACT, lambda e: e.activation(out=stg[:, :, 2], in_=stg[:, :, 1], func=AF.Sqrt), [stg], [stg])
            op(DVE, lambda e: e.reciprocal(out=stg[:, :, 3], in_=stg[:, :, 2]), [stg], [stg])
            op(DVE, lambda e: e.tensor_tensor(out=RAWg.ap, in0=RAWg.ap, in1=bc_last(stg[:, :, 3:4], 256), op=ALU.mult), [RAWg, stg], [RAWg])

        def g_post_final(g):
            for tb in range(4):
                t0 = 256 + tb * 512
                for fh in range(2):
                    pb = next_bank()
                    for kc in range(8):
                        op(PE, lambda e, kc=kc, t0=t0, fh=fh, pb=pb: e.matmul(pb.ap, Wz[:, kc, fh * 128:(fh + 1) * 128], uT[:, kc, t0:t0 + 512],
                                                                              start=(kc == 0), stop=(kc == 7)), [Wz, uT], [pb], inc=(kc == 7))
                    sg = sg2[fh]
                    op(ACT, lambda e, pb=pb, sg=sg, fh=fh: e.activation(out=sg.ap, in_=pb.ap, func=AF.Silu, bias=V("bz_g", g * 2 + fh)), [pb, vecs], [sg])
                    pt = next_bank()
                    for j in range(4):
                        op(PE, lambda e, j=j, tb=tb, fh=fh, pt=pt: e.transpose(pt[:, j * 128:(j + 1) * 128], RAWg[:, tb * 4 + j, fh * 128:(fh + 1) * 128], identf),
                           [RAWg, cst], [pt], inc=(j == 3))
                    gi = 8 + g * 2 + fh
                    op(DVE, lambda e, tb=tb, pt=pt, sg=sg, gi=gi, fh=fh: e.scalar_tensor_tensor(out=gated[gi][:, tb * 512:(tb + 1) * 512], in0=pt.ap,
                                                                                                scalar=V("gn_g", g * 2 + fh), in1=sg.ap, op0=ALU.mult, op1=ALU.mult),
                       [pt, vecs, sg], [gated[gi]])

        g_pre(0)
        for g in range(4):
            g_chain(g)
            g_post_stats(g)
            if g + 1 < 4:
                g_pre(g + 1)
            g_post_final(g)
            if g + 1 < 4:
                dma(POOL, Wz.ap, wg_d[g + 1][:, :, 512:768], writes=[Wz])
        k.barrier()
        if b == 0:
            for gi in (8, 15):
                dump("gg%d" % gi, gated[gi])
        if stage == 3:
            k.barrier()
            return nc

        k.top = work_top
        mT = [k.alloc([TL], BF16) for _ in range(8)]
        c2_top = k.top
        Wc = [k.alloc([8, 512], BF16) for _ in range(2)]
        SA = k.alloc([512], F32)
        SBb = k.alloc([512], F32)
        M1 = k.alloc([512], F32)
        M2 = k.alloc([512], F32)
        for n in range(8):
            W = Wc[n % 2]
            dma(POOL, W.ap, wc1_d[n], writes=[W])
            for tb in range(4):
                l0 = tb * 512
                pa, pbk, pm, pg = next_bank(), next_bank(), next_bank(), next_bank()
                for kc in range(8):
                    op(PE, lambda e, kc=kc, pa=pa, l0=l0, W=W: e.matmul(pa.ap, W[:, kc, 256:384], uT[:, kc, TC + l0:TC + l0 + 512], start=(kc == 0), stop=(kc == 7)),
                       [W, uT], [pa], inc=(kc == 7))
                op(ACT, lambda e, pa=pa: e.activation(out=SA.ap, in_=pa.ap, func=AF.Sigmoid, bias=V("ba", n)), [pa, vecs], [SA])
                for kc in range(8):
                    op(PE, lambda e, kc=kc, pbk=pbk, l0=l0, W=W: e.matmul(pbk.ap, W[:, kc, 384:512], uT[:, kc, TC + l0:TC + l0 + 512], start=(kc == 0), stop=(kc == 7)),
                       [W, uT], [pbk], inc=(kc == 7))
                op(ACT, lambda e, pbk=pbk: e.activation(out=SBb.ap, in_=pbk.ap, func=AF.Sigmoid, bias=V("bb", n)), [pbk, vecs], [SBb])
                for kc in range(8):
                    op(PE, lambda e, kc=kc, pm=pm, l0=l0, W=W: e.matmul(pm.ap, W[:, kc, 0:128], gated[kc][:, l0:l0 + 512], start=(kc == 0), stop=(kc == 7)),
                       [W, gated[kc]], [pm], inc=(kc == 7))
                op(DVE, lambda e, pm=pm: e.tensor_tensor(out=M1.ap, in0=pm.ap, in1=SA.ap, op=ALU.mult), [pm, SA], [M1])
                for kc in range(8):
                    op(PE, lambda e, kc=kc, pg=pg, l0=l0, W=W: e.matmul(pg.ap, W[:, kc, 128:256], gated[8 + kc][:, l0:l0 + 512], start=(kc == 0), stop=(kc == 7)),
                       [W, gated[8 + kc]], [pg], inc=(kc == 7))
                op(DVE, lambda e, pg=pg: e.tensor_tensor(out=M2.ap, in0=pg.ap, in1=SBb.ap, op=ALU.mult), [pg, SBb], [M2])
                op(DVE, lambda e, n=n, l0=l0: e.tensor_tensor(out=mT[n][:, l0:l0 + 512], in0=M1.ap, in1=M2.ap, op=ALU.add), [M1, M2], [mT[n]])
        k.barrier()
        if b == 0:
            dump("mT0", mT[0])
        if stage == 4:
            k.barrier()
            return nc

        k.top = c2_top
        h1s = h1s_view()
        wout = k.alloc([8, D], BF16)
        dma(POOL, wout.ap, wout_d, writes=[wout])
        G1bc = k.alloc([D], F32)
        LN1G = k.alloc([D], F32)
        LN1B = k.alloc([D], F32)
        dma(SP, LN1G.ap, rows_d[0:1, :].partition_broadcast(128), writes=[LN1G])
        dma(SP, LN1B.ap, rows_d[1:2, :].partition_broadcast(128), writes=[LN1B])
        dma(SP, G1bc.ap, rows_d[4:5, :].partition_broadcast(128), writes=[G1bc])
        xts = [k.alloc([D], F32) for _ in range(2)]
        Rb = [k.alloc([D], F32) for _ in range(2)]
        stc = k.alloc([2, 6], F32)
        mv = k.alloc([8], F32)
        c2w_top = k.top
        wg1 = k.alloc([8, D], BF16)
        screp = k.alloc([8, 128], BF16)
        dma(POOL, wg1.ap, wmod_d[:, :, 2 * D:3 * D], writes=[wg1])
        op(DVE, lambda e: e.tensor_copy(out=screp.ap, in_=bc_last(scT[:, :, b:b + 1], 128)), [scT], [screp])
        pg1 = PP[0]
        for hf in range(2):
            for kc in range(8):
                op(PE, lambda e, hf=hf, kc=kc: e.matmul(pg1[:, hf * 512:(hf + 1) * 512], screp[:, kc, :], wg1[:, kc, hf * 512:(hf + 1) * 512],
                                                        start=(kc == 0), stop=(kc == 7)), [screp, wg1], [pg1], inc=(kc == 7 and hf == 1))
        op(DVE, lambda e: e.tensor_tensor(out=G1bc.ap, in0=pg1.ap, in1=G1bc.ap, op=ALU.add), [pg1, G1bc], [G1bc])
        k.barrier()
        k.top = c2w_top
        xts = xts + [k.alloc([D], F32) for _ in range(3)]
        for tt in range(16):
            xt, R = xts[tt % 5], Rb[tt % 2]
            pm, ptr = PP[tt % 2], PP[2 + tt % 2]
            dma(SP, xt.ap, xin[b, TC + tt * 128:TC + (tt + 1) * 128, :], writes=[xt])
            for hf in range(2):
                for kc in range(8):
                    op(PE, lambda e, hf=hf, kc=kc, tt=tt, pm=pm: e.matmul(pm[:, hf * 512:(hf + 1) * 512], mT[kc][:, tt * 128:(tt + 1) * 128], wout[:, kc, hf * 512:(hf + 1) * 512],
                                                                          start=(kc == 0), stop=(kc == 7)), [mT[kc], wout], [pm], inc=(kc == 7 and hf == 1))
            op(DVE, lambda e, R=R, pm=pm: e.tensor_tensor(out=R.ap, in0=pm.ap, in1=G1bc.ap, op=ALU.mult), [pm, G1bc], [R])
            op(DVE, lambda e, R=R, xt=xt: e.scalar_tensor_tensor(out=R.ap, in0=xt.ap, scalar=ALPHA, in1=R.ap, op0=ALU.mult, op1=ALU.add), [xt, R], [R])
            for hf in range(2):
                op(DVE, lambda e, R=R, hf=hf: e.bn_stats(out=stc[:, hf, :], in_=R[:, hf * 512:(hf + 1) * 512]), [R], [stc])
            op(DVE, lambda e: e.bn_aggr(out=mv[:, 0:2], in_=stc.ap.rearrange("p a b -> p (a b)")), [stc], [mv])
            op(DVE, lambda e: e.tensor_scalar(out=mv[:, 2:3], in0=mv[:, 1:2], scalar1=EPS, scalar2=None, op0=ALU.add), [mv], [mv])
            op(ACT, lambda e: e.activation(out=mv[:, 3:4], in_=mv[:, 2:3], func=AF.Sqrt), [mv], [mv])
            op(DVE, lambda e: e.reciprocal(out=mv[:, 4:5], in_=mv[:, 3:4]), [mv], [mv])
            op(DVE, lambda e: e.tensor_scalar(out=mv[:, 5:6], in0=mv[:, 0:1], scalar1=-1.0, scalar2=mv[:, 4:5], op0=ALU.mult, op1=ALU.mult), [mv], [mv])
            op(ACT, lambda e, R=R: e.activation(out=R.ap, in_=R.ap, func=AF.Identity, scale=mv[:, 4:5], bias=mv[:, 5:6]), [R, mv], [R])
            op(DVE, lambda e, R=R: e.tensor_tensor(out=R.ap, in0=R.ap, in1=LN1G.ap, op=ALU.mult), [R, LN1G], [R])
            op(DVE, lambda e, R=R: e.tensor_tensor(out=R.ap, in0=R.ap, in1=LN1B.ap, op=ALU.add), [R, LN1B], [R])
            if b == 0 and tt == 0:
                dump("h1t0", R)
            for kc in range(8):
                op(PE, lambda e, kc=kc, R=R, ptr=ptr: e.transpose(ptr[:, kc * 128:(kc + 1) * 128], R[:, kc * 128:(kc + 1) * 128], identf), [R, cst], [ptr], inc=(kc == 7))
            op(DVE, lambda e, tt=tt, ptr=ptr: e.scalar_tensor_tensor(out=h1s[:, :, tt * 128:(tt + 1) * 128], in0=ptr.ap.rearrange("p (a b) -> p a b", a=8),
                                                                      scalar=ALPHA, in1=bc_last(GB[:, :, b:b + 1], 128), op0=ALU.mult, op1=ALU.add),
               [ptr, GB], [h1s])
        k.barrier()

        k.top = uT_off
        stc = k.alloc([2, 6], F32)
        mv = k.alloc([8], F32)
        aT = [k.alloc([1024], BF16) for _ in range(NFC)]
        U2 = k.alloc([8, 1088], BF16)
        G = [k.alloc([18, 66], BF16) for _ in range(2)]
        DG = [k.alloc([9, 128], BF16) for _ in range(2)]
        Wu = [k.alloc([8, 256], BF16) for _ in range(2)]
        GL = [k.alloc([512], F32) for _ in range(2)]
        Wd = [k.alloc([NFC, 128], BF16) for _ in range(2)]
        R2T = k.alloc([8, 512], F32)
        LN2G = k.alloc([D], F32)
        LN2B = k.alloc([D], F32)
        OT = [k.alloc([D], F32) for _ in range(2)]
        dma(SP, LN2G.ap, rows_d[2:3, :].partition_broadcast(128), writes=[LN2G])
        dma(SP, LN2B.ap, rows_d[3:4, :].partition_broadcast(128), writes=[LN2B])
        for hf in range(2):
            t0 = hf * 1024
            lo = 0 if hf == 0 else 960
            grow0 = 1 if hf == 0 else 0
            voff = t0 - lo
            for kc in range(8):
                op(DVE, lambda e, kc=kc, lo=lo: e.tensor_scalar(out=U2[:, kc, :], in0=h1s[:, kc, lo:lo + 1088], scalar1=A2[:, kc, b:b + 1], scalar2=B2[:, kc, b:b + 1],
                                                                op0=ALU.mult, op1=ALU.add), [h1s, A2, B2], [U2])
            for gb_ in G:
                op(DVE, lambda e, gb_=gb_: e.memset(gb_.ap, 0.0), [], [gb_])
            for c in range(NFC):
                W, Gc, Dg = Wu[c % 2], G[c % 2], DG[c % 2]
                dma(POOL, W.ap, wup_d[c], writes=[W])
                for tap in range(9):
                    op(DVE, lambda e, tap=tap, c=c, Dg=Dg: e.tensor_scalar(out=Dg[:, tap, :], in0=identb.ap, scalar1=V("fcw", tap * NFC + c), scalar2=None, op0=ALU.mult), [identb, vecs], [Dg])
                for (p0, n) in ((0, 512), (512, 512), (1024, 64)):
                    pb = next_bank()
                    for kc in range(8):
                        op(PE, lambda e, kc=kc, p0=p0, n=n, pb=pb, W=W: e.matmul(pb[:, 0:n], W[:, kc, 0:128], U2[:, kc, p0:p0 + n], start=(kc == 0), stop=(kc == 7)),
                           [W, U2], [pb], inc=(kc == 7))
                    r_ = grow0 + p0 // 64
                    nr = n // 64
                    op(ACT, lambda e, pb=pb, n=n, r_=r_, nr=nr, Gc=Gc, c=c: e.activation(out=Gc[:, r_:r_ + nr, 1:65], in_=pb[:, 0:n].rearrange("p (a b) -> p a b", b=64),
                                                                                         func=AF.Identity, bias=V("bup_g", c)), [pb, vecs], [Gc])
                for blk in range(2):
                    pc = next_bank()
                    for tap in range(9):
                        dr, dcol = tap // 3, tap % 3
                        op(PE, lambda e, tap=tap, dr=dr, dcol=dcol, blk=blk, pc=pc, Gc=Gc, Dg=Dg: e.matmul(pc.ap, Dg[:, tap, :], Gc[:, blk * 8 + dr:blk * 8 + dr + 8, dcol:dcol + 64],
                                                                                                            start=(tap == 0), stop=(tap == 8)), [Dg, Gc], [pc], inc=(tap == 8))
                    gl = GL[blk]
                    op(ACT, lambda e, pc=pc, gl=gl, c=c: e.activation(out=gl.ap, in_=pc.ap, func=AF.Gelu_apprx_tanh, bias=V("fcb", c)), [pc, vecs], [gl])
                    pv = next_bank()
                    for kc in range(8):
                        op(PE, lambda e, kc=kc, blk=blk, pv=pv, W=W: e.matmul(pv.ap, W[:, kc, 128:256], U2[:, kc, voff + blk * 512:voff + (blk + 1) * 512],
                                                                              start=(kc == 0), stop=(kc == 7)), [W, U2], [pv], inc=(kc == 7))
                    op(DVE, lambda e, pv=pv, gl=gl, c=c, blk=blk: e.scalar_tensor_tensor(out=aT[c][:, blk * 512:(blk + 1) * 512], in0=pv.ap, scalar=V("bup_v", c), in1=gl.ap,
                                                                                         op0=ALU.add, op1=ALU.mult), [pv, vecs, gl], [aT[c]])
            for tb in range(2):
                for n in range(8):
                    W = Wd[n % 2]
                    dma(POOL, W.ap, wdn_d[n], writes=[W])
                    pb = next_bank()
                    for kc in range(NFC):
                        op(PE, lambda e, kc=kc, pb=pb, W=W, tb=tb: e.matmul(pb.ap, W[:, kc, :], aT[kc][:, tb * 512:(tb + 1) * 512], start=(kc == 0), stop=(kc == NFC - 1)),
                           [W, aT[kc]], [pb], inc=(kc == NFC - 1))
                    tg = t0 + tb * 512
                    op(DVE, lambda e, n=n, pb=pb, tg=tg: e.scalar_tensor_tensor(out=R2T[:, n, :], in0=pb.ap, scalar=MOD[:, 40 + n, b:b + 1], in1=h1s[:, n, tg:tg + 512],
                                                                                op0=ALU.mult, op1=ALU.add), [pb, MOD, h1s], [R2T])
                for tt in range(4):
                    ptr = PP[2 + tt % 2]
                    ot = OT[tt % 2]
                    for n in range(8):
                        op(PE, lambda e, n=n, tt=tt, ptr=ptr: e.transpose(ptr[:, n * 128:(n + 1) * 128], R2T[:, n, tt * 128:(tt + 1) * 128], identf), [R2T, cst], [ptr], inc=(n == 7))
                    for q in range(2):
                        op(DVE, lambda e, q=q, ptr=ptr: e.bn_stats(out=stc[:, q, :], in_=ptr[:, q * 512:(q + 1) * 512]), [ptr], [stc])
                    op(DVE, lambda e: e.bn_aggr(out=mv[:, 0:2], in_=stc.ap.rearrange("p a b -> p (a b)")), [stc], [mv])
                    op(DVE, lambda e: e.tensor_scalar(out=mv[:, 2:3], in0=mv[:, 1:2], scalar1=EPS, scalar2=None, op0=ALU.add), [mv], [mv])
                    op(ACT, lambda e: e.activation(out=mv[:, 3:4], in_=mv[:, 2:3], func=AF.Sqrt), [mv], [mv])
                    op(DVE, lambda e: e.reciprocal(out=mv[:, 4:5], in_=mv[:, 3:4]), [mv], [mv])
                    op(DVE, lambda e: e.tensor_scalar(out=mv[:, 5:6], in0=mv[:, 0:1], scalar1=-1.0, scalar2=mv[:, 4:5], op0=ALU.mult, op1=ALU.mult), [mv], [mv])
                    op(DVE, lambda e, ot=ot, ptr=ptr: e.tensor_scalar(out=ot.ap, in0=ptr.ap, scalar1=mv[:, 4:5], scalar2=mv[:, 5:6], op0=ALU.mult, op1=ALU.add), [ptr, mv], [ot])
                    op(DVE, lambda e, ot=ot: e.tensor_tensor(out=ot.ap, in0=ot.ap, in1=LN2G.ap, op=ALU.mult), [ot, LN2G], [ot])
                    op(DVE, lambda e, ot=ot: e.tensor_tensor(out=ot.ap, in0=ot.ap, in1=LN2B.ap, op=ALU.add), [ot, LN2B], [ot])
                    r0 = t0 + tb * 512 + tt * 128
                    dma(SP, out_d[b, r0:r0 + 128, :], ot.ap, reads=[ot])
        k.barrier()
    k.barrier()
    return nc


def _fm(v, nchunk):
    return np.ascontiguousarray(np.asarray(v, np.float32).reshape(nchunk, 128).T)


def _kmaj(w):
    K, N = w.shape
    return np.ascontiguousarray(np.asarray(w, np.float32).reshape(K // 128, 128, N).transpose(1, 0, 2))


def prep_shared(inp):
    g = lambda n: np.asarray(inp[n], np.float32)[0]
    w_in, b_in = g("w_in"), g("b_in")
    o_q, o_k, o_v, o_o, o_g = 0, 1024, 2048, 3072, 4096
    o_qg = 4128
    o_kg, o_vg, o_zg, o_r = o_qg + 512, o_qg + 1024, o_qg + 2048, o_qg + 3072
    o_a = o_r + 32
    o_b = o_a + 1024
    vec = np.zeros((128, NV), np.float32)

    def put(name, arr):
        vec[:, VOFF[name]:VOFF[name] + arr.shape[1]] = arr

    put("bq_m", _fm(b_in[o_q:o_q + 1024], 8)); put("bk_m", _fm(b_in[o_k:o_k + 1024], 8))
    put("bv_m", _fm(b_in[o_v:o_v + 1024], 8)); put("bo_m", _fm(b_in[o_o:o_o + 1024], 8))
    cq = g("conv_qk")
    for j in range(3):
        put("cw%d" % j, _fm(cq[j], 16))
    put("bq_g", _fm(b_in[o_qg:o_qg + 512], 4)); put("bk_g", _fm(b_in[o_kg:o_kg + 512], 4))
    put("bv_g", _fm(b_in[o_vg:o_vg + 1024], 8)); put("bz_g", _fm(b_in[o_zg:o_zg + 1024], 8))
    put("gb_f", _fm(g("gla_gate_b_fwd"), 4)); put("gb_b", _fm(g("gla_gate_b_bwd"), 4))
    put("gn_m", _fm(g("mlstm_norm_g"), 8)); put("gn_g", _fm(g("gla_norm_g"), 8))
    put("ba", _fm(b_in[o_a:o_a + 1024], 8)); put("bb", _fm(b_in[o_b:o_b + 1024], 8))
    bup = g("b_up")
    put("bup_g", _fm(bup[:DFF], NFC)); put("bup_v", _fm(bup[DFF:], NFC))
    fcw = g("ffn_conv_w").reshape(9, DFF)
    put("fcw", np.concatenate([_fm(fcw[t], NFC) for t in range(9)], axis=1))
    put("fcb", _fm(g("ffn_conv_b"), NFC)); put("bdn", _fm(g("b_down"), 8)); put("bmod", _fm(g("b_mod"), 48))
    br = np.zeros((128, 2), np.float32)
    br[0:16, 0] = b_in[o_r:o_r + 16]
    br[0:16, 1] = b_in[o_r + 16:o_r + 32]
    put("br", br)
    wk = _kmaj(w_in)
    w_m = np.stack([np.concatenate([wk[:, :, o + h * 128:o + (h + 1) * 128] for o in (o_q, o_k, o_v, o_o)], axis=2) for h in range(8)])
    w_g = np.stack([np.concatenate([wk[:, :, o_qg + h * 128:o_qg + (h + 1) * 128], wk[:, :, o_kg + h * 128:o_kg + (h + 1) * 128],
                                    wk[:, :, o_vg + h * 256:o_vg + (h + 1) * 256], wk[:, :, o_zg + h * 256:o_zg + (h + 1) * 256]], axis=2) for h in range(4)])
    w_s = np.concatenate([wk[:, :, o_g:o_g + 32], wk[:, :, o_r:o_r + 32]], axis=2)
    wbm, wbg = _kmaj(g("w_branch_mlstm")), _kmaj(g("w_branch_gla"))
    wc1 = np.stack([np.concatenate([wbm[:, :, n * 128:(n + 1) * 128], wbg[:, :, n * 128:(n + 1) * 128],
                                    wk[:, :, o_a + n * 128:o_a + (n + 1) * 128], wk[:, :, o_b + n * 128:o_b + (n + 1) * 128]], axis=2) for n in range(8)])
    wu = _kmaj(g("w_up"))
    wup = np.stack([np.concatenate([wu[:, :, c * 128:(c + 1) * 128], wu[:, :, DFF + c * 128:DFF + (c + 1) * 128]], axis=2) for c in range(NFC)])
    wd = _kmaj(g("w_down"))
    wdn = np.stack([wd[:, :, n * 128:(n + 1) * 128] for n in range(8)])
    rows = np.zeros((8, D), np.float32)
    rows[0], rows[1], rows[2], rows[3] = g("ln1_g"), g("ln1_b"), g("ln2_g"), g("ln2_b")
    bm = g("b_mod")
    rows[4] = bm[2 * D:3 * D]
    rows[6, 0:32] = b_in[o_g:o_g + 32]
    idx = np.arange(128)
    cst = np.zeros((128, 4, 128), np.float32)
    cst[:, 0, :] = np.eye(128)
    cst[:, 1, :] = (idx[:, None] <= idx[None, :])
    cst[:, 2, :] = (idx[:, None] >= idx[None, :])
    cst[:, 3, :] = 1.0
    rst = np.ones((128, T), np.float32)
    rst[:, ::128] = 0.0
    c = lambda a: np.ascontiguousarray(a, dtype=np.float32)
    return dict(wmod=_kmaj(g("w_mod")), vecs=vec, w_m=c(w_m), w_g=c(w_g), w_s=c(w_s),
                gw=c(np.concatenate([g("gla_gate_w_fwd"), g("gla_gate_w_bwd")], axis=1)), rows=rows, wc1=c(wc1),
                wout=_kmaj(g("w_out")), wup=c(wup), wdn=c(wdn), cst=cst, rst=rst)


def prep_core(inp, bs):
    x, ctx, cvec, cctx = (np.asarray(inp[n], np.float32) for n in ("x", "ctx", "c", "c_ctx"))
    nb = len(bs)
    xin = np.ascontiguousarray(np.concatenate([ctx[bs], x[bs]], axis=1))
    cT = np.zeros((128, 8, 4), np.float32)
    for j, b in enumerate(bs):
        cT[:, :, j] = _fm(cvec[b], 8)
    cT[:, :, nb] = _fm(cctx, 8)
    return dict(xin=xin, cT=cT)


def kernel(**inputs):
    n = 8
    NB = 2
    shared = prep_shared(inputs)
    nc = build(NB)
    in_maps = []
    for i in range(n):
        m = dict(shared)
        m.update(prep_core(inputs, list(range(i * NB, (i + 1) * NB))))
        in_maps.append(m)
    res = run_bass_kernel_spmd(nc, in_maps, core_ids=list(range(n)))
    return np.concatenate([r["out"] for r in res.results], axis=0).astype(np.float32)
```
